# Optimizing a Trainium2 kernel written in Bass

```python
import math
import jax, jax.numpy as jnp
from jax import lax
import numpy as np

D_MODEL = 1024
BATCH = 16
SEQ = 256
DEPTH = 2
DEC_BATCH = 8
DEC_SEQ = 2048
PAST_LEN = 512

GRID_W = 64
N_HEADS = 8
HEAD_DIM = 64
V_DIM = 2 * HEAD_DIM
QK_WIDTH = N_HEADS * 2 * HEAD_DIM
ATTN_WIDTH = N_HEADS * V_DIM
POOL_WINDOWS = (2, 4, 8, 16)
POOL_GROUP = 128
POOL_WIDTH = POOL_GROUP * len(POOL_WINDOWS)
FOURIER_GROUPS = 4
FOURIER_GROUP = 128
FOURIER_WIDTH = FOURIER_GROUPS * FOURIER_GROUP
IN_WIDTH = 2 * QK_WIDTH + ATTN_WIDTH + POOL_WIDTH + FOURIER_WIDTH
D_FF = 2816
N_MOD = 9
ROPE_BASE = 10000.0
Q_BLOCK = 128
EPS = 1e-6

kernel_name = 'hybrid_diffattn_pool_fourier_prefix_step'


def rms_norm(x, g):
    xf = x.astype(jnp.float32)
    y = xf * lax.rsqrt(jnp.mean(xf * xf, axis=-1, keepdims=True) + EPS)
    return (y * g.astype(jnp.float32)).astype(x.dtype)


def modulate(x, g, shift, scale):
    return rms_norm(x, g) * (1 + scale) + shift


def swiglu(h, w13, w2):
    gate, up = jnp.split(h @ w13, 2, axis=-1)
    return (jax.nn.silu(gate) * up) @ w2


def axial_rope_tables(L):
    rows = L // GRID_W
    row = jnp.repeat(jnp.arange(rows), GRID_W).astype(jnp.float32)
    col = jnp.tile(jnp.arange(GRID_W), rows).astype(jnp.float32)
    n_freq = HEAD_DIM // 4
    inv = ROPE_BASE ** (-jnp.arange(n_freq, dtype=jnp.float32) / n_freq)
    ang = jnp.stack([row[:, None] * inv, col[:, None] * inv], axis=1)
    return jnp.cos(ang), jnp.sin(ang)


def apply_axial_rope(x, cos, sin):
    B, L = x.shape[:2]
    xr = x.reshape(B, L, N_HEADS, 2, 2, 2, HEAD_DIM // 4)
    x1, x2 = xr[..., 0, :], xr[..., 1, :]
    c = cos[None, :, None, None]
    s = sin[None, :, None, None]
    out = jnp.stack([x1 * c - x2 * s, x1 * s + x2 * c], axis=-2)
    return out.reshape(x.shape).astype(x.dtype)


def diff_attention(q, k, v, lam):
    B, Lq = q.shape[:2]
    nb = Lq // Q_BLOCK
    qb = jnp.moveaxis(q.reshape(B, nb, Q_BLOCK, N_HEADS, 2, HEAD_DIM), 1, 0)
    kf = k.astype(jnp.float32)
    vf = v.astype(jnp.float32)
    lam_f = lam.astype(jnp.float32)
    scale = HEAD_DIM ** -0.5

    def block(qblk):
        s = jnp.einsum('bqhnd,bkhnd->bnhqk', qblk.astype(jnp.float32), kf) * scale
        p = jax.nn.softmax(s, axis=-1)
        a = p[:, 0] - lam_f * p[:, 1]
        return jnp.einsum('bhqk,bkhe->bqhe', a, vf)

    o = lax.map(block, qb)
    return jnp.moveaxis(o, 0, 1).reshape(B, Lq, N_HEADS, V_DIM).astype(v.dtype)


def pool_mix(u, w_pool, pool_scale):
    B, L, _ = u.shape
    G = len(POOL_WINDOWS)
    uf = u.astype(jnp.float32).reshape(B, L, G, POOL_GROUP)
    cs = jnp.concatenate([jnp.zeros((B, 1, G, POOL_GROUP), jnp.float32), jnp.cumsum(uf, axis=1)], axis=1)
    t = jnp.arange(L)
    outs = []
    for g, w in enumerate(POOL_WINDOWS):
        lo = jnp.clip(t - w // 2, 0, L)
        hi = jnp.clip(t + w // 2, 0, L)
        csg = cs[:, :, g]
        win_sum = jnp.take(csg, hi, axis=1) - jnp.take(csg, lo, axis=1)
        cnt = (hi - lo).astype(jnp.float32)[None, :, None]
        d = win_sum / cnt - uf[:, :, g]
        outs.append(jnp.einsum('blc,cd->bld', d, w_pool[g].astype(jnp.float32)))
    y = jnp.concatenate(outs, axis=-1) * pool_scale.astype(jnp.float32)
    return y.astype(u.dtype)


def fourier_mix(u):
    B, L, _ = u.shape
    uf = u.astype(jnp.float32).reshape(B, L, FOURIER_GROUPS, FOURIER_GROUP)
    y = jnp.fft.fft2(uf, axes=(1, 3), norm='ortho').real
    return y.reshape(B, L, FOURIER_WIDTH).astype(u.dtype)


def token_mixer(h, ctx_k, ctx_v, lam_init, w_in, q_norm_g, k_norm_g, lam_qk, subln_g,
                w_pool, pool_scale, w_gate, w_pa, w_pp, w_pf, w_out):
    B, L, _ = h.shape
    proj = h @ w_in
    q, k, v, up, uf = jnp.split(
        proj, [QK_WIDTH, 2 * QK_WIDTH, 2 * QK_WIDTH + ATTN_WIDTH,
               2 * QK_WIDTH + ATTN_WIDTH + POOL_WIDTH], axis=-1)
    q = rms_norm(q.reshape(B, L, N_HEADS, 2, HEAD_DIM), q_norm_g)
    k = rms_norm(k.reshape(B, L, N_HEADS, 2, HEAD_DIM), k_norm_g)
    v = v.reshape(B, L, N_HEADS, V_DIM)
    if ctx_k is None:
        k_all, v_all = k, v
        k_out, v_out = k.reshape(B, L, N_HEADS, 2 * HEAD_DIM), v
    else:
        cos, sin = axial_rope_tables(L)
        q = apply_axial_rope(q, cos, sin)
        k = apply_axial_rope(k, cos, sin)
        P = ctx_k.shape[1]
        k_all = jnp.concatenate([ctx_k.reshape(B, P, N_HEADS, 2, HEAD_DIM).astype(k.dtype), k], axis=1)
        v_all = jnp.concatenate([ctx_v.astype(v.dtype), v], axis=1)
        k_out, v_out = None, None
    lq = lam_qk.astype(jnp.float32)
    lam = jnp.exp(jnp.sum(lq[0] * lq[1])) - jnp.exp(jnp.sum(lq[2] * lq[3])) + lam_init
    o = diff_attention(q, k_all, v_all, lam)
    a = (rms_norm(o, subln_g) * (1 - lam_init)).reshape(B, L, ATTN_WIDTH)
    p = pool_mix(up, w_pool, pool_scale)
    f = fourier_mix(uf)
    ga, gp, gf = jnp.split(jax.nn.sigmoid(h @ w_gate), 3, axis=-1)
    m = ga * (a @ w_pa) + gp * (p @ w_pp) + gf * (f @ w_pf)
    return m @ w_out, k_out, v_out


def trunk_layer(x, cond, ctx_k, ctx_v, lam_init, w_ada, b_ada, norm_g, ffn_w13, ffn_w2, w_in,
                q_norm_g, k_norm_g, lam_qk, subln_g, w_pool, pool_scale, w_gate, w_pa, w_pp, w_pf, w_out):
    mod = (jax.nn.silu(cond) @ w_ada + b_ada).reshape(cond.shape[0], 1, N_MOD, D_MODEL)
    h = modulate(x, norm_g[0], mod[:, :, 0], mod[:, :, 1])
    x = x + 0.5 * mod[:, :, 2] * swiglu(h, ffn_w13[0], ffn_w2[0])
    h = modulate(x, norm_g[1], mod[:, :, 3], mod[:, :, 4])
    m, k_out, v_out = token_mixer(h, ctx_k, ctx_v, lam_init, w_in, q_norm_g, k_norm_g, lam_qk, subln_g,
                                  w_pool, pool_scale, w_gate, w_pa, w_pp, w_pf, w_out)
    x = x + mod[:, :, 5] * m
    h = modulate(x, norm_g[2], mod[:, :, 6], mod[:, :, 7])
    x = x + 0.5 * mod[:, :, 8] * swiglu(h, ffn_w13[1], ffn_w2[1])
    return x, k_out, v_out


def setup_inputs(seed: int = 0) -> dict:
    key = jax.random.key(seed)
    ks = jax.random.split(key, 24)
    D = D_MODEL

    def nrm(k, shape, scale):
        return jax.random.normal(k, shape, jnp.float32) * scale

    return {
        'x_prompt': nrm(ks[0], (BATCH, SEQ, D), 1.0),
        'x_sample': nrm(ks[1], (DEC_BATCH, DEC_SEQ, D), 1.0),
        'cache_k': nrm(ks[2], (DEC_BATCH, DEPTH, PAST_LEN, N_HEADS, 2 * HEAD_DIM), 1.0),
        'cache_v': nrm(ks[3], (DEC_BATCH, DEPTH, PAST_LEN, N_HEADS, V_DIM), 1.0),
        'c': nrm(ks[4], (DEC_BATCH, D), 1.0),
        'c_ctx': nrm(ks[5], (D,), 1.0),
        'w_ada': nrm(ks[6], (DEPTH, D, N_MOD * D), 0.5 * D ** -0.5),
        'b_ada': nrm(ks[7], (DEPTH, N_MOD * D), 0.02),
        'norm_g': 1.0 + nrm(ks[8], (DEPTH, 3, D), 0.02),
        'ffn_w13': nrm(ks[9], (DEPTH, 2, D, 2 * D_FF), D ** -0.5),
        'ffn_w2': nrm(ks[10], (DEPTH, 2, D_FF, D), D_FF ** -0.5),
        'w_in': nrm(ks[11], (DEPTH, D, IN_WIDTH), D ** -0.5),
        'q_norm_g': 1.0 + nrm(ks[12], (DEPTH, HEAD_DIM), 0.02),
        'k_norm_g': 1.0 + nrm(ks[13], (DEPTH, HEAD_DIM), 0.02),
        'lam_qk': nrm(ks[14], (DEPTH, 4, HEAD_DIM), 0.1),
        'subln_g': 1.0 + nrm(ks[15], (DEPTH, V_DIM), 0.02),
        'w_pool': nrm(ks[16], (DEPTH, len(POOL_WINDOWS), POOL_GROUP, POOL_GROUP), POOL_GROUP ** -0.5),
        'pool_scale': 1.0 + nrm(ks[17], (DEPTH, POOL_WIDTH), 0.02),
        'w_gate': nrm(ks[18], (DEPTH, D, 3 * D), D ** -0.5),
        'w_pa': nrm(ks[19], (DEPTH, ATTN_WIDTH, D), ATTN_WIDTH ** -0.5),
        'w_pp': nrm(ks[20], (DEPTH, POOL_WIDTH, D), POOL_WIDTH ** -0.5),
        'w_pf': nrm(ks[21], (DEPTH, FOURIER_WIDTH, D), FOURIER_WIDTH ** -0.5),
        'w_out': nrm(ks[22], (DEPTH, D, D), D ** -0.5),
    }


def reference(x_prompt, x_sample, cache_k, cache_v, c, c_ctx, w_ada, b_ada, norm_g, ffn_w13, ffn_w2,
              w_in, q_norm_g, k_norm_g, lam_qk, subln_g, w_pool, pool_scale, w_gate, w_pa, w_pp, w_pf, w_out):
    cond_ctx = c_ctx[None]
    xp, xs = x_prompt, x_sample
    new_k, new_v = [], []
    for l in range(DEPTH):
        lam_init = 0.8 - 0.6 * math.exp(-0.3 * l)
        xp, k_l, v_l = trunk_layer(
            xp, cond_ctx, None, None, lam_init, w_ada[l], b_ada[l], norm_g[l], ffn_w13[l], ffn_w2[l],
            w_in[l], q_norm_g[l], k_norm_g[l], lam_qk[l], subln_g[l], w_pool[l], pool_scale[l],
            w_gate[l], w_pa[l], w_pp[l], w_pf[l], w_out[l])
        new_k.append(k_l)
        new_v.append(v_l)
        xs, _, _ = trunk_layer(
            xs, c, cache_k[:, l], cache_v[:, l], lam_init, w_ada[l], b_ada[l], norm_g[l], ffn_w13[l],
            ffn_w2[l], w_in[l], q_norm_g[l], k_norm_g[l], lam_qk[l], subln_g[l], w_pool[l], pool_scale[l],
            w_gate[l], w_pa[l], w_pp[l], w_pf[l], w_out[l])
    state_k = jnp.stack(new_k, axis=1)
    state_v = jnp.stack(new_v, axis=1)
    return (xp, xs, state_k, state_v)
```

```python
import math
from contextlib import ExitStack
import numpy as np
import ml_dtypes
import concourse.bass as bass
import concourse.mybir as mybir
from concourse.bass_utils import run_bass_kernel_spmd

F32 = mybir.dt.float32
BF = mybir.dt.bfloat16
AF = mybir.ActivationFunctionType
ALU = mybir.AluOpType
AX = mybir.AxisListType

D = 1024
NH = 8
DFF = 2816
EPS = 1e-6
GRID_W = 64
POOLW = (2, 4, 8, 16)


def dsz(dt):
    return 4 if dt == F32 else 2


class Op:
    __slots__ = ("eng", "fn", "deps", "dma_sem", "ndma", "count", "sig", "sigsem", "sigcnt", "waits")

    def __init__(self, eng, fn, dma_sem, ndma):
        self.eng = eng
        self.fn = fn
        self.deps = set()
        self.dma_sem = dma_sem
        self.ndma = ndma
        self.count = 0
        self.sig = False
        self.sigsem = None
        self.sigcnt = 0
        self.waits = None


class Sched:
    EPOCH = 8000

    def __init__(self, tracked):
        self.ops = []
        self.cells = {}
        self.tracked = tracked
        self.dma_counts = {}
        self.last_dma_on_sem = {}

    def cells_of(self, ap):
        nm = ap.tensor.name
        info = self.tracked.get(nm)
        if info is None:
            return ()
        gran, rowbytes = info
        d = dsz(ap.dtype)
        pat = ap.ap
        row_elems = rowbytes // d
        off = int(ap.offset) % row_elems
        dims = [(int(s), int(n)) for (s, n) in pat[1:]]
        if not dims:
            dims = [(1, 1)]
        inner_s, inner_n = dims[-1]
        outer = dims[:-1]
        nouter = 1
        for s, n in outer:
            nouter *= n
        if inner_s in (0, 1):
            ilen = inner_n if inner_s == 1 else 1
        else:
            ilen = inner_s * (inner_n - 1) + 1
        runs = []
        if nouter <= 256:
            offs = [off]
            for s, n in outer:
                offs = [o + s * i for o in offs for i in range(n)]
            for o in offs:
                runs.append((o, o + ilen))
        else:
            hi = off + ilen
            for s, n in outer:
                hi += s * (n - 1)
            runs.append((off, hi))
        out = set()
        for lo, hi in runs:
            b0 = (lo * d) // gran
            b1 = (hi * d - 1) // gran
            for b in range(b0, b1 + 1):
                out.add((nm, b))
        return out

    def add(self, eng, fn, reads=(), writes=(), dma_sem=None, ndma=0, extra_deps=()):
        idx = len(self.ops)
        op = Op(eng, fn, dma_sem, ndma)
        deps = set(extra_deps)
        rc = set()
        for a in reads:
            rc |= set(self.cells_of(a))
        wc = set()
        for a in writes:
            wc |= set(self.cells_of(a))
        cells = self.cells
        rkey = eng if dma_sem is None else ("dma", dma_sem)
        for c in rc:
            st = cells.get(c)
            if st is not None:
                if st[0] is not None:
                    deps.add(st[0])
                if c[0].startswith("ps"):
                    for rk, ri in st[1].items():
                        if rk != rkey:
                            deps.add(ri)
        for c in wc:
            st = cells.get(c)
            if st is not None:
                if st[0] is not None:
                    deps.add(st[0])
                deps.update(st[1].values())
        rkey = eng if dma_sem is None else ("dma", dma_sem)
        for c in rc:
            if c in wc:
                continue
            st = cells.get(c)
            if st is None:
                cells[c] = [None, {rkey: idx}]
            else:
                st[1][rkey] = idx
        for c in wc:
            cells[c] = [idx, {}]
        if dma_sem is not None:
            prev = self.last_dma_on_sem.get(dma_sem)
            if prev is not None:
                deps.add(prev)
            self.last_dma_on_sem[dma_sem] = idx
            cnt = self.dma_counts.get(dma_sem, 0) + 16 * ndma
            self.dma_counts[dma_sem] = cnt
            op.count = cnt
        deps.discard(idx)
        op.deps = deps
        self.ops.append(op)
        return idx

    def finalize(self):
        ops = self.ops
        for op in ops:
            nd = set()
            for d in op.deps:
                dop = ops[d]
                if dop.dma_sem is None and dop.eng == "pe" and op.eng == "pe" and op.dma_sem is None:
                    continue
                nd.add(d)
                if dop.dma_sem is None:
                    dop.sig = True
            op.deps = nd
        cnt = {}
        for op in ops:
            if op.dma_sem is None and op.sig:
                c = cnt.get(op.eng, 0)
                ep, within = divmod(c, self.EPOCH)
                op.sigsem = ("eng", op.eng, ep)
                op.sigcnt = within + 1
                cnt[op.eng] = c + 1
        semkeys = set()
        for op in ops:
            w = {}
            for d in op.deps:
                dop = ops[d]
                if dop.dma_sem is not None:
                    k = ("dma", dop.dma_sem)
                    v = dop.count
                else:
                    k = dop.sigsem
                    v = dop.sigcnt
                if w.get(k, 0) < v:
                    w[k] = v
                semkeys.add(k)
            op.waits = w
            if op.dma_sem is not None:
                semkeys.add(("dma", op.dma_sem))
            elif op.sig:
                semkeys.add(op.sigsem)
        return sorted(semkeys, key=str)

    def emit(self, eng, e, sems):
        waited = {}
        for op in self.ops:
            if op.eng != eng:
                continue
            for k, v in op.waits.items():
                if waited.get(k, 0) < v:
                    e.wait_ge(sems[k], v)
                    waited[k] = v
            if op.dma_sem is not None:
                op.fn(e, sems[("dma", op.dma_sem)])
            else:
                ins = op.fn(e)
                if op.sig:
                    ins.then_inc(sems[op.sigsem], 1)


CB_OFF = 0
CF_OFF = 1280
PAR_OFF = 2560
X_OFF = 7680
H_OFF = X_OFF + 65536
R1 = H_OFF + 32768
R2 = R1 + 32768
R3 = R2 + 32768
ARENA = 212736
R3SZ = ARENA - R3
assert R3SZ >= 40960

P_MODT = PAR_OFF
P_BT = P_MODT + 1152
P_GT = P_BT + 576
P_CT = P_GT + 192
P_SCT = P_CT + 64
P_GQK = P_SCT + 64
P_PST = P_GQK + 64
P_NLAM = P_PST + 64
P_GSUB = P_NLAM + 64
P_DV = P_GSUB + 1024
P_MISC = P_DV + 256
assert P_MISC + 512 <= X_OFF


class Phase:
    def __init__(self, name, T, seqs, ci, sample):
        self.name = name
        self.T = T
        self.nblk = T // 512
        self.seqs = seqs
        self.ci = ci
        self.sample = sample
        self.nk = T + 512 if sample else T


class K:
    def __init__(self, nc, es, cfg):
        self.nc = nc
        self.cfg = cfg
        self.arena = es.enter_context(nc.sbuf_tensor("arena", [128, ARENA // 2], BF))
        self.banks = [es.enter_context(nc.psum_tensor(f"ps{i}", [128, 512], F32)) for i in range(8)]
        tracked = {"arena": (256, ARENA)}
        for i in range(8):
            tracked[f"ps{i}"] = (2048, 2048)
        self.S = Sched(tracked)
        self.out_ops = []

    def av(self, off, dt, *dims, p0=0, p1=128):
        n = 1
        for x in dims:
            n *= int(x)
        nb = n * dsz(dt)
        assert off % 4 == 0 and off + nb <= ARENA, (off, nb)
        ap = self.arena[p0:p1, off // 2:(off + nb) // 2]
        if dt != BF:
            ap = ap.bitcast(dt)
        if len(dims) > 1:
            names = " ".join(f"d{i}" for i in range(len(dims)))
            ap = ap.rearrange(f"p ({names}) -> p {names}", **{f"d{i}": int(dims[i]) for i in range(len(dims))})
        return ap

    def ps(self, bank, dt=F32, *dims):
        ap = self.banks[bank][:]
        if dt != F32:
            ap = ap.bitcast(dt)
        tot = 512 if dt == F32 else 1024
        n = 1
        for x in dims:
            n *= int(x)
        if not dims:
            return ap
        ap = ap[:, 0:n]
        if len(dims) > 1:
            names = " ".join(f"d{i}" for i in range(len(dims)))
            ap = ap.rearrange(f"p ({names}) -> p {names}", **{f"d{i}": int(dims[i]) for i in range(len(dims))})
        return ap

    def mm(self, out, lhsT, rhs, start=True, stop=True, skip=False):
        if skip:
            fn = lambda e: e.matmul(out, lhsT, rhs, start=start, stop=stop, skip_group_check=True)
        else:
            fn = lambda e: e.matmul(out, lhsT, rhs, start=start, stop=stop)
        self.S.add("pe", fn, [lhsT, rhs], [out])

    def tr(self, out, in_, ident):
        self.S.add("pe", lambda e: e.transpose(out, in_, ident), [in_, ident], [out])

    def act(self, out, in_, func, bias=None, scale=None, accum=None):
        kw = {}
        reads = [in_]
        if bias is not None:
            kw["bias"] = bias
            if not isinstance(bias, (int, float)):
                reads.append(bias)
        if scale is not None:
            kw["scale"] = scale
            if not isinstance(scale, (int, float)):
                reads.append(scale)
        writes = [out]
        if accum is not None:
            kw["accum_out"] = accum
            writes.append(accum)
        self.S.add("act", lambda e: e.activation(out, in_, func, **kw), reads, writes)

    def tt(self, out, a, b, op, eng="dve"):
        self.S.add(eng, lambda e: e.tensor_tensor(out, a, b, op), [a, b], [out])

    def ts(self, out, a, s1, s2, op0, op1=None, eng="dve"):
        reads = [a]
        if not isinstance(s1, (int, float)):
            reads.append(s1)
        if s2 is not None and not isinstance(s2, (int, float)):
            reads.append(s2)
        if op1 is None:
            fn = lambda e: e.tensor_scalar(out, a, s1, None, op0)
        else:
            fn = lambda e: e.tensor_scalar(out, a, s1, s2, op0, op1)
        self.S.add(eng, fn, reads, [out])

    def stt(self, out, a, s, b, op0, op1, eng="dve"):
        reads = [a, b]
        if not isinstance(s, (int, float)):
            reads.append(s)
        self.S.add(eng, lambda e: e.scalar_tensor_tensor(out, a, s, b, op0, op1), reads, [out])

    def cp(self, out, in_, eng="dve"):
        if eng == "act":
            self.S.add("act", lambda e: e.activation(out, in_, AF.Copy), [in_], [out])
        else:
            self.S.add(eng, lambda e: e.tensor_copy(out, in_), [in_], [out])

    def rcp(self, out, in_):
        self.S.add("dve", lambda e: e.reciprocal(out, in_), [in_], [out])

    def rsum(self, out, in_):
        self.S.add("dve", lambda e: e.reduce_sum(out, in_, AX.X), [in_], [out])

    def memset(self, out, val, eng="pool"):
        self.S.add(eng, lambda e: e.memset(out, val), [], [out])

    def dma(self, eng, sem, pairs, is_out=False):
        pairs = list(pairs)

        def fn(e, s):
            for o, i in pairs:
                e.dma_start(out=o, in_=i).then_inc(s, 16)
        idx = self.S.add(eng, fn, [i for o, i in pairs], [o for o, i in pairs], dma_sem=sem, ndma=len(pairs))
        if is_out:
            self.out_ops.append(idx)
        return idx

    def setup_consts(self, d):
        self.cb = self.av(CB_OFF, BF, 640)
        self.cf = self.av(CF_OFF, F32, 288)
        self.dma("sp", "c0", [(self.cb, d["cb"]), (self.cf, d["cf"])])
        self.identb = self.cb[:, 0:128]
        self.onesD = self.cb[:, 128:256]
        self.bones = self.cb[:, 256:384]
        self.CS = self.cb[:, 384:640]
        self.identf = self.cf[:, 0:128]
        self.Rm = self.cf[:, 128:256]
        self.ptab = self.cf[:, 256:288]

    def load_T(self, dst, src_rows, r, stage_off, bank):
        st = self.av(stage_off, F32, 128, p1=r)
        self.dma("sp", "ldT", [(st, src_rows)])
        pst = self.ps(bank)[:, 0:r]
        self.tr(pst, st, self.identf[0:r, 0:r])
        self.cp(dst, pst)

    def stage_params(self, d):
        S0 = R3 + 36864
        self.modT = [self.av(P_MODT + l * 576, F32, 72, 2) for l in range(2)]
        self.bT = [self.av(P_BT + l * 288, F32, 72) for l in range(2)]
        self.gT = [self.av(P_GT + l * 96, F32, 24) for l in range(2)]
        self.cT = self.av(P_CT, F32, 16)
        self.scT = self.av(P_SCT, BF, 2, 8)
        self.gqk = [self.av(P_GQK + l * 8, F32, 2) for l in range(2)]
        self.psT = [self.av(P_PST + l * 16, F32, 4) for l in range(2)]
        self.nlam = [self.av(P_NLAM + l * 4, F32, 1) for l in range(2)]
        self.gsub = [self.av(P_GSUB + l * 512, F32, 128) for l in range(2)]
        self.dvs = [self.av(P_DV, F32, 5, 8), self.av(P_MISC + 320, F32, 5, 8)]
        self.load_T(self.cT, d["cond"], 16, S0, 0)
        self.act(self.scT.rearrange("p a b -> p (a b)"), self.cT, AF.Silu)
        for l in range(2):
            self.load_T(self.bT[l], d["b_ada"][l], 72, S0, 0)
            self.load_T(self.gT[l], d["norm_g"][l], 24, S0, 0)
            self.load_T(self.psT[l], d["pool_scale"][l], 4, S0, 0)
            qg = d["q_norm_g"][l:l + 1, :].rearrange("o d -> d o")
            kg = d["k_norm_g"][l:l + 1, :].rearrange("o d -> d o")
            self.dma("sp", "gqk", [(self.gqk[l][0:64, 0:1], qg), (self.gqk[l][64:128, 0:1], qg),
                                   (self.gqk[l][0:64, 1:2], kg), (self.gqk[l][64:128, 1:2], kg)])
            lq = self.av(S0 + 1024, F32, 4, 64)
            self.dma("sp", "lq", [(lq.rearrange("p a b -> p (a b)"), d["lam_qk"][l:l + 1, :].partition_broadcast(128))])
            pr = self.av(S0 + 2048, F32, 2, 64)
            self.tt(pr[:, 0, :], lq[:, 0, :], lq[:, 1, :], ALU.mult)
            self.tt(pr[:, 1, :], lq[:, 2, :], lq[:, 3, :], ALU.mult)
            sm = self.av(P_MISC, F32, 2)
            self.rsum(sm, pr)
            ex = self.av(P_MISC + 64, F32, 2)
            self.act(ex, sm, AF.Exp)
            lam_init = 0.8 - 0.6 * math.exp(-0.3 * l)
            self.tt(self.nlam[l], ex[:, 1:2], ex[:, 0:1], ALU.subtract)
            self.ts(self.nlam[l], self.nlam[l], -lam_init, None, ALU.add)
            self.dma("sp", "gsub", [(self.gsub[l], d["subln_g"][l:l + 1, :].partition_broadcast(128))])
            self.ts(self.gsub[l], self.gsub[l], 1.0 - lam_init, None, ALU.mult)
        self.nsl = 0

    def stage_mod(self, d, l):
        if True:
            mps = self.ps(6, F32, 72, 2)
            wv = d["w_ada"][l].rearrange("(kc p) n -> p kc n", p=128)
            for sb in range(18):
                slab = self.av(R3 + (self.nsl % 2) * 8192, BF, 8, 512)
                self.dma("pool", f"wada{self.nsl % 2}", [(slab, wv[:, :, sb * 512:(sb + 1) * 512])])
                self.nsl += 1
                for jj in range(4):
                    j = sb * 4 + jj
                    for kc in range(8):
                        self.mm(mps[:, j, :], slab[:, kc, jj * 128:(jj + 1) * 128], self.scT[:, :, kc],
                                start=(kc == 0), stop=(kc == 7))
            for ci in range(2):
                self.tt(self.modT[l][:, :, ci], mps[:, :, ci], self.bT[l], ALU.add)

    def stage_mod_gen(self, d, l, off0, off1, bank):
        wv = d["w_ada"][l].rearrange("(kc p) n -> p kc n", p=128)
        offs = (off0, off1)
        slabs = {}

        def issue(sb):
            slab = self.av(offs[sb % 2], BF, 8, 512)
            self.dma("pool", f"wadab{sb % 2}", [(slab, wv[:, :, sb * 512:(sb + 1) * 512])])
            slabs[sb] = slab
        issue(0)
        for sb in range(18):
            if sb + 1 < 18:
                issue(sb + 1)
            yield
            slab = slabs.pop(sb)
            mps = self.ps(bank, F32, 4, 2)
            for jj in range(4):
                for kc in range(8):
                    self.mm(mps[:, jj, :], slab[:, kc, jj * 128:(jj + 1) * 128], self.scT[:, :, kc],
                            start=(kc == 0), stop=(kc == 7))
            for ci in range(2):
                self.tt(self.modT[l][:, sb * 4:sb * 4 + 4, ci], mps[:, :, ci], self.bT[l][:, sb * 4:sb * 4 + 4], ALU.add)
            yield

    def derive(self, ph, l):
        m = self.modT[l]
        ci = ph.ci
        dv = self.dvs[l % 2]
        for k in range(3):
            sc = m[:, (3 * k + 1) * 8:(3 * k + 2) * 8, ci]
            self.stt(dv[:, k, :], sc, 1.0, self.gT[l][:, k * 8:(k + 1) * 8], ALU.add, ALU.mult)
        self.ts(dv[:, 3, :], m[:, 16:24, ci], 0.5, None, ALU.mult)
        self.ts(dv[:, 4, :], m[:, 64:72, ci], 0.5, None, ALU.mult)

    def shift(self, l, ph, k):
        return self.modT[l][:, (3 * k) * 8:(3 * k + 1) * 8, ph.ci]

    def views(self, ph):
        T = ph.T
        self.X3 = self.av(X_OFF, F32, 8, T)
        self.H3 = self.av(H_OFF, BF, 8, T)

    def load_x(self, ph, xd):
        for i in range(ph.T // 128):
            st = self.av(R3 + (i % 2) * 4096, F32, 1024)
            self.dma("sp", f"xin{i % 2}", [(st, xd[i * 128:(i + 1) * 128, :])])
            b0 = (i % 2) * 2
            for c in range(8):
                self.tr(self.ps(b0 + c // 4)[:, (c % 4) * 128:(c % 4 + 1) * 128], st[:, c * 128:(c + 1) * 128], self.identf)
            cols = slice(i * 128, (i + 1) * 128)
            self.cp(self.X3[:, 0:4, cols], self.ps(b0, F32, 4, 128), eng="dve")
            self.cp(self.X3[:, 4:8, cols], self.ps(b0 + 1, F32, 4, 128), eng="act")

    def store_x(self, ph, yd):
        for i in range(ph.T // 128):
            st = self.av(R3 + (i % 2) * 4096, F32, 1024)
            b0 = (i % 2) * 2
            cols = slice(i * 128, (i + 1) * 128)
            for c in range(8):
                self.tr(self.ps(b0 + c // 4)[:, (c % 4) * 128:(c % 4 + 1) * 128], self.X3[:, c, cols], self.identf)
            self.cp(st[:, 0:512], self.ps(b0), eng="dve")
            self.cp(st[:, 512:1024], self.ps(b0 + 1), eng="act")
            self.dma("sp", f"xout{i % 2}", [(yd[i * 128:(i + 1) * 128, :], st)], is_out=True)

    def rms_mod(self, ph, A_vec, shift_vec, so):
        for tb in range(ph.nblk):
            self.rms_a(ph, so, tb)
            self.rms_b(ph, A_vec, shift_vec, so, tb)

    def rms_a(self, ph, so, tb):
        cols = slice(tb * 512, (tb + 1) * 512)
        for c in range(8):
            sq = self.av(so + c * 1024, BF, 512)
            self.act(sq, self.X3[:, c, cols], AF.Square)

    def rms_b(self, ph, A_vec, shift_vec, so, tb):
        cols = slice(tb * 512, (tb + 1) * 512)
        ss = self.ps(7 - tb % 2)
        for c in range(8):
            sq = self.av(so + c * 1024, BF, 512)
            self.mm(ss, self.onesD, sq, start=(c == 0), stop=(c == 7))
        sd = self.av(so + 8192, F32, 512)
        self.act(sd, ss, AF.Sqrt, bias=self.epsv, scale=1.0)
        self.rcp(sd, sd)
        for c in range(8):
            tmp = self.av(so + 10240 + (c % 2) * 2048, F32, 512)
            self.tt(tmp, self.X3[:, c, cols], sd, ALU.mult)
            self.act(self.H3[:, c, cols], tmp, AF.Identity, bias=shift_vec[:, c:c + 1], scale=A_vec[:, c:c + 1])

    def ffn(self, ph, l, f, d, hg_vec, next_rms=None):
        T = ph.T
        w13 = d["ffn_w13"][l, f].rearrange("(kc p) n -> p kc n", p=128)
        w2d = d["ffn_w2"][l, f]
        ACT3 = self.av(R1, BF, 11, T)
        SL = R1 + 45056
        w2 = self.av(R3, BF, 11, 1024)
        SG = R3 + 22528
        for half in range(2):
            ch0 = half * 11
            w2_issued = False
            it = 0
            per = 2 if ph.sample else 4
            SLp = SL if ph.sample else R1 + 16384
            for s0 in range(0, 11, per):
                nj = min(per, 11 - s0)
                n = nj * 128
                sidx = self.slabctr % (2 if ph.sample else 3)
                self.slabctr += 1
                slab = self.av(SLp + sidx * 4096 * per, BF, 8, 2, 128 * per)
                c0 = (ch0 + s0) * 128
                self.dma("pool", f"w13_{sidx}", [(slab[:, :, 0, 0:n], w13[:, :, c0:c0 + n]),
                                                  (slab[:, :, 1, 0:n], w13[:, :, DFF + c0:DFF + c0 + n])])
                if s0 >= per and not w2_issued:
                    self.dma("pool", "w2", [(w2, w2d[ch0 * 128:(ch0 + 11) * 128, :].rearrange("(j p) n -> p j n", p=128))])
                    w2_issued = True
                for jj in range(nj):
                    jl = s0 + jj
                    for tb in range(ph.nblk):
                        cols = slice(tb * 512, (tb + 1) * 512)
                        g_ps = self.ps(it % 2)
                        u_ps = self.ps(2 + it % 2)
                        for kc in range(8):
                            self.mm(g_ps, slab[:, kc, 0, jj * 128:(jj + 1) * 128], self.H3[:, kc, cols], start=(kc == 0), stop=(kc == 7))
                        for kc in range(8):
                            self.mm(u_ps, slab[:, kc, 1, jj * 128:(jj + 1) * 128], self.H3[:, kc, cols], start=(kc == 0), stop=(kc == 7))
                        sg = self.av(SG + (it % 2) * 2048, F32, 512)
                        self.act(sg, g_ps, AF.Silu)
                        self.tt(ACT3[:, jl, cols], sg, u_ps, ALU.mult)
                        it += 1
            it = 0
            if half == 0 or next_rms is None:
                order = [(c, tb) for c in range(8) for tb in range(ph.nblk)]
            else:
                order = [(c, tb) for tb in range(ph.nblk) for c in range(8)]
            for (c, tb) in order:
                cols = slice(tb * 512, (tb + 1) * 512)
                o_ps = self.ps(4 + it % 2)
                for jl in range(11):
                    self.mm(o_ps, w2[:, jl, c * 128:(c + 1) * 128], ACT3[:, jl, cols], start=(jl == 0), stop=(jl == 10))
                self.stt(self.X3[:, c, cols], o_ps, hg_vec[:, c:c + 1], self.X3[:, c, cols], ALU.mult, ALU.add)
                it += 1
                if half == 1 and next_rms is not None and c == 7:
                    if tb >= 1:
                        next_rms[1](tb - 1)
                    next_rms[0](tb)
            if half == 1 and next_rms is not None:
                next_rms[1](ph.nblk - 1)

    def rsqrt_act(self, out, in_, scale):
        self.act(out, in_, AF.Ln, bias=self.epsv, scale=scale)
        self.act(out, out, AF.Exp, scale=-0.5)

    def qk_prep_gen(self, ph, ps_in, gcol, outs, so, rope, kn_keep=None, sq_done=False):
        n = 512
        sq = self.av(so, BF, n)
        if not sq_done:
            self.act(sq, ps_in, AF.Square)
            yield
        ss = self.ps(3)
        self.mm(ss, self.bones, sq)
        sd = self.av(so + 1024, F32, n)
        self.rsqrt_act(sd, ss, 1.0)
        qn = kn_keep if kn_keep is not None else self.av(so + 3072, F32, n)
        self.stt(qn, ps_in, gcol, sd, ALU.mult, ALU.mult)
        if not ph.sample:
            for (p0, p1, dst) in outs:
                self.cp(dst, qn[p0:p1, :], eng="act")
            return
        yield
        rot = self.ps(3)
        self.mm(rot, self.Rm, qn)
        t1 = self.av(so + 5120, F32, n)
        t2 = self.av(so + 7168, F32, n)
        self.tt(t1, qn, rope[:, 0, :], ALU.mult, eng="pool")
        self.tt(t2, rot, rope[:, 1, :], ALU.mult)
        for (p0, p1, dst) in outs:
            self.tt(dst, t1[p0:p1, :], t2[p0:p1, :], ALU.add)

    def attention(self, ph, l, d, pidx, extra=None):
        T = ph.T
        A3 = self.av(R1, BF, 8, T)
        w_in = d["w_in"][l].rearrange("(kc p) n -> p kc n", p=128)
        HBSZ = 18944
        B0 = R2
        Ebase = B0 + 2 * HBSZ
        ROPE = Ebase + 3072
        SLABO = ROPE + 8192
        SO = SLABO + 6144
        CKST = SO + 7168
        SM = SO + 9216
        assert SM + 4096 <= ARENA and SM % 256 == 0
        ropedr = d["rope"].rearrange("p (a t) -> p a t", a=2)
        if not ph.sample:
            sk_st = self.av(ROPE, F32, 4, 128)
            sv_st = self.av(ROPE + 2048, F32, 4, 128)
        hbufs = []
        for hb in range(2):
            base = B0 + hb * HBSZ
            qz = self.av(base, BF, 2, T)
            kT = self.av(base + 8192, BF, ph.nk)
            vx = self.av(base + 13312, BF, 20, 136)
            self.memset(vx[:, :, 128:130], 1.0)
            self.memset(qz[64:128, 0, :], 0.0)
            self.memset(qz[0:64, 1, :], 0.0)
            hbufs.append((qz, kT, vx))
        koff = 512 if ph.sample else 0
        vch0 = 4 if ph.sample else 0
        ropectr = [0]

        def proj_gen(h):
            hb = h % 2
            qz, kT, vx = hbufs[hb]
            slab = self.av(SLABO, BF, 8, 3, 128)
            self.dma("pool", "win", [(slab[:, :, i, :], w_in[:, :, i * 1024 + h * 128:i * 1024 + (h + 1) * 128]) for i in range(3)])
            if ph.sample:
                ckst = self.av(CKST, BF, 4, 128)
                self.dma("pool", "ck", [(ckst, d["ck"][l, :, h, :].rearrange("(i p) e -> p i e", p=128))])
                self.dma("pool", f"cv{hb}", [(vx[:, 0:4, 0:128], d["cv"][l, :, h, :].rearrange("(i p) e -> p i e", p=128))])
                yield
                pb = self.ps(3, BF, 4, 128)
                for i in range(4):
                    self.tr(pb[:, i, :], ckst[:, i, :], self.identb)
                self.cp(kT[:, 0:512], pb.rearrange("p a b -> p (a b)"), eng="act")
            for tb in range(ph.nblk):
                cols = slice(tb * 512, (tb + 1) * 512)
                rope = None
                if ph.sample:
                    ri = ropectr[0] % 2
                    ropectr[0] += 1
                    rope = self.av(ROPE + ri * 4096, F32, 2, 512)
                    self.dma("sp", f"rope{ri}", [(rope, ropedr[:, :, cols])])
                yield
                q_ps = self.ps(0)
                k_ps = self.ps(1)
                v_ps = self.ps(2, F32, 4, 128)
                for kc in range(8):
                    self.mm(q_ps, slab[:, kc, 0, :], self.H3[:, kc, cols], start=(kc == 0), stop=(kc == 7))
                    yield
                for kc in range(8):
                    self.mm(k_ps, slab[:, kc, 1, :], self.H3[:, kc, cols], start=(kc == 0), stop=(kc == 7))
                    yield
                for i in range(4):
                    tc_ = slice(tb * 512 + i * 128, tb * 512 + (i + 1) * 128)
                    for kc in range(8):
                        self.mm(v_ps[:, i, :], self.H3[:, kc, tc_], slab[:, kc, 2, :], start=(kc == 0), stop=(kc == 7))
                        if kc % 4 == 3:
                            yield
                self.cp(vx[:, vch0 + tb * 4:vch0 + tb * 4 + 4, 0:128], v_ps, eng="dve")
                if not ph.sample:
                    self.cp(sv_st, v_ps)
                    self.dma("sp", "sv", [(d["sv"][b, l, :, h, :].rearrange("(i p) e -> p i e", p=128), sv_st[:, 2 * b:2 * b + 2, :]) for b in range(2)], is_out=True)
                yield from self.qk_prep_gen(ph, q_ps, self.gqk[l][:, 0:1], [(0, 64, qz[0:64, 0, cols]), (64, 128, qz[64:128, 1, cols])], SO, rope)
                kn = None if ph.sample else self.av(SO + 3072, F32, 512)
                kdst = kT[:, koff + tb * 512:koff + (tb + 1) * 512]
                yield from self.qk_prep_gen(ph, k_ps, self.gqk[l][:, 1:2], [(0, 128, kdst)], SO, rope, kn_keep=kn)
                if not ph.sample:
                    yield
                    pk = self.ps(3, F32, 4, 128)
                    for i in range(4):
                        self.tr(pk[:, i, :], kn[:, i * 128:(i + 1) * 128], self.identf)
                    self.cp(sk_st, pk)
                    self.dma("sp", "sk", [(d["sk"][b, l, :, h, :].rearrange("(i p) e -> p i e", p=128), sk_st[:, 2 * b:2 * b + 2, :]) for b in range(2)], is_out=True)

        eit = [0]

        def attn(h, bg):
            qz, kT, vx = hbufs[h % 2]
            its = []
            for (t0, L) in ph.seqs:
                if ph.sample:
                    kch = [(kc, kc * 128) for kc in range(20)]
                else:
                    kch = [(t0 // 128 + j, t0 + j * 128) for j in range(L // 128)]
                for qb in range(L // 256):
                    for ki, (vc, kc0) in enumerate(kch):
                        its.append((t0 + qb * 256, ki, len(kch), vc, kc0))
            Es = {}
            pend = []

            def emitS(n):
                q0, ki, nk, vc, kc0 = its[n]
                e = eit[0]
                eit[0] += 1
                Sp = self.ps(4 + e % 2, F32, 2, 256)
                self.mm(Sp, kT[:, kc0:kc0 + 128], qz[:, :, q0:q0 + 256])
                E = self.av(Ebase + (e % 3) * 1024, BF, 2, 256)
                self.act(E, Sp, AF.Exp, scale=0.125)
                Es[n] = E
            emitS(0)
            if len(its) > 1:
                emitS(1)
            O1 = self.ps(6, F32, 2, 130)
            O2 = self.ps(7, F32, 2, 130)
            for n in range(len(its)):
                q0, ki, nk, vc, kc0 = its[n]
                if n + 2 < len(its):
                    emitS(n + 2)
                E = Es.pop(n)
                for mi, O in enumerate((O1, O2)):
                    for j in range(2):
                        self.mm(O[:, j, :], E[:, mi, j * 128:(j + 1) * 128], vx[:, vc, 0:130],
                                start=(ki == 0 and j == 0), stop=(ki == nk - 1), skip=True)
                if bg is not None:
                    next(bg, None)
                if extra is not None:
                    next(extra, None)
                for g in list(pend):
                    try:
                        next(g)
                    except StopIteration:
                        pend.remove(g)
                if ki == nk - 1:
                    for g in pend:
                        for _ in g:
                            pass
                    pend.clear()
                    g = epi_gen(h, q0, O1, O2)
                    next(g)
                    pend.append(g)
            for g in pend:
                for _ in g:
                    pass
            if bg is not None:
                for _ in bg:
                    pass

        def epi_gen(h, q0, O1, O2):
            Oc1 = self.av(SM + 768, F32, 2, 130)
            Oc2 = self.av(SM + 2048, F32, 2, 130)
            self.cp(Oc1, O1, eng="dve")
            self.cp(Oc2, O2, eng="dve")
            yield
            yield
            rz = self.av(SM, F32, 2, 2)
            self.rcp(rz[:, 0, :], Oc1[:, :, 128])
            self.rcp(rz[:, 1, :], Oc2[:, :, 128])
            yield
            self.ts(rz[:, 1, :], rz[:, 1, :], self.nlam[l][:, 0:1], None, ALU.mult)
            yield
            for j in range(2):
                o = Oc1[:, j, 0:128]
                self.ts(o, o, rz[:, 0, j:j + 1], None, ALU.mult)
            yield
            for j in range(2):
                o = Oc1[:, j, 0:128]
                self.stt(o, Oc2[:, j, 0:128], rz[:, 1, j:j + 1], o, ALU.mult, ALU.add)
            yield
            s2s = []
            for j in range(2):
                o = Oc1[:, j, 0:128]
                junk = self.av(SM + 3328, BF, 128)
                s2 = self.av(SM + 256 + j * 256, F32, 1)
                self.S.add("dve", (lambda junk=junk, o=o, s2=s2: (lambda e: e.scalar_tensor_tensor(junk, o, 1.0, o, ALU.mult, ALU.mult, accum_out=s2)))(), [o], [junk, s2])
                s2s.append(s2)
            yield
            yield
            for j in range(2):
                self.act(s2s[j], s2s[j], AF.Ln, bias=self.epsv, scale=1.0 / 128.0)
            yield
            for j in range(2):
                self.act(s2s[j], s2s[j], AF.Exp, scale=-0.5)
            yield
            yield
            ats = []
            for j in range(2):
                at = self.av(SM + 3584 + j * 256, BF, 128)
                self.stt(at, Oc1[:, j, 0:128], s2s[j], self.gsub[l], ALU.mult, ALU.mult)
                ats.append(at)
            yield
            yield
            pt = self.ps(3, BF, 2, 128)
            for j in range(2):
                self.tr(pt[:, j, :], ats[j], self.identb)
            self.cp(A3[:, h, q0:q0 + 256], pt.rearrange("p a b -> p (a b)"))

        for _ in proj_gen(0):
            pass
        for h in range(NH):
            bg = proj_gen(h + 1) if h + 1 < NH else None
            attn(h, bg)
        if extra is not None:
            for _ in extra:
                pass

    def mstage(self, ph, l, d, which, next_rms=None):
        T = ph.T
        w_gate = d["w_gate"][l].rearrange("(kc p) n -> p kc n", p=128)
        w_out = d["w_out"][l].rearrange("(kc p) n -> p kc n", p=128)
        g2 = self.modT[l][:, 40:48, ph.ci]
        wo = self.av(R3, BF, 8, 1024)
        SLB = R3 + 16384
        SG = R3 + 28672
        if which == 1:
            A3 = self.av(R1, BF, 8, T)
            M3 = self.av(R2, BF, 8, T)
            w_pa = d["w_pa"][l].rearrange("(kc p) n -> p kc n", p=128)
        else:
            P3 = self.av(R2, BF, 4, T)
            F3 = self.av(R2 + 16384, BF, 4, T)
            M3 = self.av(R1, BF, 8, T)
            w_pp = d["w_pp"][l].rearrange("(kc p) n -> p kc n", p=128)
            w_pf = d["w_pf"][l].rearrange("(kc p) n -> p kc n", p=128)
        it = 0
        for c in range(8):
            sidx = c % 2
            cs = slice(c * 128, (c + 1) * 128)
            if which == 1:
                wg = self.av(SLB + sidx * 6144, BF, 8, 128)
                wp = self.av(SLB + sidx * 6144 + 2048, BF, 8, 128)
                self.dma("pool", f"ms{sidx}", [(wg, w_gate[:, :, c * 128:(c + 1) * 128]), (wp, w_pa[:, :, cs])])
            else:
                wgp = self.av(SLB + sidx * 6144, BF, 8, 128)
                wgf = self.av(SLB + sidx * 6144 + 2048, BF, 8, 128)
                wpp = self.av(SLB + sidx * 6144 + 4096, BF, 4, 128)
                wpf = self.av(SLB + sidx * 6144 + 5120, BF, 4, 128)
                self.dma("pool", f"ms{sidx}", [(wgp, w_gate[:, :, 1024 + c * 128:1024 + (c + 1) * 128]),
                                               (wgf, w_gate[:, :, 2048 + c * 128:2048 + (c + 1) * 128]),
                                               (wpp, w_pp[:, :, cs]), (wpf, w_pf[:, :, cs])])
            if c == 1:
                self.dma("pool", "wout", [(wo, w_out)])
            for tb in range(ph.nblk):
                cols = slice(tb * 512, (tb + 1) * 512)
                if which == 1:
                    g_ps = self.ps(it % 2)
                    a_ps = self.ps(2 + it % 2)
                    for kc in range(8):
                        self.mm(g_ps, wg[:, kc, :], self.H3[:, kc, cols], start=(kc == 0), stop=(kc == 7))
                    for kc in range(8):
                        self.mm(a_ps, wp[:, kc, :], A3[:, kc, cols], start=(kc == 0), stop=(kc == 7))
                    sg = self.av(SG + (it % 2) * 2048, F32, 512)
                    self.act(sg, g_ps, AF.Sigmoid)
                    self.tt(M3[:, c, cols], sg, a_ps, ALU.mult)
                else:
                    gp_ps = self.ps(it % 2)
                    gf_ps = self.ps(2 + it % 2)
                    p_ps = self.ps(4)
                    f_ps = self.ps(5)
                    for kc in range(8):
                        self.mm(gp_ps, wgp[:, kc, :], self.H3[:, kc, cols], start=(kc == 0), stop=(kc == 7))
                    for kc in range(8):
                        self.mm(gf_ps, wgf[:, kc, :], self.H3[:, kc, cols], start=(kc == 0), stop=(kc == 7))
                    for kc in range(4):
                        self.mm(p_ps, wpp[:, kc, :], P3[:, kc, cols], start=(kc == 0), stop=(kc == 3))
                    for kc in range(4):
                        self.mm(f_ps, wpf[:, kc, :], F3[:, kc, cols], start=(kc == 0), stop=(kc == 3))
                    sgp = self.av(SG + (it % 2) * 2048, F32, 512)
                    sgf = self.av(SG + 4096 + (it % 2) * 2048, F32, 512)
                    tmp = self.av(SG + 8192, F32, 512)
                    self.act(sgp, gp_ps, AF.Sigmoid)
                    self.act(sgf, gf_ps, AF.Sigmoid)
                    self.tt(tmp, sgp, p_ps, ALU.mult)
                    self.tt(sgf, sgf, f_ps, ALU.mult)
                    self.tt(M3[:, c, cols], tmp, sgf, ALU.add)
                it += 1
        it = 0
        if next_rms is None:
            order = [(c, tb) for c in range(8) for tb in range(ph.nblk)]
        else:
            order = [(c, tb) for tb in range(ph.nblk) for c in range(8)]
        for (c, tb) in order:
            cols = slice(tb * 512, (tb + 1) * 512)
            o_ps = self.ps(6 + it % 2)
            for kc in range(8):
                self.mm(o_ps, wo[:, kc, c * 128:(c + 1) * 128], M3[:, kc, cols], start=(kc == 0), stop=(kc == 7))
            self.stt(self.X3[:, c, cols], o_ps, g2[:, c:c + 1], self.X3[:, c, cols], ALU.mult, ALU.add)
            it += 1
            if next_rms is not None and c == 7:
                if tb >= 1:
                    next_rms[1](tb - 1)
                next_rms[0](tb)
        if next_rms is not None:
            next_rms[1](ph.nblk - 1)

    def fourier(self, ph, l, d):
        T = ph.T
        w_in = d["w_in"][l].rearrange("(kc p) n -> p kc n", p=128)
        U3 = self.av(R2, BF, 4, T)
        F3 = self.av(R2 + 16384, BF, 4, T)
        nch = T // 128
        AT = self.av(R1, BF, nch, 4, 2, 128)
        slab = self.av(R3 + 32768, BF, 8, 512)
        self.dma("pool", "ufs", [(slab, w_in[:, :, 3584:4096])])
        it = 0
        for g in range(4):
            for tb in range(ph.nblk):
                cols = slice(tb * 512, (tb + 1) * 512)
                u_ps = self.ps(it % 2)
                for kc in range(8):
                    self.mm(u_ps, slab[:, kc, g * 128:(g + 1) * 128], self.H3[:, kc, cols], start=(kc == 0), stop=(kc == 7))
                self.cp(U3[:, g, cols], u_ps, eng=("act" if it % 2 else "dve"))
                it += 1
        for i in range(nch):
            b0 = 2 + (i % 2) * 2
            for g in range(4):
                pa = self.ps(b0 + g // 2, F32, 2, 256)
                self.mm(pa[:, g % 2, :], U3[:, g, i * 128:(i + 1) * 128], self.CS)
            self.cp(AT[:, i, 0:2, :, :].rearrange("p a b c -> p (a b c)"), self.ps(b0), eng="dve")
            self.cp(AT[:, i, 2:4, :, :].rearrange("p a b c -> p (a b c)"), self.ps(b0 + 1), eng="act")
        it = 0
        tabn = 0
        for (t0, L) in ph.seqs:
            ni = L // 128
            i0 = t0 // 128
            for tpb in range(L // 256):
                tab = self.av(R3 + (tabn % 2) * 16384, BF, ni, 2, 256)
                if ph.sample:
                    src = d["dftS"][tpb]
                else:
                    src = d["dftP"]
                self.dma("sp", f"tab{tabn % 2}", [(tab.rearrange("p a b c -> p (a b c)"), src)])
                tabn += 1
                for g in range(4):
                    f_ps = self.ps(6 + it % 2)[:, 0:256]
                    n = 0
                    for ii in range(ni):
                        for cs in range(2):
                            self.mm(f_ps, AT[:, i0 + ii, g, cs, :], tab[:, ii, cs, :], start=(n == 0), stop=(n == 2 * ni - 1))
                            n += 1
                    self.cp(F3[:, g, t0 + tpb * 256:t0 + (tpb + 1) * 256], f_ps, eng=("act" if it % 2 else "dve"))
                    it += 1

    def poolmix(self, ph, l, d):
        T = ph.T
        w_in = d["w_in"][l].rearrange("(kc p) n -> p kc n", p=128)
        P3 = self.av(R2, BF, 4, T)
        nseq = len(ph.seqs)
        L = ph.seqs[0][1]
        Lp = L + 16
        slab = self.av(R3, BF, 8, 512)
        self.dma("pool", "ups", [(slab, w_in[:, :, 3072:3584])])
        wpl = self.av(R3 + 8192, BF, 4, 128)
        self.dma("pool", "wpool", [(wpl, d["w_pool"][l].rearrange("g c e -> c g e"))])
        bufsz = ((nseq * Lp * 4 + 255) // 256) * 256
        U = self.av(R1, F32, nseq, Lp)
        Q = [self.av(R1 + bufsz * (1 + i), F32, nseq, Lp) for i in range(2)]
        Dg = self.av(R1 + 3 * bufsz, BF, nseq, L)
        assert 3 * bufsz + T * 2 <= 32768
        it = 0
        for g in range(4):
            w = POOLW[g]
            lv = g + 1
            self.memset(U[:, :, 0:8], 0.0)
            self.memset(U[:, :, L + 8:L + 16], 0.0)
            for tb in range(ph.nblk):
                cols = slice(tb * 512, (tb + 1) * 512)
                u_ps = self.ps(it % 2)
                for kc in range(8):
                    self.mm(u_ps, slab[:, kc, g * 128:(g + 1) * 128], self.H3[:, kc, cols], start=(kc == 0), stop=(kc == 7))
                if nseq == 1:
                    self.cp(U[:, 0, 8 + tb * 512:8 + (tb + 1) * 512], u_ps, eng=("act" if it % 2 else "dve"))
                else:
                    self.cp(U[:, :, 8:8 + L], u_ps.rearrange("p (s t) -> p s t", s=nseq), eng="dve")
                it += 1
            src = U
            for k in range(1, lv + 1):
                sh = 1 << (k - 1)
                dst = Q[(k - 1) % 2]
                self.tt(dst[:, :, 0:Lp - sh], src[:, :, 0:Lp - sh], src[:, :, sh:Lp], ALU.add, eng="dve")
                src = dst
            hw = w // 2
            Ssh = src[:, :, 8 - hw:8 - hw + L]
            Uc = U[:, :, 8:8 + L]
            tmpD = Q[lv % 2][:, :, 0:L]
            self.stt(tmpD, Ssh, 1.0 / w, Uc, ALU.mult, ALU.subtract)
            tb0 = {2: 0, 4: 2, 8: 6, 16: 14}[w]
            fl = self.ptab[:, tb0:tb0 + hw]
            fr = self.ptab[:, tb0 + hw:tb0 + 2 * hw]
            for s in range(nseq):
                bl = self.av(P_MISC + 128, F32, 8)[:, 0:hw]
                self.tt(bl, Ssh[:, s, 0:hw], fl, ALU.mult)
                self.tt(tmpD[:, s, 0:hw], bl, Uc[:, s, 0:hw], ALU.subtract)
                br = self.av(P_MISC + 192, F32, 8)[:, 0:hw]
                self.tt(br, Ssh[:, s, L - hw:L], fr, ALU.mult)
                self.tt(tmpD[:, s, L - hw:L], br, Uc[:, s, L - hw:L], ALU.subtract)
            self.cp(Dg, tmpD, eng="act")
            Dflat = Dg.rearrange("p s t -> p (s t)")
            for tb in range(ph.nblk):
                cols = slice(tb * 512, (tb + 1) * 512)
                p_ps = self.ps(2 + tb % 2)
                self.mm(p_ps, wpl[:, g, :], Dflat[:, cols])
                self.act(P3[:, g, cols], p_ps, AF.Identity, scale=self.psT[l][:, g:g + 1])

    def layer(self, ph, l, d, pidx, first=True, last=True):
        cfg = self.cfg
        RSO = R3 + 22528 + 4096
        full = all(cfg.get(k, True) for k in ("ffn", "mixer", "attn", "pf", "ffn2"))
        if first or not full:
            self.derive(ph, l)
        dv = self.dvs[l % 2]
        if not full:
            if cfg.get("ffn", True):
                self.rms_mod(ph, dv[:, 0, :], self.shift(l, ph, 0), RSO)
                self.ffn(ph, l, 0, d, dv[:, 3, :])
            if cfg.get("mixer", True):
                self.rms_mod(ph, dv[:, 1, :], self.shift(l, ph, 1), RSO)
                if cfg.get("attn", True):
                    self.attention(ph, l, d, pidx)
                    self.mstage(ph, l, d, 1)
                if cfg.get("pf", True):
                    self.fourier(ph, l, d)
                    self.poolmix(ph, l, d)
                    self.mstage(ph, l, d, 2)
            if cfg.get("ffn2", True):
                self.rms_mod(ph, dv[:, 2, :], self.shift(l, ph, 2), RSO)
                self.ffn(ph, l, 1, d, dv[:, 4, :])
            return
        if not first:
            self.derive(ph, l)
        self.rms_mod(ph, dv[:, 0, :], self.shift(l, ph, 0), RSO)
        self.ffn(ph, l, 0, d, dv[:, 3, :])
        self.rms_mod(ph, dv[:, 1, :], self.shift(l, ph, 1), RSO)
        extra = None
        if self.defer_mod1 and pidx == 0 and l == 0:
            extra = self.stage_mod_gen(d, 1, R1 + 16384, R1 + 24576, 3)
        self.attention(ph, l, d, pidx, extra=extra)
        self.mstage(ph, l, d, 1)
        self.fourier(ph, l, d)
        self.poolmix(ph, l, d)
        self.mstage(ph, l, d, 2)
        self.rms_mod(ph, dv[:, 2, :], self.shift(l, ph, 2), RSO)
        self.ffn(ph, l, 1, d, dv[:, 4, :])


def build(cfg=None):
    cfg = cfg or {}
    nc = bass.Bass("TRN2", target_bir_lowering=False)
    d = {}

    def inp(name, shape, dt=F32):
        d[name] = nc.dram_tensor(name, list(shape), dt, kind="ExternalInput").ap()

    def outp(name, shape):
        d[name] = nc.dram_tensor(name, list(shape), F32, kind="ExternalOutput").ap()

    inp("xs", [2048, D]); inp("xp", [512, D])
    inp("ck", [2, 512, 8, 128]); inp("cv", [2, 512, 8, 128])
    inp("cond", [16, 128])
    inp("w_ada", [2, D, 9 * D]); inp("b_ada", [2, 72, 128]); inp("norm_g", [2, 24, 128])
    inp("ffn_w13", [2, 2, D, 2 * DFF]); inp("ffn_w2", [2, 2, DFF, D])
    inp("w_in", [2, D, 4096]); inp("q_norm_g", [2, 64]); inp("k_norm_g", [2, 64])
    inp("lam_qk", [2, 256]); inp("subln_g", [2, 128]); inp("w_pool", [2, 4, 128, 128])
    inp("pool_scale", [2, 4, 128]); inp("w_gate", [2, D, 3 * D]); inp("w_pa", [2, D, D])
    inp("w_pp", [2, 512, D]); inp("w_pf", [2, 512, D]); inp("w_out", [2, D, D])
    inp("cb", [128, 640], BF); inp("cf", [128, 288]); inp("rope", [128, 2 * 2048])
    inp("dftS", [8, 128, 16 * 2 * 256], BF); inp("dftP", [128, 2 * 2 * 256], BF)
    outp("ys", [2048, D]); outp("yp", [512, D])
    outp("sk", [2, 2, 256, 8, 128]); outp("sv", [2, 2, 256, 8, 128])

    with ExitStack() as es:
        k = K(nc, es, cfg)
        k.slabctr = 0
        k.setup_consts(d)
        k.epsv = k.av(P_MISC + 256, F32, 1)
        k.memset(k.epsv, EPS)
        k.stage_params(d)
        k.stage_mod(d, 0)
        full = all(cfg.get(kk, True) for kk in ("ffn", "mixer", "attn", "pf", "ffn2"))
        k.defer_mod1 = bool(cfg.get("P", True) and full and cfg.get("layers", 2) > 1)
        if cfg.get("layers", 2) > 1 and not k.defer_mod1:
            k.stage_mod(d, 1)
        phases = []
        if cfg.get("P", True):
            phases.append((Phase("P", 512, [(0, 256), (256, 256)], 0, False), d["xp"], d["yp"]))
        if cfg.get("S", True):
            phases.append((Phase("S", 2048, [(0, 2048)], 1, True), d["xs"], d["ys"]))
        for pidx, (ph, xd, yd) in enumerate(phases):
            k.views(ph)
            k.load_x(ph, xd)
            nl = cfg.get("layers", 2)
            for l in range(nl):
                if pidx == 0 and l == 0 and nl > 1:
                    pass
                k.layer(ph, l, d, pidx, first=(l == 0), last=(l == nl - 1))
            k.store_x(ph, yd)
        S = k.S
        S.add("sp", lambda e: None, [], [], extra_deps=list(k.out_ops))
        keys = S.finalize()
        sems = {kk: es.enter_context(nc.semaphore(f"s{i}")) for i, kk in enumerate(keys)}
        with nc.Block() as block:
            @block.tensor
            def _(e):
                S.emit("pe", e, sems)

            @block.scalar
            def _(e):
                S.emit("act", e, sems)

            @block.vector
            def _(e):
                S.emit("dve", e, sems)

            @block.gpsimd
            def _(e):
                S.emit("pool", e, sems)

            @block.sync
            def _(e):
                S.emit("sp", e, sems)
        k.nops = len(S.ops)
        k.nsem = len(keys)
    return nc, k


def host_consts():
    bf = ml_dtypes.bfloat16
    cb = np.zeros((128, 640), np.float32)
    cb[:, 0:128] = np.eye(128)
    cb[:, 128:256] = 1.0 / 1024.0
    blk = np.zeros((128, 128))
    blk[:64, :64] = 1.0 / 64
    blk[64:, 64:] = 1.0 / 64
    cb[:, 256:384] = blk
    cc = np.arange(128)[:, None] * np.arange(128)[None, :]
    ang = 2 * np.pi * (cc % 128) / 128.0
    cb[:, 384:512] = np.cos(ang) / np.sqrt(128.0)
    cb[:, 512:640] = np.sin(ang) / np.sqrt(128.0)
    cf = np.zeros((128, 288), np.float32)
    cf[:, 0:128] = np.eye(128)
    R = np.zeros((128, 128), np.float32)
    for m in range(128):
        if (m % 32) < 16:
            R[m + 16, m] = -1.0
        else:
            R[m - 16, m] = 1.0
    cf[:, 128:256] = R
    col = 256
    for w in POOLW:
        hw = w // 2
        for t in range(hw):
            cf[:, col + t] = 1.0 / (t + hw)
        for i in range(hw):
            cf[:, col + hw + i] = 1.0 / (2 * hw - i)
        col += 2 * hw
    p = np.arange(128)
    dd = p % 64
    axis = dd // 32
    freq = dd % 16
    inv = (10000.0 ** (-np.arange(16, dtype=np.float32) / np.float32(16))).astype(np.float32)
    t = np.arange(2048)
    row = (t // GRID_W).astype(np.float32)
    colp = (t % GRID_W).astype(np.float32)
    pos = np.where(axis[:, None] == 0, row[None, :], colp[None, :]).astype(np.float32)
    angr = (pos * inv[freq][:, None]).astype(np.float32)
    rope = np.stack([np.cos(angr), np.sin(angr)], axis=1).astype(np.float32).reshape(128, 4096)

    def dft(L):
        tt = np.arange(L)[:, None].astype(np.int64)
        kk = np.arange(L)[None, :].astype(np.int64)
        a = 2 * np.pi * ((tt * kk) % L) / L
        return np.cos(a) / np.sqrt(L), -np.sin(a) / np.sqrt(L)
    c, s = dft(2048)
    tab = np.stack([c, s], axis=0)
    tab = tab.reshape(2, 16, 128, 8, 256)
    dftS = np.ascontiguousarray(tab.transpose(3, 2, 1, 0, 4)).reshape(8, 128, 16 * 2 * 256)
    c, s = dft(256)
    tab = np.stack([c, s], axis=0).reshape(2, 2, 128, 256)
    dftP = np.ascontiguousarray(tab.transpose(2, 1, 0, 3)).reshape(128, 2 * 2 * 256)
    return dict(cb=cb.astype(bf), cf=cf, rope=rope, dftS=dftS.astype(bf), dftP=dftP.astype(bf))


_CACHE = {}


def make_in_maps(inputs, ncores=8):
    f = lambda a: np.ascontiguousarray(np.asarray(a, dtype=np.float32))
    hc = host_consts()
    shared = dict(
        w_ada=f(inputs["w_ada"]), b_ada=f(inputs["b_ada"]).reshape(2, 72, 128), norm_g=f(inputs["norm_g"]).reshape(2, 24, 128),
        ffn_w13=f(inputs["ffn_w13"]), ffn_w2=f(inputs["ffn_w2"]), w_in=f(inputs["w_in"]),
        q_norm_g=f(inputs["q_norm_g"]), k_norm_g=f(inputs["k_norm_g"]), lam_qk=f(inputs["lam_qk"]).reshape(2, 256),
        subln_g=f(inputs["subln_g"]), w_pool=f(inputs["w_pool"]), pool_scale=f(inputs["pool_scale"]).reshape(2, 4, 128),
        w_gate=f(inputs["w_gate"]), w_pa=f(inputs["w_pa"]), w_pp=f(inputs["w_pp"]), w_pf=f(inputs["w_pf"]),
        w_out=f(inputs["w_out"]), **hc)
    xp = f(inputs["x_prompt"]); xs = f(inputs["x_sample"])
    ck = f(inputs["cache_k"]); cv = f(inputs["cache_v"])
    c = f(inputs["c"]); cc = f(inputs["c_ctx"])
    maps = []
    for i in range(ncores):
        m = dict(shared)
        m["xs"] = xs[i]
        m["xp"] = xp[2 * i:2 * i + 2].reshape(512, D)
        m["ck"] = ck[i]
        m["cv"] = cv[i]
        m["cond"] = np.ascontiguousarray(np.concatenate([cc.reshape(8, 128), c[i].reshape(8, 128)], axis=0))
        maps.append(m)
    return maps


def kernel(**inputs):
    if "nc" not in _CACHE:
        _CACHE["nc"] = build()[0]
    nc = _CACHE["nc"]
    maps = make_in_maps(inputs)
    res = run_bass_kernel_spmd(nc, maps, core_ids=list(range(8)))
    r = res.results
    y_prompt = np.concatenate([r[i]["yp"].reshape(2, 256, D) for i in range(8)], axis=0).astype(np.float32)
    y_sample = np.stack([r[i]["ys"] for i in range(8)], axis=0).astype(np.float32)
    state_k = np.concatenate([r[i]["sk"] for i in range(8)], axis=0).astype(np.float32)
    state_v = np.concatenate([r[i]["sv"] for i in range(8)], axis=0).astype(np.float32)
    return (y_prompt, y_sample, state_k, state_v)
```

```python
import math
from contextlib import ExitStack
import numpy as np
import ml_dtypes
import concourse.bass as bass
import concourse.mybir as mybir
from concourse.bass_utils import run_bass_kernel_spmd

F32 = mybir.dt.float32
BF = mybir.dt.bfloat16
AF = mybir.ActivationFunctionType
ALU = mybir.AluOpType
AX = mybir.AxisListType

D = 1024
NH = 8
DFF = 2816
EPS = 1e-6
GRID_W = 64
POOLW = (2, 4, 8, 16)


def dsz(dt):
    return 4 if dt == F32 else 2


class Op:
    __slots__ = ("eng", "fn", "deps", "dma_sem", "ndma", "count", "sig", "sigsem", "sigcnt", "waits")

    def __init__(self, eng, fn, dma_sem, ndma):
        self.eng = eng
        self.fn = fn
        self.deps = set()
        self.dma_sem = dma_sem
        self.ndma = ndma
        self.count = 0
        self.sig = False
        self.sigsem = None
        self.sigcnt = 0
        self.waits = None


class Sched:
    EPOCH = 8000

    def __init__(self, tracked):
        self.ops = []
        self.cells = {}
        self.tracked = tracked
        self.dma_counts = {}
        self.last_dma_on_sem = {}

    def cells_of(self, ap):
        nm = ap.tensor.name
        info = self.tracked.get(nm)
        if info is None:
            return ()
        gran, rowbytes = info
        d = dsz(ap.dtype)
        pat = ap.ap
        row_elems = rowbytes // d
        off = int(ap.offset) % row_elems
        dims = [(int(s), int(n)) for (s, n) in pat[1:]]
        if not dims:
            dims = [(1, 1)]
        inner_s, inner_n = dims[-1]
        outer = dims[:-1]
        nouter = 1
        for s, n in outer:
            nouter *= n
        if inner_s in (0, 1):
            ilen = inner_n if inner_s == 1 else 1
        else:
            ilen = inner_s * (inner_n - 1) + 1
        runs = []
        if nouter <= 256:
            offs = [off]
            for s, n in outer:
                offs = [o + s * i for o in offs for i in range(n)]
            for o in offs:
                runs.append((o, o + ilen))
        else:
            hi = off + ilen
            for s, n in outer:
                hi += s * (n - 1)
            runs.append((off, hi))
        out = set()
        for lo, hi in runs:
            b0 = (lo * d) // gran
            b1 = (hi * d - 1) // gran
            for b in range(b0, b1 + 1):
                out.add((nm, b))
        return out

    def add(self, eng, fn, reads=(), writes=(), dma_sem=None, ndma=0, extra_deps=()):
        idx = len(self.ops)
        op = Op(eng, fn, dma_sem, ndma)
        deps = set(extra_deps)
        rc = set()
        for a in reads:
            rc |= set(self.cells_of(a))
        wc = set()
        for a in writes:
            wc |= set(self.cells_of(a))
        cells = self.cells
        rkey = eng if dma_sem is None else ("dma", dma_sem)
        for c in rc:
            st = cells.get(c)
            if st is not None:
                if st[0] is not None:
                    deps.add(st[0])
                if c[0].startswith("ps"):
                    for rk, ri in st[1].items():
                        if rk != rkey:
                            deps.add(ri)
        for c in wc:
            st = cells.get(c)
            if st is not None:
                if st[0] is not None:
                    deps.add(st[0])
                deps.update(st[1].values())
        rkey = eng if dma_sem is None else ("dma", dma_sem)
        for c in rc:
            if c in wc:
                continue
            st = cells.get(c)
            if st is None:
                cells[c] = [None, {rkey: idx}]
            else:
                st[1][rkey] = idx
        for c in wc:
            cells[c] = [idx, {}]
        if dma_sem is not None:
            prev = self.last_dma_on_sem.get(dma_sem)
            if prev is not None:
                deps.add(prev)
            self.last_dma_on_sem[dma_sem] = idx
            cnt = self.dma_counts.get(dma_sem, 0) + 16 * ndma
            self.dma_counts[dma_sem] = cnt
            op.count = cnt
        deps.discard(idx)
        op.deps = deps
        self.ops.append(op)
        return idx

    def finalize(self):
        ops = self.ops
        for op in ops:
            nd = set()
            for d in op.deps:
                dop = ops[d]
                if dop.dma_sem is None and dop.eng == "pe" and op.eng == "pe" and op.dma_sem is None:
                    continue
                nd.add(d)
                if dop.dma_sem is None:
                    dop.sig = True
            op.deps = nd
        cnt = {}
        for op in ops:
            if op.dma_sem is None and op.sig:
                c = cnt.get(op.eng, 0)
                ep, within = divmod(c, self.EPOCH)
                op.sigsem = ("eng", op.eng, ep)
                op.sigcnt = within + 1
                cnt[op.eng] = c + 1
        semkeys = set()
        for op in ops:
            w = {}
            for d in op.deps:
                dop = ops[d]
                if dop.dma_sem is not None:
                    k = ("dma", dop.dma_sem)
                    v = dop.count
                else:
                    k = dop.sigsem
                    v = dop.sigcnt
                if w.get(k, 0) < v:
                    w[k] = v
                semkeys.add(k)
            op.waits = w
            if op.dma_sem is not None:
                semkeys.add(("dma", op.dma_sem))
            elif op.sig:
                semkeys.add(op.sigsem)
        return sorted(semkeys, key=str)

    def emit(self, eng, e, sems):
        waited = {}
        for op in self.ops:
            if op.eng != eng:
                continue
            for k, v in op.waits.items():
                if waited.get(k, 0) < v:
                    e.wait_ge(sems[k], v)
                    waited[k] = v
            if op.dma_sem is not None:
                op.fn(e, sems[("dma", op.dma_sem)])
            else:
                ins = op.fn(e)
                if op.sig:
                    ins.then_inc(sems[op.sigsem], 1)


CB_OFF = 0
CF_OFF = 1280
PAR_OFF = 2560
X_OFF = 7680
H_OFF = X_OFF + 65536
R1 = H_OFF + 32768
R2 = R1 + 32768
R3 = R2 + 32768
ARENA = 212736
R3SZ = ARENA - R3
assert R3SZ >= 40960

P_MODT = PAR_OFF
P_BT = P_MODT + 1152
P_GT = P_BT + 576
P_CT = P_GT + 192
P_SCT = P_CT + 64
P_GQK = P_SCT + 64
P_PST = P_GQK + 64
P_NLAM = P_PST + 64
P_GSUB = P_NLAM + 64
P_DV = P_GSUB + 1024
P_MISC = P_DV + 256
assert P_MISC + 512 <= X_OFF


class Phase:
    def __init__(self, name, T, seqs, ci, sample):
        self.name = name
        self.T = T
        self.nblk = T // 512
        self.seqs = seqs
        self.ci = ci
        self.sample = sample
        self.nk = T + 512 if sample else T


class K:
    def __init__(self, nc, es, cfg):
        self.nc = nc
        self.cfg = cfg
        self.arena = es.enter_context(nc.sbuf_tensor("arena", [128, ARENA // 2], BF))
        self.banks = [es.enter_context(nc.psum_tensor(f"ps{i}", [128, 512], F32)) for i in range(8)]
        tracked = {"arena": (256, ARENA)}
        for i in range(8):
            tracked[f"ps{i}"] = (2048, 2048)
        self.S = Sched(tracked)
        self.out_ops = []

    def av(self, off, dt, *dims, p0=0, p1=128):
        n = 1
        for x in dims:
            n *= int(x)
        nb = n * dsz(dt)
        assert off % 4 == 0 and off + nb <= ARENA, (off, nb)
        ap = self.arena[p0:p1, off // 2:(off + nb) // 2]
        if dt != BF:
            ap = ap.bitcast(dt)
        if len(dims) > 1:
            names = " ".join(f"d{i}" for i in range(len(dims)))
            ap = ap.rearrange(f"p ({names}) -> p {names}", **{f"d{i}": int(dims[i]) for i in range(len(dims))})
        return ap

    def ps(self, bank, dt=F32, *dims):
        ap = self.banks[bank][:]
        if dt != F32:
            ap = ap.bitcast(dt)
        tot = 512 if dt == F32 else 1024
        n = 1
        for x in dims:
            n *= int(x)
        if not dims:
            return ap
        ap = ap[:, 0:n]
        if len(dims) > 1:
            names = " ".join(f"d{i}" for i in range(len(dims)))
            ap = ap.rearrange(f"p ({names}) -> p {names}", **{f"d{i}": int(dims[i]) for i in range(len(dims))})
        return ap

    def mm(self, out, lhsT, rhs, start=True, stop=True, skip=False):
        if skip:
            fn = lambda e: e.matmul(out, lhsT, rhs, start=start, stop=stop, skip_group_check=True)
        else:
            fn = lambda e: e.matmul(out, lhsT, rhs, start=start, stop=stop)
        self.S.add("pe", fn, [lhsT, rhs], [out])

    def tr(self, out, in_, ident):
        self.S.add("pe", lambda e: e.transpose(out, in_, ident), [in_, ident], [out])

    def act(self, out, in_, func, bias=None, scale=None, accum=None):
        kw = {}
        reads = [in_]
        if bias is not None:
            kw["bias"] = bias
            if not isinstance(bias, (int, float)):
                reads.append(bias)
        if scale is not None:
            kw["scale"] = scale
            if not isinstance(scale, (int, float)):
                reads.append(scale)
        writes = [out]
        if accum is not None:
            kw["accum_out"] = accum
            writes.append(accum)
        self.S.add("act", lambda e: e.activation(out, in_, func, **kw), reads, writes)

    def tt(self, out, a, b, op, eng="dve"):
        self.S.add(eng, lambda e: e.tensor_tensor(out, a, b, op), [a, b], [out])

    def ts(self, out, a, s1, s2, op0, op1=None, eng="dve"):
        reads = [a]
        if not isinstance(s1, (int, float)):
            reads.append(s1)
        if s2 is not None and not isinstance(s2, (int, float)):
            reads.append(s2)
        if op1 is None:
            fn = lambda e: e.tensor_scalar(out, a, s1, None, op0)
        else:
            fn = lambda e: e.tensor_scalar(out, a, s1, s2, op0, op1)
        self.S.add(eng, fn, reads, [out])

    def stt(self, out, a, s, b, op0, op1, eng="dve"):
        reads = [a, b]
        if not isinstance(s, (int, float)):
            reads.append(s)
        self.S.add(eng, lambda e: e.scalar_tensor_tensor(out, a, s, b, op0, op1), reads, [out])

    def cp(self, out, in_, eng="dve"):
        if eng == "act":
            self.S.add("act", lambda e: e.activation(out, in_, AF.Copy), [in_], [out])
        else:
            self.S.add(eng, lambda e: e.tensor_copy(out, in_), [in_], [out])

    def rcp(self, out, in_):
        self.S.add("dve", lambda e: e.reciprocal(out, in_), [in_], [out])

    def rsum(self, out, in_):
        self.S.add("dve", lambda e: e.reduce_sum(out, in_, AX.X), [in_], [out])

    def memset(self, out, val, eng="pool"):
        self.S.add(eng, lambda e: e.memset(out, val), [], [out])

    def dma(self, eng, sem, pairs, is_out=False):
        pairs = list(pairs)

        def fn(e, s):
            for o, i in pairs:
                e.dma_start(out=o, in_=i).then_inc(s, 16)
        idx = self.S.add(eng, fn, [i for o, i in pairs], [o for o, i in pairs], dma_sem=sem, ndma=len(pairs))
        if is_out:
            self.out_ops.append(idx)
        return idx

    def setup_consts(self, d):
        self.cb = self.av(CB_OFF, BF, 640)
        self.cf = self.av(CF_OFF, F32, 288)
        self.dma("sp", "c0", [(self.cb, d["cb"]), (self.cf, d["cf"])])
        self.identb = self.cb[:, 0:128]
        self.onesD = self.cb[:, 128:256]
        self.bones = self.cb[:, 256:384]
        self.CS = self.cb[:, 384:640]
        self.identf = self.cf[:, 0:128]
        self.Rm = self.cf[:, 128:256]
        self.ptab = self.cf[:, 256:288]

    def load_T(self, dst, src_rows, r, stage_off, bank):
        st = self.av(stage_off, F32, 128, p1=r)
        self.dma("sp", "ldT", [(st, src_rows)])
        pst = self.ps(bank)[:, 0:r]
        self.tr(pst, st, self.identf[0:r, 0:r])
        self.cp(dst, pst)

    def stage_params(self, d):
        S0 = R3 + 36864
        self.modT = [self.av(P_MODT + l * 576, F32, 72, 2) for l in range(2)]
        self.bT = [self.av(P_BT + l * 288, F32, 72) for l in range(2)]
        self.gT = [self.av(P_GT + l * 96, F32, 24) for l in range(2)]
        self.cT = self.av(P_CT, F32, 16)
        self.scT = self.av(P_SCT, BF, 2, 8)
        self.gqk = [self.av(P_GQK + l * 8, F32, 2) for l in range(2)]
        self.psT = [self.av(P_PST + l * 16, F32, 4) for l in range(2)]
        self.nlam = [self.av(P_NLAM + l * 4, F32, 1) for l in range(2)]
        self.gsub = [self.av(P_GSUB + l * 512, F32, 128) for l in range(2)]
        self.dvs = [self.av(P_DV, F32, 5, 8), self.av(P_MISC + 320, F32, 5, 8)]
        self.load_T(self.cT, d["cond"], 16, S0, 0)
        self.act(self.scT.rearrange("p a b -> p (a b)"), self.cT, AF.Silu)
        for l in range(2):
            self.load_T(self.bT[l], d["b_ada"][l], 72, S0, 0)
            self.load_T(self.gT[l], d["norm_g"][l], 24, S0, 0)
            self.load_T(self.psT[l], d["pool_scale"][l], 4, S0, 0)
            qg = d["q_norm_g"][l:l + 1, :].rearrange("o d -> d o")
            kg = d["k_norm_g"][l:l + 1, :].rearrange("o d -> d o")
            self.dma("sp", "gqk", [(self.gqk[l][0:64, 0:1], qg), (self.gqk[l][64:128, 0:1], qg),
                                   (self.gqk[l][0:64, 1:2], kg), (self.gqk[l][64:128, 1:2], kg)])
            lq = self.av(S0 + 1024, F32, 4, 64)
            self.dma("sp", "lq", [(lq.rearrange("p a b -> p (a b)"), d["lam_qk"][l:l + 1, :].partition_broadcast(128))])
            pr = self.av(S0 + 2048, F32, 2, 64)
            self.tt(pr[:, 0, :], lq[:, 0, :], lq[:, 1, :], ALU.mult)
            self.tt(pr[:, 1, :], lq[:, 2, :], lq[:, 3, :], ALU.mult)
            sm = self.av(P_MISC, F32, 2)
            self.rsum(sm, pr)
            ex = self.av(P_MISC + 64, F32, 2)
            self.act(ex, sm, AF.Exp)
            lam_init = 0.8 - 0.6 * math.exp(-0.3 * l)
            self.tt(self.nlam[l], ex[:, 1:2], ex[:, 0:1], ALU.subtract)
            self.ts(self.nlam[l], self.nlam[l], -lam_init, None, ALU.add)
            self.dma("sp", "gsub", [(self.gsub[l], d["subln_g"][l:l + 1, :].partition_broadcast(128))])
            self.ts(self.gsub[l], self.gsub[l], 1.0 - lam_init, None, ALU.mult)
        self.nsl = 0

    def stage_mod(self, d, l):
        if True:
            mps = self.ps(6, F32, 72, 2)
            wv = d["w_ada"][l].rearrange("(kc p) n -> p kc n", p=128)
            for sb in range(18):
                slab = self.av(R3 + (self.nsl % 2) * 8192, BF, 8, 512)
                self.dma("pool", f"wada{self.nsl % 2}", [(slab, wv[:, :, sb * 512:(sb + 1) * 512])])
                self.nsl += 1
                for jj in range(4):
                    j = sb * 4 + jj
                    for kc in range(8):
                        self.mm(mps[:, j, :], slab[:, kc, jj * 128:(jj + 1) * 128], self.scT[:, :, kc],
                                start=(kc == 0), stop=(kc == 7))
            for ci in range(2):
                self.tt(self.modT[l][:, :, ci], mps[:, :, ci], self.bT[l], ALU.add)

    def stage_mod_gen(self, d, l, off0, off1, bank):
        wv = d["w_ada"][l].rearrange("(kc p) n -> p kc n", p=128)
        offs = (off0, off1)
        slabs = {}

        def issue(sb):
            slab = self.av(offs[sb % 2], BF, 8, 512)
            self.dma("pool", f"wadab{sb % 2}", [(slab, wv[:, :, sb * 512:(sb + 1) * 512])])
            slabs[sb] = slab
        issue(0)
        for sb in range(18):
            if sb + 1 < 18:
                issue(sb + 1)
            yield
            slab = slabs.pop(sb)
            mps = self.ps(bank, F32, 4, 2)
            for jj in range(4):
                for kc in range(8):
                    self.mm(mps[:, jj, :], slab[:, kc, jj * 128:(jj + 1) * 128], self.scT[:, :, kc],
                            start=(kc == 0), stop=(kc == 7))
            for ci in range(2):
                self.tt(self.modT[l][:, sb * 4:sb * 4 + 4, ci], mps[:, :, ci], self.bT[l][:, sb * 4:sb * 4 + 4], ALU.add)
            yield

    def derive(self, ph, l):
        m = self.modT[l]
        ci = ph.ci
        dv = self.dvs[l % 2]
        for k in range(3):
            sc = m[:, (3 * k + 1) * 8:(3 * k + 2) * 8, ci]
            self.stt(dv[:, k, :], sc, 1.0, self.gT[l][:, k * 8:(k + 1) * 8], ALU.add, ALU.mult)
        self.ts(dv[:, 3, :], m[:, 16:24, ci], 0.5, None, ALU.mult)
        self.ts(dv[:, 4, :], m[:, 64:72, ci], 0.5, None, ALU.mult)

    def shift(self, l, ph, k):
        return self.modT[l][:, (3 * k) * 8:(3 * k + 1) * 8, ph.ci]

    def views(self, ph):
        T = ph.T
        self.X3 = self.av(X_OFF, F32, 8, T)
        self.H3 = self.av(H_OFF, BF, 8, T)

    def load_x(self, ph, xd):
        for i in range(ph.T // 128):
            st = self.av(R3 + (i % 2) * 4096, F32, 1024)
            self.dma("sp", f"xin{i % 2}", [(st, xd[i * 128:(i + 1) * 128, :])])
            b0 = (i % 2) * 2
            for c in range(8):
                self.tr(self.ps(b0 + c // 4)[:, (c % 4) * 128:(c % 4 + 1) * 128], st[:, c * 128:(c + 1) * 128], self.identf)
            cols = slice(i * 128, (i + 1) * 128)
            self.cp(self.X3[:, 0:4, cols], self.ps(b0, F32, 4, 128), eng="dve")
            self.cp(self.X3[:, 4:8, cols], self.ps(b0 + 1, F32, 4, 128), eng="act")

    def store_x(self, ph, yd):
        for i in range(ph.T // 128):
            st = self.av(R3 + (i % 2) * 4096, F32, 1024)
            b0 = (i % 2) * 2
            cols = slice(i * 128, (i + 1) * 128)
            for c in range(8):
                self.tr(self.ps(b0 + c // 4)[:, (c % 4) * 128:(c % 4 + 1) * 128], self.X3[:, c, cols], self.identf)
            self.cp(st[:, 0:512], self.ps(b0), eng="dve")
            self.cp(st[:, 512:1024], self.ps(b0 + 1), eng="act")
            self.dma("sp", f"xout{i % 2}", [(yd[i * 128:(i + 1) * 128, :], st)], is_out=True)

    def rms_mod(self, ph, A_vec, shift_vec, so):
        for tb in range(ph.nblk):
            self.rms_a(ph, so, tb)
            self.rms_b(ph, A_vec, shift_vec, so, tb)

    def rms_a(self, ph, so, tb):
        cols = slice(tb * 512, (tb + 1) * 512)
        for c in range(8):
            sq = self.av(so + c * 1024, BF, 512)
            self.act(sq, self.X3[:, c, cols], AF.Square)

    def rms_b(self, ph, A_vec, shift_vec, so, tb):
        cols = slice(tb * 512, (tb + 1) * 512)
        ss = self.ps(7 - tb % 2)
        for c in range(8):
            sq = self.av(so + c * 1024, BF, 512)
            self.mm(ss, self.onesD, sq, start=(c == 0), stop=(c == 7))
        sd = self.av(so + 8192, F32, 512)
        self.act(sd, ss, AF.Sqrt, bias=self.epsv, scale=1.0)
        self.rcp(sd, sd)
        for c in range(8):
            tmp = self.av(so + 10240 + (c % 2) * 2048, F32, 512)
            self.tt(tmp, self.X3[:, c, cols], sd, ALU.mult)
            self.act(self.H3[:, c, cols], tmp, AF.Identity, bias=shift_vec[:, c:c + 1], scale=A_vec[:, c:c + 1])

    def ffn(self, ph, l, f, d, hg_vec, next_rms=None):
        T = ph.T
        w13 = d["ffn_w13"][l, f].rearrange("(kc p) n -> p kc n", p=128)
        w2d = d["ffn_w2"][l, f]
        ACT3 = self.av(R1, BF, 11, T)
        SL = R1 + 45056
        w2 = self.av(R3, BF, 11, 1024)
        SG = R3 + 22528
        for half in range(2):
            ch0 = half * 11
            w2_issued = False
            it = 0
            per = 2 if ph.sample else 4
            SLp = SL if ph.sample else R1 + 16384
            for s0 in range(0, 11, per):
                nj = min(per, 11 - s0)
                n = nj * 128
                sidx = self.slabctr % (2 if ph.sample else 3)
                self.slabctr += 1
                slab = self.av(SLp + sidx * 4096 * per, BF, 8, 2, 128 * per)
                c0 = (ch0 + s0) * 128
                self.dma("pool", f"w13_{sidx}", [(slab[:, :, 0, 0:n], w13[:, :, c0:c0 + n]),
                                                  (slab[:, :, 1, 0:n], w13[:, :, DFF + c0:DFF + c0 + n])])
                if s0 >= per and not w2_issued:
                    self.dma("pool", "w2", [(w2, w2d[ch0 * 128:(ch0 + 11) * 128, :].rearrange("(j p) n -> p j n", p=128))])
                    w2_issued = True
                for jj in range(nj):
                    jl = s0 + jj
                    for tb in range(ph.nblk):
                        cols = slice(tb * 512, (tb + 1) * 512)
                        g_ps = self.ps(it % 2)
                        u_ps = self.ps(2 + it % 2)
                        for kc in range(8):
                            self.mm(g_ps, slab[:, kc, 0, jj * 128:(jj + 1) * 128], self.H3[:, kc, cols], start=(kc == 0), stop=(kc == 7))
                        for kc in range(8):
                            self.mm(u_ps, slab[:, kc, 1, jj * 128:(jj + 1) * 128], self.H3[:, kc, cols], start=(kc == 0), stop=(kc == 7))
                        sg = self.av(SG + (it % 2) * 2048, F32, 512)
                        self.act(sg, g_ps, AF.Silu)
                        self.tt(ACT3[:, jl, cols], sg, u_ps, ALU.mult)
                        it += 1
            it = 0
            if half == 0 or next_rms is None:
                order = [(c, tb) for c in range(8) for tb in range(ph.nblk)]
            else:
                order = [(c, tb) for tb in range(ph.nblk) for c in range(8)]
            for (c, tb) in order:
                cols = slice(tb * 512, (tb + 1) * 512)
                o_ps = self.ps(4 + it % 2)
                for jl in range(11):
                    self.mm(o_ps, w2[:, jl, c * 128:(c + 1) * 128], ACT3[:, jl, cols], start=(jl == 0), stop=(jl == 10))
                self.stt(self.X3[:, c, cols], o_ps, hg_vec[:, c:c + 1], self.X3[:, c, cols], ALU.mult, ALU.add)
                it += 1
                if half == 1 and next_rms is not None and c == 7:
                    if tb >= 1:
                        next_rms[1](tb - 1)
                    next_rms[0](tb)
            if half == 1 and next_rms is not None:
                next_rms[1](ph.nblk - 1)

    def rsqrt_act(self, out, in_, scale):
        self.act(out, in_, AF.Ln, bias=self.epsv, scale=scale)
        self.act(out, out, AF.Exp, scale=-0.5)

    def qk_prep_gen(self, ph, ps_in, gcol, outs, so, rope, kn_keep=None, sq_done=False):
        n = 512
        sq = self.av(so, BF, n)
        if not sq_done:
            self.act(sq, ps_in, AF.Square)
            yield
        ss = self.ps(3)
        self.mm(ss, self.bones, sq)
        sd = self.av(so + 1024, F32, n)
        self.rsqrt_act(sd, ss, 1.0)
        qn = kn_keep if kn_keep is not None else self.av(so + 3072, F32, n)
        self.stt(qn, ps_in, gcol, sd, ALU.mult, ALU.mult)
        if not ph.sample:
            for (p0, p1, dst) in outs:
                self.cp(dst, qn[p0:p1, :], eng="act")
            return
        yield
        rot = self.ps(3)
        self.mm(rot, self.Rm, qn)
        t1 = self.av(so + 5120, F32, n)
        t2 = self.av(so + 7168, F32, n)
        self.tt(t1, qn, rope[:, 0, :], ALU.mult, eng="pool")
        self.tt(t2, rot, rope[:, 1, :], ALU.mult)
        for (p0, p1, dst) in outs:
            self.tt(dst, t1[p0:p1, :], t2[p0:p1, :], ALU.add)

    def attention(self, ph, l, d, pidx, extra=None):
        T = ph.T
        A3 = self.av(R1, BF, 8, T)
        w_in = d["w_in"][l].rearrange("(kc p) n -> p kc n", p=128)
        HBSZ = 18944
        B0 = R2
        Ebase = B0 + 2 * HBSZ
        ROPE = Ebase + 3072
        SLABO = ROPE + 8192
        SO = SLABO + 6144
        CKST = SO + 7168
        SM = SO + 9216
        assert SM + 4096 <= ARENA and SM % 256 == 0
        ropedr = d["rope"].rearrange("p (a t) -> p a t", a=2)
        if not ph.sample:
            sk_st = self.av(ROPE, F32, 4, 128)
            sv_st = self.av(ROPE + 2048, F32, 4, 128)
        hbufs = []
        for hb in range(2):
            base = B0 + hb * HBSZ
            qz = self.av(base, BF, 2, T)
            kT = self.av(base + 8192, BF, ph.nk)
            vx = self.av(base + 13312, BF, 20, 136)
            self.memset(vx[:, :, 128:130], 1.0)
            self.memset(qz[64:128, 0, :], 0.0)
            self.memset(qz[0:64, 1, :], 0.0)
            hbufs.append((qz, kT, vx))
        koff = 512 if ph.sample else 0
        vch0 = 4 if ph.sample else 0
        ropectr = [0]

        def proj_gen(h):
            hb = h % 2
            qz, kT, vx = hbufs[hb]
            slab = self.av(SLABO, BF, 8, 3, 128)
            self.dma("pool", "win", [(slab[:, :, i, :], w_in[:, :, i * 1024 + h * 128:i * 1024 + (h + 1) * 128]) for i in range(3)])
            if ph.sample:
                ckst = self.av(CKST, BF, 4, 128)
                self.dma("pool", "ck", [(ckst, d["ck"][l, :, h, :].rearrange("(i p) e -> p i e", p=128))])
                self.dma("pool", f"cv{hb}", [(vx[:, 0:4, 0:128], d["cv"][l, :, h, :].rearrange("(i p) e -> p i e", p=128))])
                yield
                pb = self.ps(3, BF, 4, 128)
                for i in range(4):
                    self.tr(pb[:, i, :], ckst[:, i, :], self.identb)
                self.cp(kT[:, 0:512], pb.rearrange("p a b -> p (a b)"), eng="dve")
            for tb in range(ph.nblk):
                cols = slice(tb * 512, (tb + 1) * 512)
                rope = None
                if ph.sample:
                    ri = ropectr[0] % 2
                    ropectr[0] += 1
                    rope = self.av(ROPE + ri * 4096, F32, 2, 512)
                    self.dma("sp", f"rope{ri}", [(rope, ropedr[:, :, cols])])
                yield
                q_ps = self.ps(0)
                k_ps = self.ps(1)
                v_ps = self.ps(2, F32, 4, 128)
                for kc in range(8):
                    self.mm(q_ps, slab[:, kc, 0, :], self.H3[:, kc, cols], start=(kc == 0), stop=(kc == 7))
                    yield
                for kc in range(8):
                    self.mm(k_ps, slab[:, kc, 1, :], self.H3[:, kc, cols], start=(kc == 0), stop=(kc == 7))
                    yield
                for i in range(4):
                    tc_ = slice(tb * 512 + i * 128, tb * 512 + (i + 1) * 128)
                    for kc in range(8):
                        self.mm(v_ps[:, i, :], self.H3[:, kc, tc_], slab[:, kc, 2, :], start=(kc == 0), stop=(kc == 7))
                        if kc % 4 == 3:
                            yield
                self.cp(vx[:, vch0 + tb * 4:vch0 + tb * 4 + 4, 0:128], v_ps, eng="dve")
                if not ph.sample:
                    self.cp(sv_st, v_ps)
                    self.dma("sp", "sv", [(d["sv"][b, l, :, h, :].rearrange("(i p) e -> p i e", p=128), sv_st[:, 2 * b:2 * b + 2, :]) for b in range(2)], is_out=True)
                yield from self.qk_prep_gen(ph, q_ps, self.gqk[l][:, 0:1], [(0, 64, qz[0:64, 0, cols]), (64, 128, qz[64:128, 1, cols])], SO, rope)
                kn = None if ph.sample else self.av(SO + 3072, F32, 512)
                kdst = kT[:, koff + tb * 512:koff + (tb + 1) * 512]
                yield from self.qk_prep_gen(ph, k_ps, self.gqk[l][:, 1:2], [(0, 128, kdst)], SO, rope, kn_keep=kn)
                if not ph.sample:
                    yield
                    pk = self.ps(3, F32, 4, 128)
                    for i in range(4):
                        self.tr(pk[:, i, :], kn[:, i * 128:(i + 1) * 128], self.identf)
                    self.cp(sk_st, pk)
                    self.dma("sp", "sk", [(d["sk"][b, l, :, h, :].rearrange("(i p) e -> p i e", p=128), sk_st[:, 2 * b:2 * b + 2, :]) for b in range(2)], is_out=True)

        eit = [0]

        def attn(h, bg):
            qz, kT, vx = hbufs[h % 2]
            its = []
            for (t0, L) in ph.seqs:
                if ph.sample:
                    kch = [(kc, kc * 128) for kc in range(20)]
                else:
                    kch = [(t0 // 128 + j, t0 + j * 128) for j in range(L // 128)]
                for qb in range(L // 256):
                    for ki, (vc, kc0) in enumerate(kch):
                        its.append((t0 + qb * 256, ki, len(kch), vc, kc0))
            Es = {}
            pend = []

            def emitS(n):
                q0, ki, nk, vc, kc0 = its[n]
                e = eit[0]
                eit[0] += 1
                Sp = self.ps(4 + e % 2, F32, 2, 256)
                self.mm(Sp, kT[:, kc0:kc0 + 128], qz[:, :, q0:q0 + 256])
                E = self.av(Ebase + (e % 3) * 1024, BF, 2, 256)
                self.act(E, Sp, AF.Exp, scale=0.125)
                Es[n] = E
            emitS(0)
            if len(its) > 1:
                emitS(1)
            O1 = self.ps(6, F32, 2, 130)
            O2 = self.ps(7, F32, 2, 130)
            for n in range(len(its)):
                q0, ki, nk, vc, kc0 = its[n]
                if n + 2 < len(its):
                    emitS(n + 2)
                E = Es.pop(n)
                for mi, O in enumerate((O1, O2)):
                    for j in range(2):
                        self.mm(O[:, j, :], E[:, mi, j * 128:(j + 1) * 128], vx[:, vc, 0:130],
                                start=(ki == 0 and j == 0), stop=(ki == nk - 1), skip=True)
                if bg is not None:
                    next(bg, None)
                if extra is not None:
                    next(extra, None)
                for g in list(pend):
                    try:
                        next(g)
                    except StopIteration:
                        pend.remove(g)
                if ki == nk - 1:
                    for g in pend:
                        for _ in g:
                            pass
                    pend.clear()
                    g = epi_gen(h, q0, O1, O2)
                    next(g)
                    pend.append(g)
            for g in pend:
                for _ in g:
                    pass
            if bg is not None:
                for _ in bg:
                    pass

        def epi_gen(h, q0, O1, O2):
            Oc1 = self.av(SM + 768, F32, 2, 130)
            Oc2 = self.av(SM + 2048, F32, 2, 130)
            self.cp(Oc1, O1, eng="dve")
            self.cp(Oc2, O2, eng="dve")
            yield
            yield
            rz = self.av(SM, F32, 2, 2)
            self.rcp(rz[:, 0, :], Oc1[:, :, 128])
            self.rcp(rz[:, 1, :], Oc2[:, :, 128])
            yield
            self.ts(rz[:, 1, :], rz[:, 1, :], self.nlam[l][:, 0:1], None, ALU.mult)
            yield
            for j in range(2):
                o = Oc1[:, j, 0:128]
                self.ts(o, o, rz[:, 0, j:j + 1], None, ALU.mult)
            yield
            for j in range(2):
                o = Oc1[:, j, 0:128]
                self.stt(o, Oc2[:, j, 0:128], rz[:, 1, j:j + 1], o, ALU.mult, ALU.add)
            yield
            s2s = []
            for j in range(2):
                o = Oc1[:, j, 0:128]
                junk = self.av(SM + 3328, BF, 128)
                s2 = self.av(SM + 256 + j * 256, F32, 1)
                self.S.add("dve", (lambda junk=junk, o=o, s2=s2: (lambda e: e.scalar_tensor_tensor(junk, o, 1.0, o, ALU.mult, ALU.mult, accum_out=s2)))(), [o], [junk, s2])
                s2s.append(s2)
            yield
            yield
            for j in range(2):
                self.act(s2s[j], s2s[j], AF.Ln, bias=self.epsv, scale=1.0 / 128.0)
            yield
            for j in range(2):
                self.act(s2s[j], s2s[j], AF.Exp, scale=-0.5)
            yield
            yield
            ats = []
            for j in range(2):
                at = self.av(SM + 3584 + j * 256, BF, 128)
                self.stt(at, Oc1[:, j, 0:128], s2s[j], self.gsub[l], ALU.mult, ALU.mult)
                ats.append(at)
            yield
            yield
            pt = self.ps(3, BF, 2, 128)
            for j in range(2):
                self.tr(pt[:, j, :], ats[j], self.identb)
            self.cp(A3[:, h, q0:q0 + 256], pt.rearrange("p a b -> p (a b)"))

        for _ in proj_gen(0):
            pass
        for h in range(NH):
            bg = proj_gen(h + 1) if h + 1 < NH else None
            attn(h, bg)
        if extra is not None:
            for _ in extra:
                pass

    def mstage(self, ph, l, d, which, next_rms=None):
        T = ph.T
        w_gate = d["w_gate"][l].rearrange("(kc p) n -> p kc n", p=128)
        w_out = d["w_out"][l].rearrange("(kc p) n -> p kc n", p=128)
        g2 = self.modT[l][:, 40:48, ph.ci]
        wo = self.av(R3, BF, 8, 1024)
        SLB = R3 + 16384
        SG = R3 + 28672
        if which == 1:
            A3 = self.av(R1, BF, 8, T)
            M3 = self.av(R2, BF, 8, T)
            w_pa = d["w_pa"][l].rearrange("(kc p) n -> p kc n", p=128)
        else:
            P3 = self.av(R2, BF, 4, T)
            F3 = self.av(R2 + 16384, BF, 4, T)
            M3 = self.av(R1, BF, 8, T)
            w_pp = d["w_pp"][l].rearrange("(kc p) n -> p kc n", p=128)
            w_pf = d["w_pf"][l].rearrange("(kc p) n -> p kc n", p=128)
        it = 0
        for c in range(8):
            sidx = c % 2
            cs = slice(c * 128, (c + 1) * 128)
            if which == 1:
                wg = self.av(SLB + sidx * 6144, BF, 8, 128)
                wp = self.av(SLB + sidx * 6144 + 2048, BF, 8, 128)
                self.dma("pool", f"ms{sidx}", [(wg, w_gate[:, :, c * 128:(c + 1) * 128]), (wp, w_pa[:, :, cs])])
            else:
                wgp = self.av(SLB + sidx * 6144, BF, 8, 128)
                wgf = self.av(SLB + sidx * 6144 + 2048, BF, 8, 128)
                wpp = self.av(SLB + sidx * 6144 + 4096, BF, 4, 128)
                wpf = self.av(SLB + sidx * 6144 + 5120, BF, 4, 128)
                self.dma("pool", f"ms{sidx}", [(wgp, w_gate[:, :, 1024 + c * 128:1024 + (c + 1) * 128]),
                                               (wgf, w_gate[:, :, 2048 + c * 128:2048 + (c + 1) * 128]),
                                               (wpp, w_pp[:, :, cs]), (wpf, w_pf[:, :, cs])])
            if c == 1:
                self.dma("pool", "wout", [(wo, w_out)])
            for tb in range(ph.nblk):
                cols = slice(tb * 512, (tb + 1) * 512)
                if which == 1:
                    g_ps = self.ps(it % 2)
                    a_ps = self.ps(2 + it % 2)
                    for kc in range(8):
                        self.mm(g_ps, wg[:, kc, :], self.H3[:, kc, cols], start=(kc == 0), stop=(kc == 7))
                    for kc in range(8):
                        self.mm(a_ps, wp[:, kc, :], A3[:, kc, cols], start=(kc == 0), stop=(kc == 7))
                    sg = self.av(SG + (it % 2) * 2048, F32, 512)
                    self.act(sg, g_ps, AF.Sigmoid)
                    self.tt(M3[:, c, cols], sg, a_ps, ALU.mult)
                else:
                    gp_ps = self.ps(it % 2)
                    gf_ps = self.ps(2 + it % 2)
                    p_ps = self.ps(4)
                    f_ps = self.ps(5)
                    for kc in range(8):
                        self.mm(gp_ps, wgp[:, kc, :], self.H3[:, kc, cols], start=(kc == 0), stop=(kc == 7))
                    for kc in range(8):
                        self.mm(gf_ps, wgf[:, kc, :], self.H3[:, kc, cols], start=(kc == 0), stop=(kc == 7))
                    for kc in range(4):
                        self.mm(p_ps, wpp[:, kc, :], P3[:, kc, cols], start=(kc == 0), stop=(kc == 3))
                    for kc in range(4):
                        self.mm(f_ps, wpf[:, kc, :], F3[:, kc, cols], start=(kc == 0), stop=(kc == 3))
                    sgp = self.av(SG + (it % 2) * 2048, F32, 512)
                    sgf = self.av(SG + 4096 + (it % 2) * 2048, F32, 512)
                    tmp = self.av(SG + 8192, F32, 512)
                    self.act(sgp, gp_ps, AF.Sigmoid)
                    self.act(sgf, gf_ps, AF.Sigmoid)
                    self.tt(tmp, sgp, p_ps, ALU.mult)
                    self.tt(sgf, sgf, f_ps, ALU.mult)
                    self.tt(M3[:, c, cols], tmp, sgf, ALU.add)
                it += 1
        it = 0
        if next_rms is None:
            order = [(c, tb) for c in range(8) for tb in range(ph.nblk)]
        else:
            order = [(c, tb) for tb in range(ph.nblk) for c in range(8)]
        for (c, tb) in order:
            cols = slice(tb * 512, (tb + 1) * 512)
            o_ps = self.ps(6 + it % 2)
            for kc in range(8):
                self.mm(o_ps, wo[:, kc, c * 128:(c + 1) * 128], M3[:, kc, cols], start=(kc == 0), stop=(kc == 7))
            self.stt(self.X3[:, c, cols], o_ps, g2[:, c:c + 1], self.X3[:, c, cols], ALU.mult, ALU.add)
            it += 1
            if next_rms is not None and c == 7:
                if tb >= 1:
                    next_rms[1](tb - 1)
                next_rms[0](tb)
        if next_rms is not None:
            next_rms[1](ph.nblk - 1)

    def fourier(self, ph, l, d):
        T = ph.T
        w_in = d["w_in"][l].rearrange("(kc p) n -> p kc n", p=128)
        U3 = self.av(R2, BF, 4, T)
        F3 = self.av(R2 + 16384, BF, 4, T)
        nch = T // 128
        AT = self.av(R1, BF, nch, 4, 2, 128)
        slab = self.av(R3 + 32768, BF, 8, 512)
        self.dma("pool", "ufs", [(slab, w_in[:, :, 3584:4096])])
        it = 0
        for g in range(4):
            for tb in range(ph.nblk):
                cols = slice(tb * 512, (tb + 1) * 512)
                u_ps = self.ps(it % 2)
                for kc in range(8):
                    self.mm(u_ps, slab[:, kc, g * 128:(g + 1) * 128], self.H3[:, kc, cols], start=(kc == 0), stop=(kc == 7))
                self.cp(U3[:, g, cols], u_ps, eng=("act" if it % 2 else "dve"))
                it += 1
        for i in range(nch):
            b0 = 2 + (i % 2) * 2
            for g in range(4):
                pa = self.ps(b0 + g // 2, F32, 2, 256)
                self.mm(pa[:, g % 2, :], U3[:, g, i * 128:(i + 1) * 128], self.CS)
            self.cp(AT[:, i, 0:2, :, :].rearrange("p a b c -> p (a b c)"), self.ps(b0), eng="dve")
            self.cp(AT[:, i, 2:4, :, :].rearrange("p a b c -> p (a b c)"), self.ps(b0 + 1), eng="act")
        it = 0
        tabn = 0
        for (t0, L) in ph.seqs:
            ni = L // 128
            i0 = t0 // 128
            for tpb in range(L // 256):
                tab = self.av(R3 + (tabn % 2) * 16384, BF, ni, 2, 256)
                if ph.sample:
                    src = d["dftS"][tpb]
                else:
                    src = d["dftP"]
                self.dma("sp", f"tab{tabn % 2}", [(tab.rearrange("p a b c -> p (a b c)"), src)])
                tabn += 1
                for g in range(4):
                    f_ps = self.ps(6 + it % 2)[:, 0:256]
                    n = 0
                    for ii in range(ni):
                        for cs in range(2):
                            self.mm(f_ps, AT[:, i0 + ii, g, cs, :], tab[:, ii, cs, :], start=(n == 0), stop=(n == 2 * ni - 1))
                            n += 1
                    self.cp(F3[:, g, t0 + tpb * 256:t0 + (tpb + 1) * 256], f_ps, eng=("act" if it % 2 else "dve"))
                    it += 1

    def poolmix(self, ph, l, d):
        T = ph.T
        w_in = d["w_in"][l].rearrange("(kc p) n -> p kc n", p=128)
        P3 = self.av(R2, BF, 4, T)
        nseq = len(ph.seqs)
        L = ph.seqs[0][1]
        Lp = L + 16
        slab = self.av(R3, BF, 8, 512)
        self.dma("pool", "ups", [(slab, w_in[:, :, 3072:3584])])
        wpl = self.av(R3 + 8192, BF, 4, 128)
        self.dma("pool", "wpool", [(wpl, d["w_pool"][l].rearrange("g c e -> c g e"))])
        bufsz = ((nseq * Lp * 4 + 255) // 256) * 256
        U = self.av(R1, F32, nseq, Lp)
        Q = [self.av(R1 + bufsz * (1 + i), F32, nseq, Lp) for i in range(2)]
        Dg = self.av(R1 + 3 * bufsz, BF, nseq, L)
        assert 3 * bufsz + T * 2 <= 32768
        it = 0
        for g in range(4):
            w = POOLW[g]
            lv = g + 1
            self.memset(U[:, :, 0:8], 0.0)
            self.memset(U[:, :, L + 8:L + 16], 0.0)
            for tb in range(ph.nblk):
                cols = slice(tb * 512, (tb + 1) * 512)
                u_ps = self.ps(it % 2)
                for kc in range(8):
                    self.mm(u_ps, slab[:, kc, g * 128:(g + 1) * 128], self.H3[:, kc, cols], start=(kc == 0), stop=(kc == 7))
                if nseq == 1:
                    self.cp(U[:, 0, 8 + tb * 512:8 + (tb + 1) * 512], u_ps, eng=("act" if it % 2 else "dve"))
                else:
                    self.cp(U[:, :, 8:8 + L], u_ps.rearrange("p (s t) -> p s t", s=nseq), eng="dve")
                it += 1
            src = U
            for k in range(1, lv + 1):
                sh = 1 << (k - 1)
                dst = Q[(k - 1) % 2]
                self.tt(dst[:, :, 0:Lp - sh], src[:, :, 0:Lp - sh], src[:, :, sh:Lp], ALU.add, eng="dve")
                src = dst
            hw = w // 2
            Ssh = src[:, :, 8 - hw:8 - hw + L]
            Uc = U[:, :, 8:8 + L]
            tmpD = Q[lv % 2][:, :, 0:L]
            self.stt(tmpD, Ssh, 1.0 / w, Uc, ALU.mult, ALU.subtract)
            tb0 = {2: 0, 4: 2, 8: 6, 16: 14}[w]
            fl = self.ptab[:, tb0:tb0 + hw]
            fr = self.ptab[:, tb0 + hw:tb0 + 2 * hw]
            for s in range(nseq):
                bl = self.av(P_MISC + 128, F32, 8)[:, 0:hw]
                self.tt(bl, Ssh[:, s, 0:hw], fl, ALU.mult)
                self.tt(tmpD[:, s, 0:hw], bl, Uc[:, s, 0:hw], ALU.subtract)
                br = self.av(P_MISC + 192, F32, 8)[:, 0:hw]
                self.tt(br, Ssh[:, s, L - hw:L], fr, ALU.mult)
                self.tt(tmpD[:, s, L - hw:L], br, Uc[:, s, L - hw:L], ALU.subtract)
            self.cp(Dg, tmpD, eng="act")
            Dflat = Dg.rearrange("p s t -> p (s t)")
            for tb in range(ph.nblk):
                cols = slice(tb * 512, (tb + 1) * 512)
                p_ps = self.ps(2 + tb % 2)
                self.mm(p_ps, wpl[:, g, :], Dflat[:, cols])
                self.act(P3[:, g, cols], p_ps, AF.Identity, scale=self.psT[l][:, g:g + 1])

    def layer(self, ph, l, d, pidx, first=True, last=True):
        cfg = self.cfg
        RSO = R3 + 22528 + 4096
        full = all(cfg.get(k, True) for k in ("ffn", "mixer", "attn", "pf", "ffn2"))
        if first or not full:
            self.derive(ph, l)
        dv = self.dvs[l % 2]
        if not full:
            if cfg.get("ffn", True):
                self.rms_mod(ph, dv[:, 0, :], self.shift(l, ph, 0), RSO)
                self.ffn(ph, l, 0, d, dv[:, 3, :])
            if cfg.get("mixer", True):
                self.rms_mod(ph, dv[:, 1, :], self.shift(l, ph, 1), RSO)
                if cfg.get("attn", True):
                    self.attention(ph, l, d, pidx)
                    self.mstage(ph, l, d, 1)
                if cfg.get("pf", True):
                    self.fourier(ph, l, d)
                    self.poolmix(ph, l, d)
                    self.mstage(ph, l, d, 2)
            if cfg.get("ffn2", True):
                self.rms_mod(ph, dv[:, 2, :], self.shift(l, ph, 2), RSO)
                self.ffn(ph, l, 1, d, dv[:, 4, :])
            return
        if not first:
            self.derive(ph, l)
        self.rms_mod(ph, dv[:, 0, :], self.shift(l, ph, 0), RSO)
        self.ffn(ph, l, 0, d, dv[:, 3, :])
        self.rms_mod(ph, dv[:, 1, :], self.shift(l, ph, 1), RSO)
        extra = None
        if self.defer_mod1 and pidx == 0 and l == 0:
            extra = self.stage_mod_gen(d, 1, R1 + 16384, R1 + 24576, 3)
        self.attention(ph, l, d, pidx, extra=extra)
        self.mstage(ph, l, d, 1)
        self.fourier(ph, l, d)
        self.poolmix(ph, l, d)
        self.mstage(ph, l, d, 2)
        self.rms_mod(ph, dv[:, 2, :], self.shift(l, ph, 2), RSO)
        self.ffn(ph, l, 1, d, dv[:, 4, :])


def build(cfg=None):
    cfg = cfg or {}
    nc = bass.Bass("TRN2", target_bir_lowering=False)
    d = {}

    def inp(name, shape, dt=F32):
        d[name] = nc.dram_tensor(name, list(shape), dt, kind="ExternalInput").ap()

    def outp(name, shape):
        d[name] = nc.dram_tensor(name, list(shape), F32, kind="ExternalOutput").ap()

    inp("xs", [2048, D]); inp("xp", [512, D])
    inp("ck", [2, 512, 8, 128]); inp("cv", [2, 512, 8, 128])
    inp("cond", [16, 128])
    inp("w_ada", [2, D, 9 * D]); inp("b_ada", [2, 72, 128]); inp("norm_g", [2, 24, 128])
    inp("ffn_w13", [2, 2, D, 2 * DFF]); inp("ffn_w2", [2, 2, DFF, D])
    inp("w_in", [2, D, 4096]); inp("q_norm_g", [2, 64]); inp("k_norm_g", [2, 64])
    inp("lam_qk", [2, 256]); inp("subln_g", [2, 128]); inp("w_pool", [2, 4, 128, 128])
    inp("pool_scale", [2, 4, 128]); inp("w_gate", [2, D, 3 * D]); inp("w_pa", [2, D, D])
    inp("w_pp", [2, 512, D]); inp("w_pf", [2, 512, D]); inp("w_out", [2, D, D])
    inp("cb", [128, 640], BF); inp("cf", [128, 288]); inp("rope", [128, 2 * 2048])
    inp("dftS", [8, 128, 16 * 2 * 256], BF); inp("dftP", [128, 2 * 2 * 256], BF)
    outp("ys", [2048, D]); outp("yp", [512, D])
    outp("sk", [2, 2, 256, 8, 128]); outp("sv", [2, 2, 256, 8, 128])

    with ExitStack() as es:
        k = K(nc, es, cfg)
        k.slabctr = 0
        k.setup_consts(d)
        k.epsv = k.av(P_MISC + 256, F32, 1)
        k.memset(k.epsv, EPS)
        k.stage_params(d)
        k.stage_mod(d, 0)
        full = all(cfg.get(kk, True) for kk in ("ffn", "mixer", "attn", "pf", "ffn2"))
        k.defer_mod1 = bool(cfg.get("P", True) and full and cfg.get("layers", 2) > 1)
        if cfg.get("layers", 2) > 1 and not k.defer_mod1:
            k.stage_mod(d, 1)
        phases = []
        if cfg.get("P", True):
            phases.append((Phase("P", 512, [(0, 256), (256, 256)], 0, False), d["xp"], d["yp"]))
        if cfg.get("S", True):
            phases.append((Phase("S", 2048, [(0, 2048)], 1, True), d["xs"], d["ys"]))
        for pidx, (ph, xd, yd) in enumerate(phases):
            k.views(ph)
            k.load_x(ph, xd)
            nl = cfg.get("layers", 2)
            for l in range(nl):
                if pidx == 0 and l == 0 and nl > 1:
                    pass
                k.layer(ph, l, d, pidx, first=(l == 0), last=(l == nl - 1))
            k.store_x(ph, yd)
        S = k.S
        S.add("sp", lambda e: None, [], [], extra_deps=list(k.out_ops))
        keys = S.finalize()
        sems = {kk: es.enter_context(nc.semaphore(f"s{i}")) for i, kk in enumerate(keys)}
        with nc.Block() as block:
            @block.tensor
            def _(e):
                S.emit("pe", e, sems)

            @block.scalar
            def _(e):
                S.emit("act", e, sems)

            @block.vector
            def _(e):
                S.emit("dve", e, sems)

            @block.gpsimd
            def _(e):
                S.emit("pool", e, sems)

            @block.sync
            def _(e):
                S.emit("sp", e, sems)
        k.nops = len(S.ops)
        k.nsem = len(keys)
    return nc, k


def host_consts():
    bf = ml_dtypes.bfloat16
    cb = np.zeros((128, 640), np.float32)
    cb[:, 0:128] = np.eye(128)
    cb[:, 128:256] = 1.0 / 1024.0
    blk = np.zeros((128, 128))
    blk[:64, :64] = 1.0 / 64
    blk[64:, 64:] = 1.0 / 64
    cb[:, 256:384] = blk
    cc = np.arange(128)[:, None] * np.arange(128)[None, :]
    ang = 2 * np.pi * (cc % 128) / 128.0
    cb[:, 384:512] = np.cos(ang) / np.sqrt(128.0)
    cb[:, 512:640] = np.sin(ang) / np.sqrt(128.0)
    cf = np.zeros((128, 288), np.float32)
    cf[:, 0:128] = np.eye(128)
    R = np.zeros((128, 128), np.float32)
    for m in range(128):
        if (m % 32) < 16:
            R[m + 16, m] = -1.0
        else:
            R[m - 16, m] = 1.0
    cf[:, 128:256] = R
    col = 256
    for w in POOLW:
        hw = w // 2
        for t in range(hw):
            cf[:, col + t] = 1.0 / (t + hw)
        for i in range(hw):
            cf[:, col + hw + i] = 1.0 / (2 * hw - i)
        col += 2 * hw
    p = np.arange(128)
    dd = p % 64
    axis = dd // 32
    freq = dd % 16
    inv = (10000.0 ** (-np.arange(16, dtype=np.float32) / np.float32(16))).astype(np.float32)
    t = np.arange(2048)
    row = (t // GRID_W).astype(np.float32)
    colp = (t % GRID_W).astype(np.float32)
    pos = np.where(axis[:, None] == 0, row[None, :], colp[None, :]).astype(np.float32)
    angr = (pos * inv[freq][:, None]).astype(np.float32)
    rope = np.stack([np.cos(angr), np.sin(angr)], axis=1).astype(np.float32).reshape(128, 4096)

    def dft(L):
        tt = np.arange(L)[:, None].astype(np.int64)
        kk = np.arange(L)[None, :].astype(np.int64)
        a = 2 * np.pi * ((tt * kk) % L) / L
        return np.cos(a) / np.sqrt(L), -np.sin(a) / np.sqrt(L)
    c, s = dft(2048)
    tab = np.stack([c, s], axis=0)
    tab = tab.reshape(2, 16, 128, 8, 256)
    dftS = np.ascontiguousarray(tab.transpose(3, 2, 1, 0, 4)).reshape(8, 128, 16 * 2 * 256)
    c, s = dft(256)
    tab = np.stack([c, s], axis=0).reshape(2, 2, 128, 256)
    dftP = np.ascontiguousarray(tab.transpose(2, 1, 0, 3)).reshape(128, 2 * 2 * 256)
    return dict(cb=cb.astype(bf), cf=cf, rope=rope, dftS=dftS.astype(bf), dftP=dftP.astype(bf))


_CACHE = {}


def make_in_maps(inputs, ncores=8):
    f = lambda a: np.ascontiguousarray(np.asarray(a, dtype=np.float32))
    hc = host_consts()
    shared = dict(
        w_ada=f(inputs["w_ada"]), b_ada=f(inputs["b_ada"]).reshape(2, 72, 128), norm_g=f(inputs["norm_g"]).reshape(2, 24, 128),
        ffn_w13=f(inputs["ffn_w13"]), ffn_w2=f(inputs["ffn_w2"]), w_in=f(inputs["w_in"]),
        q_norm_g=f(inputs["q_norm_g"]), k_norm_g=f(inputs["k_norm_g"]), lam_qk=f(inputs["lam_qk"]).reshape(2, 256),
        subln_g=f(inputs["subln_g"]), w_pool=f(inputs["w_pool"]), pool_scale=f(inputs["pool_scale"]).reshape(2, 4, 128),
        w_gate=f(inputs["w_gate"]), w_pa=f(inputs["w_pa"]), w_pp=f(inputs["w_pp"]), w_pf=f(inputs["w_pf"]),
        w_out=f(inputs["w_out"]), **hc)
    xp = f(inputs["x_prompt"]); xs = f(inputs["x_sample"])
    ck = f(inputs["cache_k"]); cv = f(inputs["cache_v"])
    c = f(inputs["c"]); cc = f(inputs["c_ctx"])
    maps = []
    for i in range(ncores):
        m = dict(shared)
        m["xs"] = xs[i]
        m["xp"] = xp[2 * i:2 * i + 2].reshape(512, D)
        m["ck"] = ck[i]
        m["cv"] = cv[i]
        m["cond"] = np.ascontiguousarray(np.concatenate([cc.reshape(8, 128), c[i].reshape(8, 128)], axis=0))
        maps.append(m)
    return maps


def kernel(**inputs):
    if "nc" not in _CACHE:
        _CACHE["nc"] = build()[0]
    nc = _CACHE["nc"]
    maps = make_in_maps(inputs)
    res = run_bass_kernel_spmd(nc, maps, core_ids=list(range(8)))
    r = res.results
    y_prompt = np.concatenate([r[i]["yp"].reshape(2, 256, D) for i in range(8)], axis=0).astype(np.float32)
    y_sample = np.stack([r[i]["ys"] for i in range(8)], axis=0).astype(np.float32)
    state_k = np.concatenate([r[i]["sk"] for i in range(8)], axis=0).astype(np.float32)
    state_v = np.concatenate([r[i]["sv"] for i in range(8)], axis=0).astype(np.float32)
    return (y_prompt, y_sample, state_k, state_v)
```

```python
import math
from contextlib import ExitStack
import numpy as np
import ml_dtypes
import concourse.bass as bass
import concourse.mybir as mybir
from concourse.bass_utils import run_bass_kernel_spmd

F32 = mybir.dt.float32
BF = mybir.dt.bfloat16
AF = mybir.ActivationFunctionType
ALU = mybir.AluOpType
AX = mybir.AxisListType

D = 1024
NH = 8
DFF = 2816
EPS = 1e-6
GRID_W = 64
POOLW = (2, 4, 8, 16)


def dsz(dt):
    return 4 if dt == F32 else 2


class Op:
    __slots__ = ("eng", "fn", "deps", "dma_sem", "ndma", "count", "sig", "sigsem", "sigcnt", "waits")

    def __init__(self, eng, fn, dma_sem, ndma):
        self.eng = eng
        self.fn = fn
        self.deps = set()
        self.dma_sem = dma_sem
        self.ndma = ndma
        self.count = 0
        self.sig = False
        self.sigsem = None
        self.sigcnt = 0
        self.waits = None


class Sched:
    EPOCH = 8000

    def __init__(self, tracked):
        self.ops = []
        self.cells = {}
        self.tracked = tracked
        self.dma_counts = {}
        self.last_dma_on_sem = {}

    def cells_of(self, ap):
        nm = ap.tensor.name
        info = self.tracked.get(nm)
        if info is None:
            return ()
        gran, rowbytes = info
        d = dsz(ap.dtype)
        pat = ap.ap
        row_elems = rowbytes // d
        off = int(ap.offset) % row_elems
        dims = [(int(s), int(n)) for (s, n) in pat[1:]]
        if not dims:
            dims = [(1, 1)]
        inner_s, inner_n = dims[-1]
        outer = dims[:-1]
        nouter = 1
        for s, n in outer:
            nouter *= n
        if inner_s in (0, 1):
            ilen = inner_n if inner_s == 1 else 1
        else:
            ilen = inner_s * (inner_n - 1) + 1
        runs = []
        if nouter <= 256:
            offs = [off]
            for s, n in outer:
                offs = [o + s * i for o in offs for i in range(n)]
            for o in offs:
                runs.append((o, o + ilen))
        else:
            hi = off + ilen
            for s, n in outer:
                hi += s * (n - 1)
            runs.append((off, hi))
        out = set()
        for lo, hi in runs:
            b0 = (lo * d) // gran
            b1 = (hi * d - 1) // gran
            for b in range(b0, b1 + 1):
                out.add((nm, b))
        return out

    def add(self, eng, fn, reads=(), writes=(), dma_sem=None, ndma=0, extra_deps=()):
        idx = len(self.ops)
        op = Op(eng, fn, dma_sem, ndma)
        deps = set(extra_deps)
        rc = set()
        for a in reads:
            rc |= set(self.cells_of(a))
        wc = set()
        for a in writes:
            wc |= set(self.cells_of(a))
        cells = self.cells
        rkey = eng if dma_sem is None else ("dma", dma_sem)
        for c in rc:
            st = cells.get(c)
            if st is not None:
                if st[0] is not None:
                    deps.add(st[0])
                if c[0].startswith("ps"):
                    for rk, ri in st[1].items():
                        if rk != rkey:
                            deps.add(ri)
        for c in wc:
            st = cells.get(c)
            if st is not None:
                if st[0] is not None:
                    deps.add(st[0])
                deps.update(st[1].values())
        rkey = eng if dma_sem is None else ("dma", dma_sem)
        for c in rc:
            if c in wc:
                continue
            st = cells.get(c)
            if st is None:
                cells[c] = [None, {rkey: idx}]
            else:
                st[1][rkey] = idx
        for c in wc:
            cells[c] = [idx, {}]
        if dma_sem is not None:
            prev = self.last_dma_on_sem.get(dma_sem)
            if prev is not None:
                deps.add(prev)
            self.last_dma_on_sem[dma_sem] = idx
            cnt = self.dma_counts.get(dma_sem, 0) + 16 * ndma
            self.dma_counts[dma_sem] = cnt
            op.count = cnt
        deps.discard(idx)
        op.deps = deps
        self.ops.append(op)
        return idx

    def finalize(self):
        ops = self.ops
        for op in ops:
            nd = set()
            for d in op.deps:
                dop = ops[d]
                if dop.dma_sem is None and dop.eng == "pe" and op.eng == "pe" and op.dma_sem is None:
                    continue
                nd.add(d)
                if dop.dma_sem is None:
                    dop.sig = True
            op.deps = nd
        cnt = {}
        for op in ops:
            if op.dma_sem is None and op.sig:
                c = cnt.get(op.eng, 0)
                ep, within = divmod(c, self.EPOCH)
                op.sigsem = ("eng", op.eng, ep)
                op.sigcnt = within + 1
                cnt[op.eng] = c + 1
        semkeys = set()
        for op in ops:
            w = {}
            for d in op.deps:
                dop = ops[d]
                if dop.dma_sem is not None:
                    k = ("dma", dop.dma_sem)
                    v = dop.count
                else:
                    k = dop.sigsem
                    v = dop.sigcnt
                if w.get(k, 0) < v:
                    w[k] = v
                semkeys.add(k)
            op.waits = w
            if op.dma_sem is not None:
                semkeys.add(("dma", op.dma_sem))
            elif op.sig:
                semkeys.add(op.sigsem)
        return sorted(semkeys, key=str)

    def emit(self, eng, e, sems):
        waited = {}
        for op in self.ops:
            if op.eng != eng:
                continue
            for k, v in op.waits.items():
                if waited.get(k, 0) < v:
                    e.wait_ge(sems[k], v)
                    waited[k] = v
            if op.dma_sem is not None:
                op.fn(e, sems[("dma", op.dma_sem)])
            else:
                ins = op.fn(e)
                if op.sig:
                    ins.then_inc(sems[op.sigsem], 1)


CB_OFF = 0
CF_OFF = 1280
PAR_OFF = 2560
X_OFF = 7680
H_OFF = X_OFF + 65536
R1 = H_OFF + 32768
R2 = R1 + 32768
R3 = R2 + 32768
ARENA = 212736
R3SZ = ARENA - R3
assert R3SZ >= 40960

P_MODT = PAR_OFF
P_BT = P_MODT + 1152
P_GT = P_BT + 576
P_CT = P_GT + 192
P_SCT = P_CT + 64
P_GQK = P_SCT + 64
P_PST = P_GQK + 64
P_NLAM = P_PST + 64
P_GSUB = P_NLAM + 64
P_DV = P_GSUB + 1024
P_MISC = P_DV + 256
assert P_MISC + 512 <= X_OFF


class Phase:
    def __init__(self, name, T, seqs, ci, sample):
        self.name = name
        self.T = T
        self.nblk = T // 512
        self.seqs = seqs
        self.ci = ci
        self.sample = sample
        self.nk = T + 512 if sample else T


class K:
    def __init__(self, nc, es, cfg):
        self.nc = nc
        self.cfg = cfg
        self.arena = es.enter_context(nc.sbuf_tensor("arena", [128, ARENA // 2], BF))
        self.banks = [es.enter_context(nc.psum_tensor(f"ps{i}", [128, 512], F32)) for i in range(8)]
        tracked = {"arena": (256, ARENA)}
        for i in range(8):
            tracked[f"ps{i}"] = (2048, 2048)
        self.S = Sched(tracked)
        self.out_ops = []

    def av(self, off, dt, *dims, p0=0, p1=128):
        n = 1
        for x in dims:
            n *= int(x)
        nb = n * dsz(dt)
        assert off % 4 == 0 and off + nb <= ARENA, (off, nb)
        ap = self.arena[p0:p1, off // 2:(off + nb) // 2]
        if dt != BF:
            ap = ap.bitcast(dt)
        if len(dims) > 1:
            names = " ".join(f"d{i}" for i in range(len(dims)))
            ap = ap.rearrange(f"p ({names}) -> p {names}", **{f"d{i}": int(dims[i]) for i in range(len(dims))})
        return ap

    def ps(self, bank, dt=F32, *dims):
        ap = self.banks[bank][:]
        if dt != F32:
            ap = ap.bitcast(dt)
        tot = 512 if dt == F32 else 1024
        n = 1
        for x in dims:
            n *= int(x)
        if not dims:
            return ap
        ap = ap[:, 0:n]
        if len(dims) > 1:
            names = " ".join(f"d{i}" for i in range(len(dims)))
            ap = ap.rearrange(f"p ({names}) -> p {names}", **{f"d{i}": int(dims[i]) for i in range(len(dims))})
        return ap

    def mm(self, out, lhsT, rhs, start=True, stop=True, skip=False):
        if skip:
            fn = lambda e: e.matmul(out, lhsT, rhs, start=start, stop=stop, skip_group_check=True)
        else:
            fn = lambda e: e.matmul(out, lhsT, rhs, start=start, stop=stop)
        self.S.add("pe", fn, [lhsT, rhs], [out])

    def tr(self, out, in_, ident):
        self.S.add("pe", lambda e: e.transpose(out, in_, ident), [in_, ident], [out])

    def act(self, out, in_, func, bias=None, scale=None, accum=None):
        kw = {}
        reads = [in_]
        if bias is not None:
            kw["bias"] = bias
            if not isinstance(bias, (int, float)):
                reads.append(bias)
        if scale is not None:
            kw["scale"] = scale
            if not isinstance(scale, (int, float)):
                reads.append(scale)
        writes = [out]
        if accum is not None:
            kw["accum_out"] = accum
            writes.append(accum)
        self.S.add("act", lambda e: e.activation(out, in_, func, **kw), reads, writes)

    def tt(self, out, a, b, op, eng="dve"):
        self.S.add(eng, lambda e: e.tensor_tensor(out, a, b, op), [a, b], [out])

    def ts(self, out, a, s1, s2, op0, op1=None, eng="dve"):
        reads = [a]
        if not isinstance(s1, (int, float)):
            reads.append(s1)
        if s2 is not None and not isinstance(s2, (int, float)):
            reads.append(s2)
        if op1 is None:
            fn = lambda e: e.tensor_scalar(out, a, s1, None, op0)
        else:
            fn = lambda e: e.tensor_scalar(out, a, s1, s2, op0, op1)
        self.S.add(eng, fn, reads, [out])

    def stt(self, out, a, s, b, op0, op1, eng="dve"):
        reads = [a, b]
        if not isinstance(s, (int, float)):
            reads.append(s)
        self.S.add(eng, lambda e: e.scalar_tensor_tensor(out, a, s, b, op0, op1), reads, [out])

    def cp(self, out, in_, eng="dve"):
        if eng == "act":
            self.S.add("act", lambda e: e.activation(out, in_, AF.Copy), [in_], [out])
        else:
            self.S.add(eng, lambda e: e.tensor_copy(out, in_), [in_], [out])

    def rcp(self, out, in_):
        self.S.add("dve", lambda e: e.reciprocal(out, in_), [in_], [out])

    def rsum(self, out, in_):
        self.S.add("dve", lambda e: e.reduce_sum(out, in_, AX.X), [in_], [out])

    def memset(self, out, val, eng="pool"):
        self.S.add(eng, lambda e: e.memset(out, val), [], [out])

    def dma(self, eng, sem, pairs, is_out=False):
        pairs = list(pairs)

        def fn(e, s):
            for o, i in pairs:
                e.dma_start(out=o, in_=i).then_inc(s, 16)
        idx = self.S.add(eng, fn, [i for o, i in pairs], [o for o, i in pairs], dma_sem=sem, ndma=len(pairs))
        if is_out:
            self.out_ops.append(idx)
        return idx

    def setup_consts(self, d):
        self.cb = self.av(CB_OFF, BF, 640)
        self.cf = self.av(CF_OFF, F32, 288)
        self.dma("sp", "c0", [(self.cb, d["cb"]), (self.cf, d["cf"])])
        self.identb = self.cb[:, 0:128]
        self.onesD = self.cb[:, 128:256]
        self.bones = self.cb[:, 256:384]
        self.CS = self.cb[:, 384:640]
        self.identf = self.cf[:, 0:128]
        self.Rm = self.cf[:, 128:256]
        self.ptab = self.cf[:, 256:288]

    def load_T(self, dst, src_rows, r, stage_off, bank):
        st = self.av(stage_off, F32, 128, p1=r)
        self.dma("sp", "ldT", [(st, src_rows)])
        pst = self.ps(bank)[:, 0:r]
        self.tr(pst, st, self.identf[0:r, 0:r])
        self.cp(dst, pst)

    def stage_params(self, d):
        S0 = R3 + 36864
        self.modT = [self.av(P_MODT + l * 576, F32, 72, 2) for l in range(2)]
        self.bT = [self.av(P_BT + l * 288, F32, 72) for l in range(2)]
        self.gT = [self.av(P_GT + l * 96, F32, 24) for l in range(2)]
        self.cT = self.av(P_CT, F32, 16)
        self.scT = self.av(P_SCT, BF, 2, 8)
        self.gqk = [self.av(P_GQK + l * 8, F32, 2) for l in range(2)]
        self.psT = [self.av(P_PST + l * 16, F32, 4) for l in range(2)]
        self.nlam = [self.av(P_NLAM + l * 4, F32, 1) for l in range(2)]
        self.gsub = [self.av(P_GSUB + l * 512, F32, 128) for l in range(2)]
        self.dvs = [self.av(P_DV, F32, 5, 8), self.av(P_MISC + 320, F32, 5, 8)]
        self.load_T(self.cT, d["cond"], 16, S0, 0)
        self.act(self.scT.rearrange("p a b -> p (a b)"), self.cT, AF.Silu)
        for l in range(2):
            self.load_T(self.bT[l], d["b_ada"][l], 72, S0, 0)
            self.load_T(self.gT[l], d["norm_g"][l], 24, S0, 0)
            self.load_T(self.psT[l], d["pool_scale"][l], 4, S0, 0)
            qg = d["q_norm_g"][l:l + 1, :].rearrange("o d -> d o")
            kg = d["k_norm_g"][l:l + 1, :].rearrange("o d -> d o")
            self.dma("sp", "gqk", [(self.gqk[l][0:64, 0:1], qg), (self.gqk[l][64:128, 0:1], qg),
                                   (self.gqk[l][0:64, 1:2], kg), (self.gqk[l][64:128, 1:2], kg)])
            lq = self.av(S0 + 1024, F32, 4, 64)
            self.dma("sp", "lq", [(lq.rearrange("p a b -> p (a b)"), d["lam_qk"][l:l + 1, :].partition_broadcast(128))])
            pr = self.av(S0 + 2048, F32, 2, 64)
            self.tt(pr[:, 0, :], lq[:, 0, :], lq[:, 1, :], ALU.mult)
            self.tt(pr[:, 1, :], lq[:, 2, :], lq[:, 3, :], ALU.mult)
            sm = self.av(P_MISC, F32, 2)
            self.rsum(sm, pr)
            ex = self.av(P_MISC + 64, F32, 2)
            self.act(ex, sm, AF.Exp)
            lam_init = 0.8 - 0.6 * math.exp(-0.3 * l)
            self.tt(self.nlam[l], ex[:, 1:2], ex[:, 0:1], ALU.subtract)
            self.ts(self.nlam[l], self.nlam[l], -lam_init, None, ALU.add)
            self.dma("sp", "gsub", [(self.gsub[l], d["subln_g"][l:l + 1, :].partition_broadcast(128))])
            self.ts(self.gsub[l], self.gsub[l], 1.0 - lam_init, None, ALU.mult)
        self.nsl = 0

    def stage_mod(self, d, l):
        if True:
            mps = self.ps(6, F32, 72, 2)
            wv = d["w_ada"][l].rearrange("(kc p) n -> p kc n", p=128)
            for sb in range(18):
                slab = self.av(R3 + (self.nsl % 2) * 8192, BF, 8, 512)
                self.dma("pool", f"wada{self.nsl % 2}", [(slab, wv[:, :, sb * 512:(sb + 1) * 512])])
                self.nsl += 1
                for jj in range(4):
                    j = sb * 4 + jj
                    for kc in range(8):
                        self.mm(mps[:, j, :], slab[:, kc, jj * 128:(jj + 1) * 128], self.scT[:, :, kc],
                                start=(kc == 0), stop=(kc == 7))
            for ci in range(2):
                self.tt(self.modT[l][:, :, ci], mps[:, :, ci], self.bT[l], ALU.add)

    def stage_mod_gen(self, d, l, off0, off1, bank):
        wv = d["w_ada"][l].rearrange("(kc p) n -> p kc n", p=128)
        offs = (off0, off1)
        slabs = {}

        def issue(sb):
            slab = self.av(offs[sb % 2], BF, 8, 512)
            self.dma("pool", f"wadab{sb % 2}", [(slab, wv[:, :, sb * 512:(sb + 1) * 512])])
            slabs[sb] = slab
        issue(0)
        for sb in range(18):
            if sb + 1 < 18:
                issue(sb + 1)
            yield
            slab = slabs.pop(sb)
            mps = self.ps(bank, F32, 4, 2)
            for jj in range(4):
                for kc in range(8):
                    self.mm(mps[:, jj, :], slab[:, kc, jj * 128:(jj + 1) * 128], self.scT[:, :, kc],
                            start=(kc == 0), stop=(kc == 7))
            for ci in range(2):
                self.tt(self.modT[l][:, sb * 4:sb * 4 + 4, ci], mps[:, :, ci], self.bT[l][:, sb * 4:sb * 4 + 4], ALU.add)
            yield

    def derive(self, ph, l):
        m = self.modT[l]
        ci = ph.ci
        dv = self.dvs[l % 2]
        for k in range(3):
            sc = m[:, (3 * k + 1) * 8:(3 * k + 2) * 8, ci]
            self.stt(dv[:, k, :], sc, 1.0, self.gT[l][:, k * 8:(k + 1) * 8], ALU.add, ALU.mult)
        self.ts(dv[:, 3, :], m[:, 16:24, ci], 0.5, None, ALU.mult)
        self.ts(dv[:, 4, :], m[:, 64:72, ci], 0.5, None, ALU.mult)

    def shift(self, l, ph, k):
        return self.modT[l][:, (3 * k) * 8:(3 * k + 1) * 8, ph.ci]

    def views(self, ph):
        T = ph.T
        self.X3 = self.av(X_OFF, F32, 8, T)
        self.H3 = self.av(H_OFF, BF, 8, T)

    def load_x(self, ph, xd):
        for i in range(ph.T // 128):
            st = self.av(R3 + (i % 2) * 4096, F32, 1024)
            self.dma("sp", f"xin{i % 2}", [(st, xd[i * 128:(i + 1) * 128, :])])
            b0 = (i % 2) * 2
            for c in range(8):
                self.tr(self.ps(b0 + c // 4)[:, (c % 4) * 128:(c % 4 + 1) * 128], st[:, c * 128:(c + 1) * 128], self.identf)
            cols = slice(i * 128, (i + 1) * 128)
            self.cp(self.X3[:, 0:4, cols], self.ps(b0, F32, 4, 128), eng="dve")
            self.cp(self.X3[:, 4:8, cols], self.ps(b0 + 1, F32, 4, 128), eng="act")

    def store_x(self, ph, yd):
        for i in range(ph.T // 128):
            st = self.av(R3 + (i % 2) * 4096, F32, 1024)
            b0 = (i % 2) * 2
            cols = slice(i * 128, (i + 1) * 128)
            for c in range(8):
                self.tr(self.ps(b0 + c // 4)[:, (c % 4) * 128:(c % 4 + 1) * 128], self.X3[:, c, cols], self.identf)
            self.cp(st[:, 0:512], self.ps(b0), eng="dve")
            self.cp(st[:, 512:1024], self.ps(b0 + 1), eng="act")
            self.dma("sp", f"xout{i % 2}", [(yd[i * 128:(i + 1) * 128, :], st)], is_out=True)

    def rms_mod(self, ph, A_vec, shift_vec, so):
        for tb in range(ph.nblk):
            self.rms_a(ph, so, tb)
            self.rms_b(ph, A_vec, shift_vec, so, tb)

    def rms_a(self, ph, so, tb):
        cols = slice(tb * 512, (tb + 1) * 512)
        for c in range(8):
            sq = self.av(so + c * 1024, BF, 512)
            x = self.X3[:, c, cols]
            if c % 8 in (1, 4, 6):
                self.tt(sq, x, x, ALU.mult)
            else:
                self.act(sq, x, AF.Square)

    def rms_b(self, ph, A_vec, shift_vec, so, tb):
        cols = slice(tb * 512, (tb + 1) * 512)
        ss = self.ps(7 - tb % 2)
        for c in range(8):
            sq = self.av(so + c * 1024, BF, 512)
            self.mm(ss, self.onesD, sq, start=(c == 0), stop=(c == 7))
        sd = self.av(so + 8192, F32, 512)
        self.act(sd, ss, AF.Sqrt, bias=self.epsv, scale=1.0)
        self.rcp(sd, sd)
        for c in range(8):
            tmp = self.av(so + 10240 + (c % 2) * 2048, F32, 512)
            self.tt(tmp, self.X3[:, c, cols], sd, ALU.mult)
            self.act(self.H3[:, c, cols], tmp, AF.Identity, bias=shift_vec[:, c:c + 1], scale=A_vec[:, c:c + 1])

    def ffn(self, ph, l, f, d, hg_vec, next_rms=None):
        T = ph.T
        w13 = d["ffn_w13"][l, f].rearrange("(kc p) n -> p kc n", p=128)
        w2d = d["ffn_w2"][l, f]
        ACT3 = self.av(R1, BF, 11, T)
        SL = R1 + 45056
        w2 = self.av(R3, BF, 11, 1024)
        SG = R3 + 22528
        for half in range(2):
            ch0 = half * 11
            w2_issued = False
            it = 0
            per = 2 if ph.sample else 4
            SLp = SL if ph.sample else R1 + 16384
            for s0 in range(0, 11, per):
                nj = min(per, 11 - s0)
                n = nj * 128
                sidx = self.slabctr % (2 if ph.sample else 3)
                self.slabctr += 1
                slab = self.av(SLp + sidx * 4096 * per, BF, 8, 2, 128 * per)
                c0 = (ch0 + s0) * 128
                self.dma("pool", f"w13_{sidx}", [(slab[:, :, 0, 0:n], w13[:, :, c0:c0 + n]),
                                                  (slab[:, :, 1, 0:n], w13[:, :, DFF + c0:DFF + c0 + n])])
                if s0 >= per and not w2_issued:
                    self.dma("pool", "w2", [(w2, w2d[ch0 * 128:(ch0 + 11) * 128, :].rearrange("(j p) n -> p j n", p=128))])
                    w2_issued = True
                for jj in range(nj):
                    jl = s0 + jj
                    for tb in range(ph.nblk):
                        cols = slice(tb * 512, (tb + 1) * 512)
                        g_ps = self.ps(it % 2)
                        u_ps = self.ps(2 + it % 2)
                        for kc in range(8):
                            self.mm(g_ps, slab[:, kc, 0, jj * 128:(jj + 1) * 128], self.H3[:, kc, cols], start=(kc == 0), stop=(kc == 7))
                        for kc in range(8):
                            self.mm(u_ps, slab[:, kc, 1, jj * 128:(jj + 1) * 128], self.H3[:, kc, cols], start=(kc == 0), stop=(kc == 7))
                        sg = self.av(SG + (it % 2) * 2048, F32, 512)
                        self.act(sg, g_ps, AF.Silu)
                        self.tt(ACT3[:, jl, cols], sg, u_ps, ALU.mult)
                        it += 1
            it = 0
            if half == 0 or next_rms is None:
                order = [(c, tb) for c in range(8) for tb in range(ph.nblk)]
            else:
                order = [(c, tb) for tb in range(ph.nblk) for c in range(8)]
            for (c, tb) in order:
                cols = slice(tb * 512, (tb + 1) * 512)
                o_ps = self.ps(4 + it % 2)
                for jl in range(11):
                    self.mm(o_ps, w2[:, jl, c * 128:(c + 1) * 128], ACT3[:, jl, cols], start=(jl == 0), stop=(jl == 10))
                self.stt(self.X3[:, c, cols], o_ps, hg_vec[:, c:c + 1], self.X3[:, c, cols], ALU.mult, ALU.add)
                it += 1
                if half == 1 and next_rms is not None and c == 7:
                    if tb >= 1:
                        next_rms[1](tb - 1)
                    next_rms[0](tb)
            if half == 1 and next_rms is not None:
                next_rms[1](ph.nblk - 1)

    def rsqrt_act(self, out, in_, scale):
        self.act(out, in_, AF.Ln, bias=self.epsv, scale=scale)
        self.act(out, out, AF.Exp, scale=-0.5)

    def qk_prep_gen(self, ph, ps_in, gcol, outs, so, rope, kn_keep=None, sq_done=False):
        n = 512
        sq = self.av(so, BF, n)
        if not sq_done:
            self.act(sq, ps_in, AF.Square)
            yield
        ss = self.ps(3)
        self.mm(ss, self.bones, sq)
        sd = self.av(so + 1024, F32, n)
        self.rsqrt_act(sd, ss, 1.0)
        qn = kn_keep if kn_keep is not None else self.av(so + 3072, F32, n)
        self.stt(qn, ps_in, gcol, sd, ALU.mult, ALU.mult)
        if not ph.sample:
            for (p0, p1, dst) in outs:
                self.cp(dst, qn[p0:p1, :], eng="act")
            return
        yield
        rot = self.ps(3)
        self.mm(rot, self.Rm, qn)
        t1 = self.av(so + 5120, F32, n)
        t2 = self.av(so + 7168, F32, n)
        self.tt(t1, qn, rope[:, 0, :], ALU.mult, eng="pool")
        self.tt(t2, rot, rope[:, 1, :], ALU.mult)
        for (p0, p1, dst) in outs:
            self.tt(dst, t1[p0:p1, :], t2[p0:p1, :], ALU.add)

    def attention(self, ph, l, d, pidx, extra=None):
        T = ph.T
        A3 = self.av(R1, BF, 8, T)
        w_in = d["w_in"][l].rearrange("(kc p) n -> p kc n", p=128)
        HBSZ = 18944
        B0 = R2
        Ebase = B0 + 2 * HBSZ
        ROPE = Ebase + 3072
        SLABO = ROPE + 8192
        SO = SLABO + 6144
        CKST = SO + 7168
        SM = SO + 9216
        assert SM + 4096 <= ARENA and SM % 256 == 0
        ropedr = d["rope"].rearrange("p (a t) -> p a t", a=2)
        if not ph.sample:
            sk_st = self.av(ROPE, F32, 4, 128)
            sv_st = self.av(ROPE + 2048, F32, 4, 128)
        hbufs = []
        for hb in range(2):
            base = B0 + hb * HBSZ
            qz = self.av(base, BF, 2, T)
            kT = self.av(base + 8192, BF, ph.nk)
            vx = self.av(base + 13312, BF, 20, 136)
            self.memset(vx[:, :, 128:130], 1.0)
            self.memset(qz[64:128, 0, :], 0.0)
            self.memset(qz[0:64, 1, :], 0.0)
            hbufs.append((qz, kT, vx))
        koff = 512 if ph.sample else 0
        vch0 = 4 if ph.sample else 0
        ropectr = [0]

        def proj_gen(h):
            hb = h % 2
            qz, kT, vx = hbufs[hb]
            slab = self.av(SLABO, BF, 8, 3, 128)
            self.dma("pool", "win", [(slab[:, :, i, :], w_in[:, :, i * 1024 + h * 128:i * 1024 + (h + 1) * 128]) for i in range(3)])
            if ph.sample:
                ckst = self.av(CKST, BF, 4, 128)
                self.dma("pool", "ck", [(ckst, d["ck"][l, :, h, :].rearrange("(i p) e -> p i e", p=128))])
                self.dma("pool", f"cv{hb}", [(vx[:, 0:4, 0:128], d["cv"][l, :, h, :].rearrange("(i p) e -> p i e", p=128))])
                yield
                pb = self.ps(3, BF, 4, 128)
                for i in range(4):
                    self.tr(pb[:, i, :], ckst[:, i, :], self.identb)
                self.cp(kT[:, 0:512], pb.rearrange("p a b -> p (a b)"), eng="dve")
            for tb in range(ph.nblk):
                cols = slice(tb * 512, (tb + 1) * 512)
                rope = None
                if ph.sample:
                    ri = ropectr[0] % 2
                    ropectr[0] += 1
                    rope = self.av(ROPE + ri * 4096, F32, 2, 512)
                    self.dma("sp", f"rope{ri}", [(rope, ropedr[:, :, cols])])
                yield
                q_ps = self.ps(0)
                k_ps = self.ps(1)
                v_ps = self.ps(2, F32, 4, 128)
                for kc in range(8):
                    self.mm(q_ps, slab[:, kc, 0, :], self.H3[:, kc, cols], start=(kc == 0), stop=(kc == 7))
                    yield
                for kc in range(8):
                    self.mm(k_ps, slab[:, kc, 1, :], self.H3[:, kc, cols], start=(kc == 0), stop=(kc == 7))
                    yield
                for i in range(4):
                    tc_ = slice(tb * 512 + i * 128, tb * 512 + (i + 1) * 128)
                    for kc in range(8):
                        self.mm(v_ps[:, i, :], self.H3[:, kc, tc_], slab[:, kc, 2, :], start=(kc == 0), stop=(kc == 7))
                        if kc % 4 == 3:
                            yield
                self.cp(vx[:, vch0 + tb * 4:vch0 + tb * 4 + 4, 0:128], v_ps, eng="dve")
                if not ph.sample:
                    self.cp(sv_st, v_ps)
                    self.dma("sp", "sv", [(d["sv"][b, l, :, h, :].rearrange("(i p) e -> p i e", p=128), sv_st[:, 2 * b:2 * b + 2, :]) for b in range(2)], is_out=True)
                yield from self.qk_prep_gen(ph, q_ps, self.gqk[l][:, 0:1], [(0, 64, qz[0:64, 0, cols]), (64, 128, qz[64:128, 1, cols])], SO, rope)
                kn = None if ph.sample else self.av(SO + 3072, F32, 512)
                kdst = kT[:, koff + tb * 512:koff + (tb + 1) * 512]
                yield from self.qk_prep_gen(ph, k_ps, self.gqk[l][:, 1:2], [(0, 128, kdst)], SO, rope, kn_keep=kn)
                if not ph.sample:
                    yield
                    pk = self.ps(3, F32, 4, 128)
                    for i in range(4):
                        self.tr(pk[:, i, :], kn[:, i * 128:(i + 1) * 128], self.identf)
                    self.cp(sk_st, pk)
                    self.dma("sp", "sk", [(d["sk"][b, l, :, h, :].rearrange("(i p) e -> p i e", p=128), sk_st[:, 2 * b:2 * b + 2, :]) for b in range(2)], is_out=True)

        eit = [0]

        def attn(h, bg):
            qz, kT, vx = hbufs[h % 2]
            its = []
            for (t0, L) in ph.seqs:
                if ph.sample:
                    kch = [(kc, kc * 128) for kc in range(20)]
                else:
                    kch = [(t0 // 128 + j, t0 + j * 128) for j in range(L // 128)]
                for qb in range(L // 256):
                    for ki, (vc, kc0) in enumerate(kch):
                        its.append((t0 + qb * 256, ki, len(kch), vc, kc0))
            Es = {}
            pend = []

            def emitS(n):
                q0, ki, nk, vc, kc0 = its[n]
                e = eit[0]
                eit[0] += 1
                Sp = self.ps(4 + e % 2, F32, 2, 256)
                self.mm(Sp, kT[:, kc0:kc0 + 128], qz[:, :, q0:q0 + 256])
                E = self.av(Ebase + (e % 3) * 1024, BF, 2, 256)
                self.act(E, Sp, AF.Exp, scale=0.125)
                Es[n] = E
            emitS(0)
            if len(its) > 1:
                emitS(1)
            O1 = self.ps(6, F32, 2, 130)
            O2 = self.ps(7, F32, 2, 130)
            for n in range(len(its)):
                q0, ki, nk, vc, kc0 = its[n]
                if n + 2 < len(its):
                    emitS(n + 2)
                E = Es.pop(n)
                for mi, O in enumerate((O1, O2)):
                    for j in range(2):
                        self.mm(O[:, j, :], E[:, mi, j * 128:(j + 1) * 128], vx[:, vc, 0:130],
                                start=(ki == 0 and j == 0), stop=(ki == nk - 1), skip=True)
                if bg is not None:
                    next(bg, None)
                if extra is not None:
                    next(extra, None)
                for g in list(pend):
                    try:
                        next(g)
                    except StopIteration:
                        pend.remove(g)
                if ki == nk - 1:
                    for g in pend:
                        for _ in g:
                            pass
                    pend.clear()
                    g = epi_gen(h, q0, O1, O2)
                    next(g)
                    pend.append(g)
            for g in pend:
                for _ in g:
                    pass
            if bg is not None:
                for _ in bg:
                    pass

        def epi_gen(h, q0, O1, O2):
            Oc1 = self.av(SM + 768, F32, 2, 130)
            Oc2 = self.av(SM + 2048, F32, 2, 130)
            self.cp(Oc1, O1, eng="dve")
            self.cp(Oc2, O2, eng="dve")
            yield
            yield
            rz = self.av(SM, F32, 2, 2)
            self.rcp(rz[:, 0, :], Oc1[:, :, 128])
            self.rcp(rz[:, 1, :], Oc2[:, :, 128])
            yield
            self.ts(rz[:, 1, :], rz[:, 1, :], self.nlam[l][:, 0:1], None, ALU.mult)
            yield
            for j in range(2):
                o = Oc1[:, j, 0:128]
                self.ts(o, o, rz[:, 0, j:j + 1], None, ALU.mult)
            yield
            for j in range(2):
                o = Oc1[:, j, 0:128]
                self.stt(o, Oc2[:, j, 0:128], rz[:, 1, j:j + 1], o, ALU.mult, ALU.add)
            yield
            s2s = []
            for j in range(2):
                o = Oc1[:, j, 0:128]
                junk = self.av(SM + 3328, BF, 128)
                s2 = self.av(SM + 256 + j * 256, F32, 1)
                self.S.add("dve", (lambda junk=junk, o=o, s2=s2: (lambda e: e.scalar_tensor_tensor(junk, o, 1.0, o, ALU.mult, ALU.mult, accum_out=s2)))(), [o], [junk, s2])
                s2s.append(s2)
            yield
            yield
            for j in range(2):
                self.act(s2s[j], s2s[j], AF.Ln, bias=self.epsv, scale=1.0 / 128.0)
            yield
            for j in range(2):
                self.act(s2s[j], s2s[j], AF.Exp, scale=-0.5)
            yield
            yield
            ats = []
            for j in range(2):
                at = self.av(SM + 3584 + j * 256, BF, 128)
                self.stt(at, Oc1[:, j, 0:128], s2s[j], self.gsub[l], ALU.mult, ALU.mult)
                ats.append(at)
            yield
            yield
            pt = self.ps(3, BF, 2, 128)
            for j in range(2):
                self.tr(pt[:, j, :], ats[j], self.identb)
            self.cp(A3[:, h, q0:q0 + 256], pt.rearrange("p a b -> p (a b)"))

        for _ in proj_gen(0):
            pass
        for h in range(NH):
            bg = proj_gen(h + 1) if h + 1 < NH else None
            attn(h, bg)
        if extra is not None:
            for _ in extra:
                pass

    def mstage(self, ph, l, d, which, next_rms=None):
        T = ph.T
        w_gate = d["w_gate"][l].rearrange("(kc p) n -> p kc n", p=128)
        w_out = d["w_out"][l].rearrange("(kc p) n -> p kc n", p=128)
        g2 = self.modT[l][:, 40:48, ph.ci]
        wo = self.av(R3, BF, 8, 1024)
        SLB = R3 + 16384
        SG = R3 + 28672
        if which == 1:
            A3 = self.av(R1, BF, 8, T)
            M3 = self.av(R2, BF, 8, T)
            w_pa = d["w_pa"][l].rearrange("(kc p) n -> p kc n", p=128)
        else:
            P3 = self.av(R2, BF, 4, T)
            F3 = self.av(R2 + 16384, BF, 4, T)
            M3 = self.av(R1, BF, 8, T)
            w_pp = d["w_pp"][l].rearrange("(kc p) n -> p kc n", p=128)
            w_pf = d["w_pf"][l].rearrange("(kc p) n -> p kc n", p=128)
        it = 0
        for c in range(8):
            sidx = c % 2
            cs = slice(c * 128, (c + 1) * 128)
            if which == 1:
                wg = self.av(SLB + sidx * 6144, BF, 8, 128)
                wp = self.av(SLB + sidx * 6144 + 2048, BF, 8, 128)
                self.dma("pool", f"ms{sidx}", [(wg, w_gate[:, :, c * 128:(c + 1) * 128]), (wp, w_pa[:, :, cs])])
            else:
                wgp = self.av(SLB + sidx * 6144, BF, 8, 128)
                wgf = self.av(SLB + sidx * 6144 + 2048, BF, 8, 128)
                wpp = self.av(SLB + sidx * 6144 + 4096, BF, 4, 128)
                wpf = self.av(SLB + sidx * 6144 + 5120, BF, 4, 128)
                self.dma("pool", f"ms{sidx}", [(wgp, w_gate[:, :, 1024 + c * 128:1024 + (c + 1) * 128]),
                                               (wgf, w_gate[:, :, 2048 + c * 128:2048 + (c + 1) * 128]),
                                               (wpp, w_pp[:, :, cs]), (wpf, w_pf[:, :, cs])])
            if c == 1:
                self.dma("pool", "wout", [(wo, w_out)])
            for tb in range(ph.nblk):
                cols = slice(tb * 512, (tb + 1) * 512)
                if which == 1:
                    g_ps = self.ps(it % 2)
                    a_ps = self.ps(2 + it % 2)
                    for kc in range(8):
                        self.mm(g_ps, wg[:, kc, :], self.H3[:, kc, cols], start=(kc == 0), stop=(kc == 7))
                    for kc in range(8):
                        self.mm(a_ps, wp[:, kc, :], A3[:, kc, cols], start=(kc == 0), stop=(kc == 7))
                    sg = self.av(SG + (it % 2) * 2048, F32, 512)
                    self.act(sg, g_ps, AF.Sigmoid)
                    self.tt(M3[:, c, cols], sg, a_ps, ALU.mult)
                else:
                    gp_ps = self.ps(it % 2)
                    gf_ps = self.ps(2 + it % 2)
                    p_ps = self.ps(4)
                    f_ps = self.ps(5)
                    for kc in range(8):
                        self.mm(gp_ps, wgp[:, kc, :], self.H3[:, kc, cols], start=(kc == 0), stop=(kc == 7))
                    for kc in range(8):
                        self.mm(gf_ps, wgf[:, kc, :], self.H3[:, kc, cols], start=(kc == 0), stop=(kc == 7))
                    for kc in range(4):
                        self.mm(p_ps, wpp[:, kc, :], P3[:, kc, cols], start=(kc == 0), stop=(kc == 3))
                    for kc in range(4):
                        self.mm(f_ps, wpf[:, kc, :], F3[:, kc, cols], start=(kc == 0), stop=(kc == 3))
                    sgp = self.av(SG + (it % 2) * 2048, F32, 512)
                    sgf = self.av(SG + 4096 + (it % 2) * 2048, F32, 512)
                    tmp = self.av(SG + 8192, F32, 512)
                    self.act(sgp, gp_ps, AF.Sigmoid)
                    self.act(sgf, gf_ps, AF.Sigmoid)
                    self.tt(tmp, sgp, p_ps, ALU.mult)
                    self.tt(sgf, sgf, f_ps, ALU.mult)
                    self.tt(M3[:, c, cols], tmp, sgf, ALU.add)
                it += 1
        it = 0
        if next_rms is None:
            order = [(c, tb) for c in range(8) for tb in range(ph.nblk)]
        else:
            order = [(c, tb) for tb in range(ph.nblk) for c in range(8)]
        for (c, tb) in order:
            cols = slice(tb * 512, (tb + 1) * 512)
            o_ps = self.ps(6 + it % 2)
            for kc in range(8):
                self.mm(o_ps, wo[:, kc, c * 128:(c + 1) * 128], M3[:, kc, cols], start=(kc == 0), stop=(kc == 7))
            self.stt(self.X3[:, c, cols], o_ps, g2[:, c:c + 1], self.X3[:, c, cols], ALU.mult, ALU.add)
            it += 1
            if next_rms is not None and c == 7:
                if tb >= 1:
                    next_rms[1](tb - 1)
                next_rms[0](tb)
        if next_rms is not None:
            next_rms[1](ph.nblk - 1)

    def fourier(self, ph, l, d):
        T = ph.T
        w_in = d["w_in"][l].rearrange("(kc p) n -> p kc n", p=128)
        U3 = self.av(R2, BF, 4, T)
        F3 = self.av(R2 + 16384, BF, 4, T)
        nch = T // 128
        AT = self.av(R1, BF, nch, 4, 2, 128)
        slab = self.av(R3 + 32768, BF, 8, 512)
        self.dma("pool", "ufs", [(slab, w_in[:, :, 3584:4096])])
        it = 0
        for g in range(4):
            for tb in range(ph.nblk):
                cols = slice(tb * 512, (tb + 1) * 512)
                u_ps = self.ps(it % 2)
                for kc in range(8):
                    self.mm(u_ps, slab[:, kc, g * 128:(g + 1) * 128], self.H3[:, kc, cols], start=(kc == 0), stop=(kc == 7))
                self.cp(U3[:, g, cols], u_ps, eng=("act" if it % 2 else "dve"))
                it += 1
        for i in range(nch):
            b0 = 2 + (i % 2) * 2
            for g in range(4):
                pa = self.ps(b0 + g // 2, F32, 2, 256)
                self.mm(pa[:, g % 2, :], U3[:, g, i * 128:(i + 1) * 128], self.CS)
            self.cp(AT[:, i, 0:2, :, :].rearrange("p a b c -> p (a b c)"), self.ps(b0), eng="dve")
            self.cp(AT[:, i, 2:4, :, :].rearrange("p a b c -> p (a b c)"), self.ps(b0 + 1), eng="act")
        it = 0
        tabn = 0
        for (t0, L) in ph.seqs:
            ni = L // 128
            i0 = t0 // 128
            for tpb in range(L // 256):
                tab = self.av(R3 + (tabn % 2) * 16384, BF, ni, 2, 256)
                if ph.sample:
                    src = d["dftS"][tpb]
                else:
                    src = d["dftP"]
                self.dma("sp", f"tab{tabn % 2}", [(tab.rearrange("p a b c -> p (a b c)"), src)])
                tabn += 1
                for g in range(4):
                    f_ps = self.ps(6 + it % 2)[:, 0:256]
                    n = 0
                    for ii in range(ni):
                        for cs in range(2):
                            self.mm(f_ps, AT[:, i0 + ii, g, cs, :], tab[:, ii, cs, :], start=(n == 0), stop=(n == 2 * ni - 1))
                            n += 1
                    self.cp(F3[:, g, t0 + tpb * 256:t0 + (tpb + 1) * 256], f_ps, eng=("act" if it % 2 else "dve"))
                    it += 1

    def poolmix(self, ph, l, d):
        T = ph.T
        w_in = d["w_in"][l].rearrange("(kc p) n -> p kc n", p=128)
        P3 = self.av(R2, BF, 4, T)
        nseq = len(ph.seqs)
        L = ph.seqs[0][1]
        Lp = L + 16
        slab = self.av(R3, BF, 8, 512)
        self.dma("pool", "ups", [(slab, w_in[:, :, 3072:3584])])
        wpl = self.av(R3 + 8192, BF, 4, 128)
        self.dma("pool", "wpool", [(wpl, d["w_pool"][l].rearrange("g c e -> c g e"))])
        bufsz = ((nseq * Lp * 4 + 255) // 256) * 256
        U = self.av(R1, F32, nseq, Lp)
        Q = [self.av(R1 + bufsz * (1 + i), F32, nseq, Lp) for i in range(2)]
        Dg = self.av(R1 + 3 * bufsz, BF, nseq, L)
        assert 3 * bufsz + T * 2 <= 32768
        it = 0
        for g in range(4):
            w = POOLW[g]
            lv = g + 1
            self.memset(U[:, :, 0:8], 0.0)
            self.memset(U[:, :, L + 8:L + 16], 0.0)
            for tb in range(ph.nblk):
                cols = slice(tb * 512, (tb + 1) * 512)
                u_ps = self.ps(it % 2)
                for kc in range(8):
                    self.mm(u_ps, slab[:, kc, g * 128:(g + 1) * 128], self.H3[:, kc, cols], start=(kc == 0), stop=(kc == 7))
                if nseq == 1:
                    self.cp(U[:, 0, 8 + tb * 512:8 + (tb + 1) * 512], u_ps, eng=("act" if it % 2 else "dve"))
                else:
                    self.cp(U[:, :, 8:8 + L], u_ps.rearrange("p (s t) -> p s t", s=nseq), eng="dve")
                it += 1
            src = U
            for k in range(1, lv + 1):
                sh = 1 << (k - 1)
                dst = Q[(k - 1) % 2]
                self.tt(dst[:, :, 0:Lp - sh], src[:, :, 0:Lp - sh], src[:, :, sh:Lp], ALU.add, eng="dve")
                src = dst
            hw = w // 2
            Ssh = src[:, :, 8 - hw:8 - hw + L]
            Uc = U[:, :, 8:8 + L]
            tmpD = Q[lv % 2][:, :, 0:L]
            self.stt(tmpD, Ssh, 1.0 / w, Uc, ALU.mult, ALU.subtract)
            tb0 = {2: 0, 4: 2, 8: 6, 16: 14}[w]
            fl = self.ptab[:, tb0:tb0 + hw]
            fr = self.ptab[:, tb0 + hw:tb0 + 2 * hw]
            for s in range(nseq):
                bl = self.av(P_MISC + 128, F32, 8)[:, 0:hw]
                self.tt(bl, Ssh[:, s, 0:hw], fl, ALU.mult)
                self.tt(tmpD[:, s, 0:hw], bl, Uc[:, s, 0:hw], ALU.subtract)
                br = self.av(P_MISC + 192, F32, 8)[:, 0:hw]
                self.tt(br, Ssh[:, s, L - hw:L], fr, ALU.mult)
                self.tt(tmpD[:, s, L - hw:L], br, Uc[:, s, L - hw:L], ALU.subtract)
            self.cp(Dg, tmpD, eng="act")
            Dflat = Dg.rearrange("p s t -> p (s t)")
            for tb in range(ph.nblk):
                cols = slice(tb * 512, (tb + 1) * 512)
                p_ps = self.ps(2 + tb % 2)
                self.mm(p_ps, wpl[:, g, :], Dflat[:, cols])
                self.act(P3[:, g, cols], p_ps, AF.Identity, scale=self.psT[l][:, g:g + 1])

    def layer(self, ph, l, d, pidx, first=True, last=True):
        cfg = self.cfg
        RSO = R3 + 22528 + 4096
        full = all(cfg.get(k, True) for k in ("ffn", "mixer", "attn", "pf", "ffn2"))
        if first or not full:
            self.derive(ph, l)
        dv = self.dvs[l % 2]
        if not full:
            if cfg.get("ffn", True):
                self.rms_mod(ph, dv[:, 0, :], self.shift(l, ph, 0), RSO)
                self.ffn(ph, l, 0, d, dv[:, 3, :])
            if cfg.get("mixer", True):
                self.rms_mod(ph, dv[:, 1, :], self.shift(l, ph, 1), RSO)
                if cfg.get("attn", True):
                    self.attention(ph, l, d, pidx)
                    self.mstage(ph, l, d, 1)
                if cfg.get("pf", True):
                    self.fourier(ph, l, d)
                    self.poolmix(ph, l, d)
                    self.mstage(ph, l, d, 2)
            if cfg.get("ffn2", True):
                self.rms_mod(ph, dv[:, 2, :], self.shift(l, ph, 2), RSO)
                self.ffn(ph, l, 1, d, dv[:, 4, :])
            return
        if not first:
            self.derive(ph, l)
        self.rms_mod(ph, dv[:, 0, :], self.shift(l, ph, 0), RSO)
        self.ffn(ph, l, 0, d, dv[:, 3, :])
        self.rms_mod(ph, dv[:, 1, :], self.shift(l, ph, 1), RSO)
        extra = None
        if self.defer_mod1 and pidx == 0 and l == 0:
            extra = self.stage_mod_gen(d, 1, R1 + 16384, R1 + 24576, 3)
        self.attention(ph, l, d, pidx, extra=extra)
        self.mstage(ph, l, d, 1)
        self.fourier(ph, l, d)
        self.poolmix(ph, l, d)
        self.mstage(ph, l, d, 2)
        self.rms_mod(ph, dv[:, 2, :], self.shift(l, ph, 2), RSO)
        self.ffn(ph, l, 1, d, dv[:, 4, :])


def build(cfg=None):
    cfg = cfg or {}
    nc = bass.Bass("TRN2", target_bir_lowering=False)
    d = {}

    def inp(name, shape, dt=F32):
        d[name] = nc.dram_tensor(name, list(shape), dt, kind="ExternalInput").ap()

    def outp(name, shape):
        d[name] = nc.dram_tensor(name, list(shape), F32, kind="ExternalOutput").ap()

    inp("xs", [2048, D]); inp("xp", [512, D])
    inp("ck", [2, 512, 8, 128]); inp("cv", [2, 512, 8, 128])
    inp("cond", [16, 128])
    inp("w_ada", [2, D, 9 * D]); inp("b_ada", [2, 72, 128]); inp("norm_g", [2, 24, 128])
    inp("ffn_w13", [2, 2, D, 2 * DFF]); inp("ffn_w2", [2, 2, DFF, D])
    inp("w_in", [2, D, 4096]); inp("q_norm_g", [2, 64]); inp("k_norm_g", [2, 64])
    inp("lam_qk", [2, 256]); inp("subln_g", [2, 128]); inp("w_pool", [2, 4, 128, 128])
    inp("pool_scale", [2, 4, 128]); inp("w_gate", [2, D, 3 * D]); inp("w_pa", [2, D, D])
    inp("w_pp", [2, 512, D]); inp("w_pf", [2, 512, D]); inp("w_out", [2, D, D])
    inp("cb", [128, 640], BF); inp("cf", [128, 288]); inp("rope", [128, 2 * 2048])
    inp("dftS", [8, 128, 16 * 2 * 256], BF); inp("dftP", [128, 2 * 2 * 256], BF)
    outp("ys", [2048, D]); outp("yp", [512, D])
    outp("sk", [2, 2, 256, 8, 128]); outp("sv", [2, 2, 256, 8, 128])

    with ExitStack() as es:
        k = K(nc, es, cfg)
        k.slabctr = 0
        k.setup_consts(d)
        k.epsv = k.av(P_MISC + 256, F32, 1)
        k.memset(k.epsv, EPS)
        k.stage_params(d)
        k.stage_mod(d, 0)
        full = all(cfg.get(kk, True) for kk in ("ffn", "mixer", "attn", "pf", "ffn2"))
        k.defer_mod1 = bool(cfg.get("P", True) and full and cfg.get("layers", 2) > 1)
        if cfg.get("layers", 2) > 1 and not k.defer_mod1:
            k.stage_mod(d, 1)
        phases = []
        if cfg.get("P", True):
            phases.append((Phase("P", 512, [(0, 256), (256, 256)], 0, False), d["xp"], d["yp"]))
        if cfg.get("S", True):
            phases.append((Phase("S", 2048, [(0, 2048)], 1, True), d["xs"], d["ys"]))
        for pidx, (ph, xd, yd) in enumerate(phases):
            k.views(ph)
            k.load_x(ph, xd)
            nl = cfg.get("layers", 2)
            for l in range(nl):
                if pidx == 0 and l == 0 and nl > 1:
                    pass
                k.layer(ph, l, d, pidx, first=(l == 0), last=(l == nl - 1))
            k.store_x(ph, yd)
        S = k.S
        S.add("sp", lambda e: None, [], [], extra_deps=list(k.out_ops))
        keys = S.finalize()
        sems = {kk: es.enter_context(nc.semaphore(f"s{i}")) for i, kk in enumerate(keys)}
        with nc.Block() as block:
            @block.tensor
            def _(e):
                S.emit("pe", e, sems)

            @block.scalar
            def _(e):
                S.emit("act", e, sems)

            @block.vector
            def _(e):
                S.emit("dve", e, sems)

            @block.gpsimd
            def _(e):
                S.emit("pool", e, sems)

            @block.sync
            def _(e):
                S.emit("sp", e, sems)
        k.nops = len(S.ops)
        k.nsem = len(keys)
    return nc, k


def host_consts():
    bf = ml_dtypes.bfloat16
    cb = np.zeros((128, 640), np.float32)
    cb[:, 0:128] = np.eye(128)
    cb[:, 128:256] = 1.0 / 1024.0
    blk = np.zeros((128, 128))
    blk[:64, :64] = 1.0 / 64
    blk[64:, 64:] = 1.0 / 64
    cb[:, 256:384] = blk
    cc = np.arange(128)[:, None] * np.arange(128)[None, :]
    ang = 2 * np.pi * (cc % 128) / 128.0
    cb[:, 384:512] = np.cos(ang) / np.sqrt(128.0)
    cb[:, 512:640] = np.sin(ang) / np.sqrt(128.0)
    cf = np.zeros((128, 288), np.float32)
    cf[:, 0:128] = np.eye(128)
    R = np.zeros((128, 128), np.float32)
    for m in range(128):
        if (m % 32) < 16:
            R[m + 16, m] = -1.0
        else:
            R[m - 16, m] = 1.0
    cf[:, 128:256] = R
    col = 256
    for w in POOLW:
        hw = w // 2
        for t in range(hw):
            cf[:, col + t] = 1.0 / (t + hw)
        for i in range(hw):
            cf[:, col + hw + i] = 1.0 / (2 * hw - i)
        col += 2 * hw
    p = np.arange(128)
    dd = p % 64
    axis = dd // 32
    freq = dd % 16
    inv = (10000.0 ** (-np.arange(16, dtype=np.float32) / np.float32(16))).astype(np.float32)
    t = np.arange(2048)
    row = (t // GRID_W).astype(np.float32)
    colp = (t % GRID_W).astype(np.float32)
    pos = np.where(axis[:, None] == 0, row[None, :], colp[None, :]).astype(np.float32)
    angr = (pos * inv[freq][:, None]).astype(np.float32)
    rope = np.stack([np.cos(angr), np.sin(angr)], axis=1).astype(np.float32).reshape(128, 4096)

    def dft(L):
        tt = np.arange(L)[:, None].astype(np.int64)
        kk = np.arange(L)[None, :].astype(np.int64)
        a = 2 * np.pi * ((tt * kk) % L) / L
        return np.cos(a) / np.sqrt(L), -np.sin(a) / np.sqrt(L)
    c, s = dft(2048)
    tab = np.stack([c, s], axis=0)
    tab = tab.reshape(2, 16, 128, 8, 256)
    dftS = np.ascontiguousarray(tab.transpose(3, 2, 1, 0, 4)).reshape(8, 128, 16 * 2 * 256)
    c, s = dft(256)
    tab = np.stack([c, s], axis=0).reshape(2, 2, 128, 256)
    dftP = np.ascontiguousarray(tab.transpose(2, 1, 0, 3)).reshape(128, 2 * 2 * 256)
    return dict(cb=cb.astype(bf), cf=cf, rope=rope, dftS=dftS.astype(bf), dftP=dftP.astype(bf))


_CACHE = {}


def make_in_maps(inputs, ncores=8):
    f = lambda a: np.ascontiguousarray(np.asarray(a, dtype=np.float32))
    hc = host_consts()
    shared = dict(
        w_ada=f(inputs["w_ada"]), b_ada=f(inputs["b_ada"]).reshape(2, 72, 128), norm_g=f(inputs["norm_g"]).reshape(2, 24, 128),
        ffn_w13=f(inputs["ffn_w13"]), ffn_w2=f(inputs["ffn_w2"]), w_in=f(inputs["w_in"]),
        q_norm_g=f(inputs["q_norm_g"]), k_norm_g=f(inputs["k_norm_g"]), lam_qk=f(inputs["lam_qk"]).reshape(2, 256),
        subln_g=f(inputs["subln_g"]), w_pool=f(inputs["w_pool"]), pool_scale=f(inputs["pool_scale"]).reshape(2, 4, 128),
        w_gate=f(inputs["w_gate"]), w_pa=f(inputs["w_pa"]), w_pp=f(inputs["w_pp"]), w_pf=f(inputs["w_pf"]),
        w_out=f(inputs["w_out"]), **hc)
    xp = f(inputs["x_prompt"]); xs = f(inputs["x_sample"])
    ck = f(inputs["cache_k"]); cv = f(inputs["cache_v"])
    c = f(inputs["c"]); cc = f(inputs["c_ctx"])
    maps = []
    for i in range(ncores):
        m = dict(shared)
        m["xs"] = xs[i]
        m["xp"] = xp[2 * i:2 * i + 2].reshape(512, D)
        m["ck"] = ck[i]
        m["cv"] = cv[i]
        m["cond"] = np.ascontiguousarray(np.concatenate([cc.reshape(8, 128), c[i].reshape(8, 128)], axis=0))
        maps.append(m)
    return maps


def kernel(**inputs):
    if "nc" not in _CACHE:
        _CACHE["nc"] = build()[0]
    nc = _CACHE["nc"]
    maps = make_in_maps(inputs)
    res = run_bass_kernel_spmd(nc, maps, core_ids=list(range(8)))
    r = res.results
    y_prompt = np.concatenate([r[i]["yp"].reshape(2, 256, D) for i in range(8)], axis=0).astype(np.float32)
    y_sample = np.stack([r[i]["ys"] for i in range(8)], axis=0).astype(np.float32)
    state_k = np.concatenate([r[i]["sk"] for i in range(8)], axis=0).astype(np.float32)
    state_v = np.concatenate([r[i]["sv"] for i in range(8)], axis=0).astype(np.float32)
    return (y_prompt, y_sample, state_k, state_v)
```

```python
import math
from contextlib import ExitStack
import numpy as np
import ml_dtypes
import concourse.bass as bass
import concourse.mybir as mybir
from concourse.bass_utils import run_bass_kernel_spmd

F32 = mybir.dt.float32
BF = mybir.dt.bfloat16
AF = mybir.ActivationFunctionType
ALU = mybir.AluOpType
AX = mybir.AxisListType

D = 1024
NH = 8
DFF = 2816
EPS = 1e-6
GRID_W = 64
POOLW = (2, 4, 8, 16)


def dsz(dt):
    return 4 if dt == F32 else 2


class Op:
    __slots__ = ("eng", "fn", "deps", "dma_sem", "ndma", "count", "sig", "sigsem", "sigcnt", "waits")

    def __init__(self, eng, fn, dma_sem, ndma):
        self.eng = eng
        self.fn = fn
        self.deps = set()
        self.dma_sem = dma_sem
        self.ndma = ndma
        self.count = 0
        self.sig = False
        self.sigsem = None
        self.sigcnt = 0
        self.waits = None


class Sched:
    EPOCH = 8000

    def __init__(self, tracked):
        self.ops = []
        self.cells = {}
        self.tracked = tracked
        self.dma_counts = {}
        self.last_dma_on_sem = {}

    def cells_of(self, ap):
        nm = ap.tensor.name
        info = self.tracked.get(nm)
        if info is None:
            return ()
        gran, rowbytes = info
        d = dsz(ap.dtype)
        pat = ap.ap
        row_elems = rowbytes // d
        off = int(ap.offset) % row_elems
        dims = [(int(s), int(n)) for (s, n) in pat[1:]]
        if not dims:
            dims = [(1, 1)]
        inner_s, inner_n = dims[-1]
        outer = dims[:-1]
        nouter = 1
        for s, n in outer:
            nouter *= n
        if inner_s in (0, 1):
            ilen = inner_n if inner_s == 1 else 1
        else:
            ilen = inner_s * (inner_n - 1) + 1
        runs = []
        if nouter <= 256:
            offs = [off]
            for s, n in outer:
                offs = [o + s * i for o in offs for i in range(n)]
            for o in offs:
                runs.append((o, o + ilen))
        else:
            hi = off + ilen
            for s, n in outer:
                hi += s * (n - 1)
            runs.append((off, hi))
        out = set()
        for lo, hi in runs:
            b0 = (lo * d) // gran
            b1 = (hi * d - 1) // gran
            for b in range(b0, b1 + 1):
                out.add((nm, b))
        return out

    def add(self, eng, fn, reads=(), writes=(), dma_sem=None, ndma=0, extra_deps=()):
        idx = len(self.ops)
        op = Op(eng, fn, dma_sem, ndma)
        deps = set(extra_deps)
        rc = set()
        for a in reads:
            rc |= set(self.cells_of(a))
        wc = set()
        for a in writes:
            wc |= set(self.cells_of(a))
        cells = self.cells
        rkey = eng if dma_sem is None else ("dma", dma_sem)
        for c in rc:
            st = cells.get(c)
            if st is not None:
                if st[0] is not None:
                    deps.add(st[0])
                if c[0].startswith("ps"):
                    for rk, ri in st[1].items():
                        if rk != rkey:
                            deps.add(ri)
        for c in wc:
            st = cells.get(c)
            if st is not None:
                if st[0] is not None:
                    deps.add(st[0])
                deps.update(st[1].values())
        rkey = eng if dma_sem is None else ("dma", dma_sem)
        for c in rc:
            if c in wc:
                continue
            st = cells.get(c)
            if st is None:
                cells[c] = [None, {rkey: idx}]
            else:
                st[1][rkey] = idx
        for c in wc:
            cells[c] = [idx, {}]
        if dma_sem is not None:
            prev = self.last_dma_on_sem.get(dma_sem)
            if prev is not None:
                deps.add(prev)
            self.last_dma_on_sem[dma_sem] = idx
            cnt = self.dma_counts.get(dma_sem, 0) + 16 * ndma
            self.dma_counts[dma_sem] = cnt
            op.count = cnt
        deps.discard(idx)
        op.deps = deps
        self.ops.append(op)
        return idx

    def finalize(self):
        ops = self.ops
        for op in ops:
            nd = set()
            for d in op.deps:
                dop = ops[d]
                if dop.dma_sem is None and dop.eng == "pe" and op.eng == "pe" and op.dma_sem is None:
                    continue
                nd.add(d)
                if dop.dma_sem is None:
                    dop.sig = True
            op.deps = nd
        cnt = {}
        for op in ops:
            if op.dma_sem is None and op.sig:
                c = cnt.get(op.eng, 0)
                ep, within = divmod(c, self.EPOCH)
                op.sigsem = ("eng", op.eng, ep)
                op.sigcnt = within + 1
                cnt[op.eng] = c + 1
        semkeys = set()
        for op in ops:
            w = {}
            for d in op.deps:
                dop = ops[d]
                if dop.dma_sem is not None:
                    k = ("dma", dop.dma_sem)
                    v = dop.count
                else:
                    k = dop.sigsem
                    v = dop.sigcnt
                if w.get(k, 0) < v:
                    w[k] = v
                semkeys.add(k)
            op.waits = w
            if op.dma_sem is not None:
                semkeys.add(("dma", op.dma_sem))
            elif op.sig:
                semkeys.add(op.sigsem)
        return sorted(semkeys, key=str)

    def emit(self, eng, e, sems):
        waited = {}
        for op in self.ops:
            if op.eng != eng:
                continue
            for k, v in op.waits.items():
                if waited.get(k, 0) < v:
                    e.wait_ge(sems[k], v)
                    waited[k] = v
            if op.dma_sem is not None:
                op.fn(e, sems[("dma", op.dma_sem)])
            else:
                ins = op.fn(e)
                if op.sig:
                    ins.then_inc(sems[op.sigsem], 1)


CB_OFF = 0
CF_OFF = 1280
PAR_OFF = 2560
X_OFF = 7680
H_OFF = X_OFF + 65536
R1 = H_OFF + 32768
R2 = R1 + 32768
R3 = R2 + 32768
ARENA = 212736
R3SZ = ARENA - R3
assert R3SZ >= 40960

P_MODT = PAR_OFF
P_BT = P_MODT + 1152
P_GT = P_BT + 576
P_CT = P_GT + 192
P_SCT = P_CT + 64
P_GQK = P_SCT + 64
P_PST = P_GQK + 64
P_NLAM = P_PST + 64
P_GSUB = P_NLAM + 64
P_DV = P_GSUB + 1024
P_MISC = P_DV + 256
assert P_MISC + 512 <= X_OFF


class Phase:
    def __init__(self, name, T, seqs, ci, sample):
        self.name = name
        self.T = T
        self.nblk = T // 512
        self.seqs = seqs
        self.ci = ci
        self.sample = sample
        self.nk = T + 512 if sample else T


class K:
    def __init__(self, nc, es, cfg):
        self.nc = nc
        self.cfg = cfg
        self.arena = es.enter_context(nc.sbuf_tensor("arena", [128, ARENA // 2], BF))
        self.banks = [es.enter_context(nc.psum_tensor(f"ps{i}", [128, 512], F32)) for i in range(8)]
        tracked = {"arena": (256, ARENA)}
        for i in range(8):
            tracked[f"ps{i}"] = (2048, 2048)
        self.S = Sched(tracked)
        self.out_ops = []

    def av(self, off, dt, *dims, p0=0, p1=128):
        n = 1
        for x in dims:
            n *= int(x)
        nb = n * dsz(dt)
        assert off % 4 == 0 and off + nb <= ARENA, (off, nb)
        ap = self.arena[p0:p1, off // 2:(off + nb) // 2]
        if dt != BF:
            ap = ap.bitcast(dt)
        if len(dims) > 1:
            names = " ".join(f"d{i}" for i in range(len(dims)))
            ap = ap.rearrange(f"p ({names}) -> p {names}", **{f"d{i}": int(dims[i]) for i in range(len(dims))})
        return ap

    def ps(self, bank, dt=F32, *dims):
        ap = self.banks[bank][:]
        if dt != F32:
            ap = ap.bitcast(dt)
        tot = 512 if dt == F32 else 1024
        n = 1
        for x in dims:
            n *= int(x)
        if not dims:
            return ap
        ap = ap[:, 0:n]
        if len(dims) > 1:
            names = " ".join(f"d{i}" for i in range(len(dims)))
            ap = ap.rearrange(f"p ({names}) -> p {names}", **{f"d{i}": int(dims[i]) for i in range(len(dims))})
        return ap

    def mm(self, out, lhsT, rhs, start=True, stop=True, skip=False):
        if skip:
            fn = lambda e: e.matmul(out, lhsT, rhs, start=start, stop=stop, skip_group_check=True)
        else:
            fn = lambda e: e.matmul(out, lhsT, rhs, start=start, stop=stop)
        self.S.add("pe", fn, [lhsT, rhs], [out])

    def tr(self, out, in_, ident):
        self.S.add("pe", lambda e: e.transpose(out, in_, ident), [in_, ident], [out])

    def act(self, out, in_, func, bias=None, scale=None, accum=None):
        kw = {}
        reads = [in_]
        if bias is not None:
            kw["bias"] = bias
            if not isinstance(bias, (int, float)):
                reads.append(bias)
        if scale is not None:
            kw["scale"] = scale
            if not isinstance(scale, (int, float)):
                reads.append(scale)
        writes = [out]
        if accum is not None:
            kw["accum_out"] = accum
            writes.append(accum)
        self.S.add("act", lambda e: e.activation(out, in_, func, **kw), reads, writes)

    def tt(self, out, a, b, op, eng="dve"):
        self.S.add(eng, lambda e: e.tensor_tensor(out, a, b, op), [a, b], [out])

    def ts(self, out, a, s1, s2, op0, op1=None, eng="dve"):
        reads = [a]
        if not isinstance(s1, (int, float)):
            reads.append(s1)
        if s2 is not None and not isinstance(s2, (int, float)):
            reads.append(s2)
        if op1 is None:
            fn = lambda e: e.tensor_scalar(out, a, s1, None, op0)
        else:
            fn = lambda e: e.tensor_scalar(out, a, s1, s2, op0, op1)
        self.S.add(eng, fn, reads, [out])

    def stt(self, out, a, s, b, op0, op1, eng="dve"):
        reads = [a, b]
        if not isinstance(s, (int, float)):
            reads.append(s)
        self.S.add(eng, lambda e: e.scalar_tensor_tensor(out, a, s, b, op0, op1), reads, [out])

    def cp(self, out, in_, eng="dve"):
        if eng == "act":
            self.S.add("act", lambda e: e.activation(out, in_, AF.Copy), [in_], [out])
        else:
            self.S.add(eng, lambda e: e.tensor_copy(out, in_), [in_], [out])

    def rcp(self, out, in_):
        self.S.add("dve", lambda e: e.reciprocal(out, in_), [in_], [out])

    def rsum(self, out, in_):
        self.S.add("dve", lambda e: e.reduce_sum(out, in_, AX.X), [in_], [out])

    def memset(self, out, val, eng="pool"):
        self.S.add(eng, lambda e: e.memset(out, val), [], [out])

    def dma(self, eng, sem, pairs, is_out=False):
        pairs = list(pairs)

        def fn(e, s):
            for o, i in pairs:
                e.dma_start(out=o, in_=i).then_inc(s, 16)
        idx = self.S.add(eng, fn, [i for o, i in pairs], [o for o, i in pairs], dma_sem=sem, ndma=len(pairs))
        if is_out:
            self.out_ops.append(idx)
        return idx

    def setup_consts(self, d):
        self.cb = self.av(CB_OFF, BF, 640)
        self.cf = self.av(CF_OFF, F32, 288)
        self.dma("sp", "c0", [(self.cb, d["cb"]), (self.cf, d["cf"])])
        self.identb = self.cb[:, 0:128]
        self.onesD = self.cb[:, 128:256]
        self.bones = self.cb[:, 256:384]
        self.CS = self.cb[:, 384:640]
        self.identf = self.cf[:, 0:128]
        self.Rm = self.cf[:, 128:256]
        self.ptab = self.cf[:, 256:288]

    def load_T(self, dst, src_rows, r, stage_off, bank):
        st = self.av(stage_off, F32, 128, p1=r)
        self.dma("sp", "ldT", [(st, src_rows)])
        pst = self.ps(bank)[:, 0:r]
        self.tr(pst, st, self.identf[0:r, 0:r])
        self.cp(dst, pst)

    def stage_params(self, d):
        S0 = R3 + 36864
        self.modT = [self.av(P_MODT + l * 576, F32, 72, 2) for l in range(2)]
        self.bT = [self.av(P_BT + l * 288, F32, 72) for l in range(2)]
        self.gT = [self.av(P_GT + l * 96, F32, 24) for l in range(2)]
        self.cT = self.av(P_CT, F32, 16)
        self.scT = self.av(P_SCT, BF, 2, 8)
        self.gqk = [self.av(P_GQK + l * 8, F32, 2) for l in range(2)]
        self.psT = [self.av(P_PST + l * 16, F32, 4) for l in range(2)]
        self.nlam = [self.av(P_NLAM + l * 4, F32, 1) for l in range(2)]
        self.gsub = [self.av(P_GSUB + l * 512, F32, 128) for l in range(2)]
        self.dvs = [self.av(P_DV, F32, 5, 8), self.av(P_MISC + 320, F32, 5, 8)]
        self.load_T(self.cT, d["cond"], 16, S0, 0)
        self.act(self.scT.rearrange("p a b -> p (a b)"), self.cT, AF.Silu)
        for l in range(2):
            self.load_T(self.bT[l], d["b_ada"][l], 72, S0, 0)
            self.load_T(self.gT[l], d["norm_g"][l], 24, S0, 0)
            self.load_T(self.psT[l], d["pool_scale"][l], 4, S0, 0)
            qg = d["q_norm_g"][l:l + 1, :].rearrange("o d -> d o")
            kg = d["k_norm_g"][l:l + 1, :].rearrange("o d -> d o")
            self.dma("sp", "gqk", [(self.gqk[l][0:64, 0:1], qg), (self.gqk[l][64:128, 0:1], qg),
                                   (self.gqk[l][0:64, 1:2], kg), (self.gqk[l][64:128, 1:2], kg)])
            lq = self.av(S0 + 1024, F32, 4, 64)
            self.dma("sp", "lq", [(lq.rearrange("p a b -> p (a b)"), d["lam_qk"][l:l + 1, :].partition_broadcast(128))])
            pr = self.av(S0 + 2048, F32, 2, 64)
            self.tt(pr[:, 0, :], lq[:, 0, :], lq[:, 1, :], ALU.mult)
            self.tt(pr[:, 1, :], lq[:, 2, :], lq[:, 3, :], ALU.mult)
            sm = self.av(P_MISC, F32, 2)
            self.rsum(sm, pr)
            ex = self.av(P_MISC + 64, F32, 2)
            self.act(ex, sm, AF.Exp)
            lam_init = 0.8 - 0.6 * math.exp(-0.3 * l)
            self.tt(self.nlam[l], ex[:, 1:2], ex[:, 0:1], ALU.subtract)
            self.ts(self.nlam[l], self.nlam[l], -lam_init, None, ALU.add)
            self.dma("sp", "gsub", [(self.gsub[l], d["subln_g"][l:l + 1, :].partition_broadcast(128))])
            self.ts(self.gsub[l], self.gsub[l], 1.0 - lam_init, None, ALU.mult)
        self.nsl = 0

    def stage_mod(self, d, l):
        if True:
            mps = self.ps(6, F32, 72, 2)
            wv = d["w_ada"][l].rearrange("(kc p) n -> p kc n", p=128)
            for sb in range(18):
                slab = self.av(R3 + (self.nsl % 2) * 8192, BF, 8, 512)
                self.dma("pool", f"wada{self.nsl % 2}", [(slab, wv[:, :, sb * 512:(sb + 1) * 512])])
                self.nsl += 1
                for jj in range(4):
                    j = sb * 4 + jj
                    for kc in range(8):
                        self.mm(mps[:, j, :], slab[:, kc, jj * 128:(jj + 1) * 128], self.scT[:, :, kc],
                                start=(kc == 0), stop=(kc == 7))
            for ci in range(2):
                self.tt(self.modT[l][:, :, ci], mps[:, :, ci], self.bT[l], ALU.add)

    def stage_mod_gen(self, d, l, off0, off1, bank):
        wv = d["w_ada"][l].rearrange("(kc p) n -> p kc n", p=128)
        offs = (off0, off1)
        slabs = {}

        def issue(sb):
            slab = self.av(offs[sb % 2], BF, 8, 512)
            self.dma("pool", f"wadab{sb % 2}", [(slab, wv[:, :, sb * 512:(sb + 1) * 512])])
            slabs[sb] = slab
        issue(0)
        for sb in range(18):
            if sb + 1 < 18:
                issue(sb + 1)
            yield
            slab = slabs.pop(sb)
            mps = self.ps(bank, F32, 4, 2)
            for jj in range(4):
                for kc in range(8):
                    self.mm(mps[:, jj, :], slab[:, kc, jj * 128:(jj + 1) * 128], self.scT[:, :, kc],
                            start=(kc == 0), stop=(kc == 7))
            for ci in range(2):
                self.tt(self.modT[l][:, sb * 4:sb * 4 + 4, ci], mps[:, :, ci], self.bT[l][:, sb * 4:sb * 4 + 4], ALU.add)
            yield

    def derive(self, ph, l):
        m = self.modT[l]
        ci = ph.ci
        dv = self.dvs[l % 2]
        for k in range(3):
            sc = m[:, (3 * k + 1) * 8:(3 * k + 2) * 8, ci]
            self.stt(dv[:, k, :], sc, 1.0, self.gT[l][:, k * 8:(k + 1) * 8], ALU.add, ALU.mult)
        self.ts(dv[:, 3, :], m[:, 16:24, ci], 0.5, None, ALU.mult)
        self.ts(dv[:, 4, :], m[:, 64:72, ci], 0.5, None, ALU.mult)

    def shift(self, l, ph, k):
        return self.modT[l][:, (3 * k) * 8:(3 * k + 1) * 8, ph.ci]

    def views(self, ph):
        T = ph.T
        self.X3 = self.av(X_OFF, F32, 8, T)
        self.H3 = self.av(H_OFF, BF, 8, T)

    def load_x(self, ph, xd):
        for i in range(ph.T // 128):
            st = self.av(R3 + (i % 2) * 4096, F32, 1024)
            self.dma("sp", f"xin{i % 2}", [(st, xd[i * 128:(i + 1) * 128, :])])
            b0 = (i % 2) * 2
            for c in range(8):
                self.tr(self.ps(b0 + c // 4)[:, (c % 4) * 128:(c % 4 + 1) * 128], st[:, c * 128:(c + 1) * 128], self.identf)
            cols = slice(i * 128, (i + 1) * 128)
            self.cp(self.X3[:, 0:4, cols], self.ps(b0, F32, 4, 128), eng="dve")
            self.cp(self.X3[:, 4:8, cols], self.ps(b0 + 1, F32, 4, 128), eng="act")

    def store_x(self, ph, yd, tiles=None, stoff=None):
        stoff = R3 if stoff is None else stoff
        for i in (range(ph.T // 128) if tiles is None else tiles):
            st = self.av(stoff + (i % 2) * 4096, F32, 1024)
            b0 = (i % 2) * 2
            cols = slice(i * 128, (i + 1) * 128)
            for c in range(8):
                self.tr(self.ps(b0 + c // 4)[:, (c % 4) * 128:(c % 4 + 1) * 128], self.X3[:, c, cols], self.identf)
            self.cp(st[:, 0:512], self.ps(b0), eng="dve")
            self.cp(st[:, 512:1024], self.ps(b0 + 1), eng="act")
            self.dma("sp", f"xout{i % 2}", [(yd[i * 128:(i + 1) * 128, :], st)], is_out=True)

    def rms_mod(self, ph, A_vec, shift_vec, so):
        for tb in range(ph.nblk):
            self.rms_a(ph, so, tb)
            self.rms_b(ph, A_vec, shift_vec, so, tb)

    def rms_a(self, ph, so, tb):
        cols = slice(tb * 512, (tb + 1) * 512)
        for c in range(8):
            sq = self.av(so + c * 1024, BF, 512)
            x = self.X3[:, c, cols]
            if c % 8 in (1, 4, 6):
                self.tt(sq, x, x, ALU.mult)
            else:
                self.act(sq, x, AF.Square)

    def rms_b(self, ph, A_vec, shift_vec, so, tb):
        cols = slice(tb * 512, (tb + 1) * 512)
        ss = self.ps(7 - tb % 2)
        for c in range(8):
            sq = self.av(so + c * 1024, BF, 512)
            self.mm(ss, self.onesD, sq, start=(c == 0), stop=(c == 7))
        sd = self.av(so + 8192, F32, 512)
        self.act(sd, ss, AF.Sqrt, bias=self.epsv, scale=1.0)
        self.rcp(sd, sd)
        for c in range(8):
            tmp = self.av(so + 10240 + (c % 2) * 2048, F32, 512)
            self.tt(tmp, self.X3[:, c, cols], sd, ALU.mult)
            self.act(self.H3[:, c, cols], tmp, AF.Identity, bias=shift_vec[:, c:c + 1], scale=A_vec[:, c:c + 1])

    def ffn(self, ph, l, f, d, hg_vec, next_rms=None):
        T = ph.T
        w13 = d["ffn_w13"][l, f].rearrange("(kc p) n -> p kc n", p=128)
        w2d = d["ffn_w2"][l, f]
        ACT3 = self.av(R1, BF, 11, T)
        SL = R1 + 45056
        w2 = self.av(R3, BF, 11, 1024)
        SG = R3 + 22528
        for half in range(2):
            ch0 = half * 11
            w2_issued = False
            it = 0
            per = 2 if ph.sample else 4
            SLp = SL if ph.sample else R1 + 16384
            for s0 in range(0, 11, per):
                nj = min(per, 11 - s0)
                n = nj * 128
                sidx = self.slabctr % (2 if ph.sample else 3)
                self.slabctr += 1
                slab = self.av(SLp + sidx * 4096 * per, BF, 8, 2, 128 * per)
                c0 = (ch0 + s0) * 128
                self.dma("pool", f"w13_{sidx}", [(slab[:, :, 0, 0:n], w13[:, :, c0:c0 + n]),
                                                  (slab[:, :, 1, 0:n], w13[:, :, DFF + c0:DFF + c0 + n])])
                if s0 >= per and not w2_issued:
                    self.dma("pool", "w2", [(w2, w2d[ch0 * 128:(ch0 + 11) * 128, :].rearrange("(j p) n -> p j n", p=128))])
                    w2_issued = True
                for jj in range(nj):
                    jl = s0 + jj
                    for tb in range(ph.nblk):
                        cols = slice(tb * 512, (tb + 1) * 512)
                        g_ps = self.ps(it % 2)
                        u_ps = self.ps(2 + it % 2)
                        for kc in range(8):
                            self.mm(g_ps, slab[:, kc, 0, jj * 128:(jj + 1) * 128], self.H3[:, kc, cols], start=(kc == 0), stop=(kc == 7))
                        for kc in range(8):
                            self.mm(u_ps, slab[:, kc, 1, jj * 128:(jj + 1) * 128], self.H3[:, kc, cols], start=(kc == 0), stop=(kc == 7))
                        sg = self.av(SG + (it % 2) * 2048, F32, 512)
                        self.act(sg, g_ps, AF.Silu)
                        self.tt(ACT3[:, jl, cols], sg, u_ps, ALU.mult)
                        it += 1
            it = 0
            if half == 0 or next_rms is None:
                order = [(c, tb) for c in range(8) for tb in range(ph.nblk)]
            else:
                order = [(c, tb) for tb in range(ph.nblk) for c in range(8)]
            for (c, tb) in order:
                cols = slice(tb * 512, (tb + 1) * 512)
                o_ps = self.ps(4 + it % 2)
                for jl in range(11):
                    self.mm(o_ps, w2[:, jl, c * 128:(c + 1) * 128], ACT3[:, jl, cols], start=(jl == 0), stop=(jl == 10))
                self.stt(self.X3[:, c, cols], o_ps, hg_vec[:, c:c + 1], self.X3[:, c, cols], ALU.mult, ALU.add)
                it += 1
                if half == 1 and next_rms is not None and c == 7:
                    if tb >= 1:
                        next_rms[1](tb - 1)
                    next_rms[0](tb)
            if half == 1 and next_rms is not None:
                next_rms[1](ph.nblk - 1)

    def rsqrt_act(self, out, in_, scale):
        self.act(out, in_, AF.Ln, bias=self.epsv, scale=scale)
        self.act(out, out, AF.Exp, scale=-0.5)

    def qk_prep_gen(self, ph, ps_in, gcol, outs, so, rope, kn_keep=None, sq_done=False):
        n = 512
        sq = self.av(so, BF, n)
        if not sq_done:
            self.act(sq, ps_in, AF.Square)
            yield
        ss = self.ps(3)
        self.mm(ss, self.bones, sq)
        sd = self.av(so + 1024, F32, n)
        self.rsqrt_act(sd, ss, 1.0)
        qn = kn_keep if kn_keep is not None else self.av(so + 3072, F32, n)
        self.stt(qn, ps_in, gcol, sd, ALU.mult, ALU.mult)
        if not ph.sample:
            for (p0, p1, dst) in outs:
                self.cp(dst, qn[p0:p1, :], eng="act")
            return
        yield
        rot = self.ps(3)
        self.mm(rot, self.Rm, qn)
        t1 = self.av(so + 5120, F32, n)
        t2 = self.av(so + 7168, F32, n)
        self.tt(t1, qn, rope[:, 0, :], ALU.mult, eng="pool")
        self.tt(t2, rot, rope[:, 1, :], ALU.mult)
        for (p0, p1, dst) in outs:
            self.tt(dst, t1[p0:p1, :], t2[p0:p1, :], ALU.add)

    def attention(self, ph, l, d, pidx, extra=None):
        T = ph.T
        A3 = self.av(R1, BF, 8, T)
        w_in = d["w_in"][l].rearrange("(kc p) n -> p kc n", p=128)
        HBSZ = 18944
        B0 = R2
        Ebase = B0 + 2 * HBSZ
        ROPE = Ebase + 3072
        SLABO = ROPE + 8192
        SO = SLABO + 6144
        CKST = SO + 7168
        SM = SO + 9216
        assert SM + 4096 <= ARENA and SM % 256 == 0
        ropedr = d["rope"].rearrange("p (a t) -> p a t", a=2)
        if not ph.sample:
            sk_st = self.av(ROPE, F32, 4, 128)
            sv_st = self.av(ROPE + 2048, F32, 4, 128)
        hbufs = []
        for hb in range(2):
            base = B0 + hb * HBSZ
            qz = self.av(base, BF, 2, T)
            kT = self.av(base + 8192, BF, ph.nk)
            vx = self.av(base + 13312, BF, 20, 136)
            self.memset(vx[:, :, 128:130], 1.0)
            self.memset(qz[64:128, 0, :], 0.0)
            self.memset(qz[0:64, 1, :], 0.0)
            hbufs.append((qz, kT, vx))
        koff = 512 if ph.sample else 0
        vch0 = 4 if ph.sample else 0
        ropectr = [0]

        def proj_gen(h):
            hb = h % 2
            qz, kT, vx = hbufs[hb]
            slab = self.av(SLABO, BF, 8, 3, 128)
            self.dma("pool", "win", [(slab[:, :, i, :], w_in[:, :, i * 1024 + h * 128:i * 1024 + (h + 1) * 128]) for i in range(3)])
            if ph.sample:
                ckst = self.av(CKST, BF, 4, 128)
                self.dma("pool", "ck", [(ckst, d["ck"][l, :, h, :].rearrange("(i p) e -> p i e", p=128))])
                self.dma("pool", f"cv{hb}", [(vx[:, 0:4, 0:128], d["cv"][l, :, h, :].rearrange("(i p) e -> p i e", p=128))])
                yield
                pb = self.ps(3, BF, 4, 128)
                for i in range(4):
                    self.tr(pb[:, i, :], ckst[:, i, :], self.identb)
                self.cp(kT[:, 0:512], pb.rearrange("p a b -> p (a b)"), eng="dve")
            for tb in range(ph.nblk):
                cols = slice(tb * 512, (tb + 1) * 512)
                rope = None
                if ph.sample:
                    ri = ropectr[0] % 2
                    ropectr[0] += 1
                    rope = self.av(ROPE + ri * 4096, F32, 2, 512)
                    self.dma("sp", f"rope{ri}", [(rope, ropedr[:, :, cols])])
                yield
                q_ps = self.ps(0)
                k_ps = self.ps(1)
                v_ps = self.ps(2, F32, 4, 128)
                for kc in range(8):
                    self.mm(q_ps, slab[:, kc, 0, :], self.H3[:, kc, cols], start=(kc == 0), stop=(kc == 7))
                    yield
                for kc in range(8):
                    self.mm(k_ps, slab[:, kc, 1, :], self.H3[:, kc, cols], start=(kc == 0), stop=(kc == 7))
                    yield
                for i in range(4):
                    tc_ = slice(tb * 512 + i * 128, tb * 512 + (i + 1) * 128)
                    for kc in range(8):
                        self.mm(v_ps[:, i, :], self.H3[:, kc, tc_], slab[:, kc, 2, :], start=(kc == 0), stop=(kc == 7))
                        if kc % 4 == 3:
                            yield
                self.cp(vx[:, vch0 + tb * 4:vch0 + tb * 4 + 4, 0:128], v_ps, eng="dve")
                if not ph.sample:
                    self.cp(sv_st, v_ps)
                    self.dma("sp", "sv", [(d["sv"][b, l, :, h, :].rearrange("(i p) e -> p i e", p=128), sv_st[:, 2 * b:2 * b + 2, :]) for b in range(2)], is_out=True)
                yield from self.qk_prep_gen(ph, q_ps, self.gqk[l][:, 0:1], [(0, 64, qz[0:64, 0, cols]), (64, 128, qz[64:128, 1, cols])], SO, rope)
                kn = None if ph.sample else self.av(SO + 3072, F32, 512)
                kdst = kT[:, koff + tb * 512:koff + (tb + 1) * 512]
                yield from self.qk_prep_gen(ph, k_ps, self.gqk[l][:, 1:2], [(0, 128, kdst)], SO, rope, kn_keep=kn)
                if not ph.sample:
                    yield
                    pk = self.ps(3, F32, 4, 128)
                    for i in range(4):
                        self.tr(pk[:, i, :], kn[:, i * 128:(i + 1) * 128], self.identf)
                    self.cp(sk_st, pk)
                    self.dma("sp", "sk", [(d["sk"][b, l, :, h, :].rearrange("(i p) e -> p i e", p=128), sk_st[:, 2 * b:2 * b + 2, :]) for b in range(2)], is_out=True)

        eit = [0]

        def attn(h, bg):
            qz, kT, vx = hbufs[h % 2]
            its = []
            for (t0, L) in ph.seqs:
                if ph.sample:
                    kch = [(kc, kc * 128) for kc in range(20)]
                else:
                    kch = [(t0 // 128 + j, t0 + j * 128) for j in range(L // 128)]
                for qb in range(L // 256):
                    for ki, (vc, kc0) in enumerate(kch):
                        its.append((t0 + qb * 256, ki, len(kch), vc, kc0))
            Es = {}
            pend = []

            def emitS(n):
                q0, ki, nk, vc, kc0 = its[n]
                e = eit[0]
                eit[0] += 1
                Sp = self.ps(4 + e % 2, F32, 2, 256)
                self.mm(Sp, kT[:, kc0:kc0 + 128], qz[:, :, q0:q0 + 256])
                E = self.av(Ebase + (e % 3) * 1024, BF, 2, 256)
                self.act(E, Sp, AF.Exp, scale=0.125)
                Es[n] = E
            emitS(0)
            if len(its) > 1:
                emitS(1)
            O1 = self.ps(6, F32, 2, 130)
            O2 = self.ps(7, F32, 2, 130)
            for n in range(len(its)):
                q0, ki, nk, vc, kc0 = its[n]
                if n + 2 < len(its):
                    emitS(n + 2)
                E = Es.pop(n)
                for mi, O in enumerate((O1, O2)):
                    for j in range(2):
                        self.mm(O[:, j, :], E[:, mi, j * 128:(j + 1) * 128], vx[:, vc, 0:130],
                                start=(ki == 0 and j == 0), stop=(ki == nk - 1), skip=True)
                if bg is not None:
                    next(bg, None)
                if extra is not None:
                    next(extra, None)
                for g in list(pend):
                    try:
                        next(g)
                    except StopIteration:
                        pend.remove(g)
                if ki == nk - 1:
                    for g in pend:
                        for _ in g:
                            pass
                    pend.clear()
                    g = epi_gen(h, q0, O1, O2)
                    next(g)
                    pend.append(g)
            for g in pend:
                for _ in g:
                    pass
            if bg is not None:
                for _ in bg:
                    pass

        def epi_gen(h, q0, O1, O2):
            Oc1 = self.av(SM + 768, F32, 2, 130)
            Oc2 = self.av(SM + 2048, F32, 2, 130)
            self.cp(Oc1, O1, eng="dve")
            self.cp(Oc2, O2, eng="dve")
            yield
            yield
            rz = self.av(SM, F32, 2, 2)
            self.rcp(rz[:, 0, :], Oc1[:, :, 128])
            self.rcp(rz[:, 1, :], Oc2[:, :, 128])
            yield
            self.ts(rz[:, 1, :], rz[:, 1, :], self.nlam[l][:, 0:1], None, ALU.mult)
            yield
            for j in range(2):
                o = Oc1[:, j, 0:128]
                self.ts(o, o, rz[:, 0, j:j + 1], None, ALU.mult)
            yield
            for j in range(2):
                o = Oc1[:, j, 0:128]
                self.stt(o, Oc2[:, j, 0:128], rz[:, 1, j:j + 1], o, ALU.mult, ALU.add)
            yield
            s2s = []
            for j in range(2):
                o = Oc1[:, j, 0:128]
                junk = self.av(SM + 3328, BF, 128)
                s2 = self.av(SM + 256 + j * 256, F32, 1)
                self.S.add("dve", (lambda junk=junk, o=o, s2=s2: (lambda e: e.scalar_tensor_tensor(junk, o, 1.0, o, ALU.mult, ALU.mult, accum_out=s2)))(), [o], [junk, s2])
                s2s.append(s2)
            yield
            yield
            for j in range(2):
                self.act(s2s[j], s2s[j], AF.Ln, bias=self.epsv, scale=1.0 / 128.0)
            yield
            for j in range(2):
                self.act(s2s[j], s2s[j], AF.Exp, scale=-0.5)
            yield
            yield
            ats = []
            for j in range(2):
                at = self.av(SM + 3584 + j * 256, BF, 128)
                self.stt(at, Oc1[:, j, 0:128], s2s[j], self.gsub[l], ALU.mult, ALU.mult)
                ats.append(at)
            yield
            yield
            pt = self.ps(3, BF, 2, 128)
            for j in range(2):
                self.tr(pt[:, j, :], ats[j], self.identb)
            self.cp(A3[:, h, q0:q0 + 256], pt.rearrange("p a b -> p (a b)"))

        for _ in proj_gen(0):
            pass
        for h in range(NH):
            bg = proj_gen(h + 1) if h + 1 < NH else None
            attn(h, bg)
        if extra is not None:
            for _ in extra:
                pass

    def mstage(self, ph, l, d, which, next_rms=None):
        T = ph.T
        w_gate = d["w_gate"][l].rearrange("(kc p) n -> p kc n", p=128)
        w_out = d["w_out"][l].rearrange("(kc p) n -> p kc n", p=128)
        g2 = self.modT[l][:, 40:48, ph.ci]
        wo = self.av(R3, BF, 8, 1024)
        SLB = R3 + 16384
        SG = R3 + 28672
        if which == 1:
            A3 = self.av(R1, BF, 8, T)
            M3 = self.av(R2, BF, 8, T)
            w_pa = d["w_pa"][l].rearrange("(kc p) n -> p kc n", p=128)
        else:
            P3 = self.av(R2, BF, 4, T)
            F3 = self.av(R2 + 16384, BF, 4, T)
            M3 = self.av(R1, BF, 8, T)
            w_pp = d["w_pp"][l].rearrange("(kc p) n -> p kc n", p=128)
            w_pf = d["w_pf"][l].rearrange("(kc p) n -> p kc n", p=128)
        it = 0
        for c in range(8):
            sidx = c % 2
            cs = slice(c * 128, (c + 1) * 128)
            if which == 1:
                wg = self.av(SLB + sidx * 6144, BF, 8, 128)
                wp = self.av(SLB + sidx * 6144 + 2048, BF, 8, 128)
                self.dma("pool", f"ms{sidx}", [(wg, w_gate[:, :, c * 128:(c + 1) * 128]), (wp, w_pa[:, :, cs])])
            else:
                wgp = self.av(SLB + sidx * 6144, BF, 8, 128)
                wgf = self.av(SLB + sidx * 6144 + 2048, BF, 8, 128)
                wpp = self.av(SLB + sidx * 6144 + 4096, BF, 4, 128)
                wpf = self.av(SLB + sidx * 6144 + 5120, BF, 4, 128)
                self.dma("pool", f"ms{sidx}", [(wgp, w_gate[:, :, 1024 + c * 128:1024 + (c + 1) * 128]),
                                               (wgf, w_gate[:, :, 2048 + c * 128:2048 + (c + 1) * 128]),
                                               (wpp, w_pp[:, :, cs]), (wpf, w_pf[:, :, cs])])
            if c == 1:
                self.dma("pool", "wout", [(wo, w_out)])
            for tb in range(ph.nblk):
                cols = slice(tb * 512, (tb + 1) * 512)
                if which == 1:
                    g_ps = self.ps(it % 2)
                    a_ps = self.ps(2 + it % 2)
                    for kc in range(8):
                        self.mm(g_ps, wg[:, kc, :], self.H3[:, kc, cols], start=(kc == 0), stop=(kc == 7))
                    for kc in range(8):
                        self.mm(a_ps, wp[:, kc, :], A3[:, kc, cols], start=(kc == 0), stop=(kc == 7))
                    sg = self.av(SG + (it % 2) * 2048, F32, 512)
                    self.act(sg, g_ps, AF.Sigmoid)
                    self.tt(M3[:, c, cols], sg, a_ps, ALU.mult)
                else:
                    gp_ps = self.ps(it % 2)
                    gf_ps = self.ps(2 + it % 2)
                    p_ps = self.ps(4)
                    f_ps = self.ps(5)
                    for kc in range(8):
                        self.mm(gp_ps, wgp[:, kc, :], self.H3[:, kc, cols], start=(kc == 0), stop=(kc == 7))
                    for kc in range(8):
                        self.mm(gf_ps, wgf[:, kc, :], self.H3[:, kc, cols], start=(kc == 0), stop=(kc == 7))
                    for kc in range(4):
                        self.mm(p_ps, wpp[:, kc, :], P3[:, kc, cols], start=(kc == 0), stop=(kc == 3))
                    for kc in range(4):
                        self.mm(f_ps, wpf[:, kc, :], F3[:, kc, cols], start=(kc == 0), stop=(kc == 3))
                    sgp = self.av(SG + (it % 2) * 2048, F32, 512)
                    sgf = self.av(SG + 4096 + (it % 2) * 2048, F32, 512)
                    tmp = self.av(SG + 8192, F32, 512)
                    self.act(sgp, gp_ps, AF.Sigmoid)
                    self.act(sgf, gf_ps, AF.Sigmoid)
                    self.tt(tmp, sgp, p_ps, ALU.mult)
                    self.tt(sgf, sgf, f_ps, ALU.mult)
                    self.tt(M3[:, c, cols], tmp, sgf, ALU.add)
                it += 1
        it = 0
        if next_rms is None:
            order = [(c, tb) for c in range(8) for tb in range(ph.nblk)]
        else:
            order = [(c, tb) for tb in range(ph.nblk) for c in range(8)]
        for (c, tb) in order:
            cols = slice(tb * 512, (tb + 1) * 512)
            o_ps = self.ps(6 + it % 2)
            for kc in range(8):
                self.mm(o_ps, wo[:, kc, c * 128:(c + 1) * 128], M3[:, kc, cols], start=(kc == 0), stop=(kc == 7))
            self.stt(self.X3[:, c, cols], o_ps, g2[:, c:c + 1], self.X3[:, c, cols], ALU.mult, ALU.add)
            it += 1
            if next_rms is not None and c == 7:
                if tb >= 1:
                    next_rms[1](tb - 1)
                next_rms[0](tb)
        if next_rms is not None:
            next_rms[1](ph.nblk - 1)

    def fourier(self, ph, l, d):
        T = ph.T
        w_in = d["w_in"][l].rearrange("(kc p) n -> p kc n", p=128)
        U3 = self.av(R2, BF, 4, T)
        F3 = self.av(R2 + 16384, BF, 4, T)
        nch = T // 128
        AT = self.av(R1, BF, nch, 4, 2, 128)
        slab = self.av(R3 + 32768, BF, 8, 512)
        self.dma("pool", "ufs", [(slab, w_in[:, :, 3584:4096])])
        it = 0
        for g in range(4):
            for tb in range(ph.nblk):
                cols = slice(tb * 512, (tb + 1) * 512)
                u_ps = self.ps(it % 2)
                for kc in range(8):
                    self.mm(u_ps, slab[:, kc, g * 128:(g + 1) * 128], self.H3[:, kc, cols], start=(kc == 0), stop=(kc == 7))
                self.cp(U3[:, g, cols], u_ps, eng=("act" if it % 2 else "dve"))
                it += 1
        for i in range(nch):
            b0 = 2 + (i % 2) * 2
            for g in range(4):
                pa = self.ps(b0 + g // 2, F32, 2, 256)
                self.mm(pa[:, g % 2, :], U3[:, g, i * 128:(i + 1) * 128], self.CS)
            self.cp(AT[:, i, 0:2, :, :].rearrange("p a b c -> p (a b c)"), self.ps(b0), eng="dve")
            self.cp(AT[:, i, 2:4, :, :].rearrange("p a b c -> p (a b c)"), self.ps(b0 + 1), eng="act")
        it = 0
        tabn = 0
        for (t0, L) in ph.seqs:
            ni = L // 128
            i0 = t0 // 128
            for tpb in range(L // 256):
                tab = self.av(R3 + (tabn % 2) * 16384, BF, ni, 2, 256)
                if ph.sample:
                    src = d["dftS"][tpb]
                else:
                    src = d["dftP"]
                self.dma("sp", f"tab{tabn % 2}", [(tab.rearrange("p a b c -> p (a b c)"), src)])
                tabn += 1
                for g in range(4):
                    f_ps = self.ps(6 + it % 2)[:, 0:256]
                    n = 0
                    for ii in range(ni):
                        for cs in range(2):
                            self.mm(f_ps, AT[:, i0 + ii, g, cs, :], tab[:, ii, cs, :], start=(n == 0), stop=(n == 2 * ni - 1))
                            n += 1
                    self.cp(F3[:, g, t0 + tpb * 256:t0 + (tpb + 1) * 256], f_ps, eng=("act" if it % 2 else "dve"))
                    it += 1

    def poolmix(self, ph, l, d):
        T = ph.T
        w_in = d["w_in"][l].rearrange("(kc p) n -> p kc n", p=128)
        P3 = self.av(R2, BF, 4, T)
        nseq = len(ph.seqs)
        L = ph.seqs[0][1]
        Lp = L + 16
        slab = self.av(R3, BF, 8, 512)
        self.dma("pool", "ups", [(slab, w_in[:, :, 3072:3584])])
        wpl = self.av(R3 + 8192, BF, 4, 128)
        self.dma("pool", "wpool", [(wpl, d["w_pool"][l].rearrange("g c e -> c g e"))])
        bufsz = ((nseq * Lp * 4 + 255) // 256) * 256
        U = self.av(R1, F32, nseq, Lp)
        Q = [self.av(R1 + bufsz * (1 + i), F32, nseq, Lp) for i in range(2)]
        Dg = self.av(R1 + 3 * bufsz, BF, nseq, L)
        assert 3 * bufsz + T * 2 <= 32768
        it = 0
        for g in range(4):
            w = POOLW[g]
            lv = g + 1
            self.memset(U[:, :, 0:8], 0.0)
            self.memset(U[:, :, L + 8:L + 16], 0.0)
            for tb in range(ph.nblk):
                cols = slice(tb * 512, (tb + 1) * 512)
                u_ps = self.ps(it % 2)
                for kc in range(8):
                    self.mm(u_ps, slab[:, kc, g * 128:(g + 1) * 128], self.H3[:, kc, cols], start=(kc == 0), stop=(kc == 7))
                if nseq == 1:
                    self.cp(U[:, 0, 8 + tb * 512:8 + (tb + 1) * 512], u_ps, eng=("act" if it % 2 else "dve"))
                else:
                    self.cp(U[:, :, 8:8 + L], u_ps.rearrange("p (s t) -> p s t", s=nseq), eng="dve")
                it += 1
            src = U
            for k in range(1, lv + 1):
                sh = 1 << (k - 1)
                dst = Q[(k - 1) % 2]
                self.tt(dst[:, :, 0:Lp - sh], src[:, :, 0:Lp - sh], src[:, :, sh:Lp], ALU.add, eng="dve")
                src = dst
            hw = w // 2
            Ssh = src[:, :, 8 - hw:8 - hw + L]
            Uc = U[:, :, 8:8 + L]
            tmpD = Q[lv % 2][:, :, 0:L]
            self.stt(tmpD, Ssh, 1.0 / w, Uc, ALU.mult, ALU.subtract)
            tb0 = {2: 0, 4: 2, 8: 6, 16: 14}[w]
            fl = self.ptab[:, tb0:tb0 + hw]
            fr = self.ptab[:, tb0 + hw:tb0 + 2 * hw]
            for s in range(nseq):
                bl = self.av(P_MISC + 128, F32, 8)[:, 0:hw]
                self.tt(bl, Ssh[:, s, 0:hw], fl, ALU.mult)
                self.tt(tmpD[:, s, 0:hw], bl, Uc[:, s, 0:hw], ALU.subtract)
                br = self.av(P_MISC + 192, F32, 8)[:, 0:hw]
                self.tt(br, Ssh[:, s, L - hw:L], fr, ALU.mult)
                self.tt(tmpD[:, s, L - hw:L], br, Uc[:, s, L - hw:L], ALU.subtract)
            self.cp(Dg, tmpD, eng="act")
            Dflat = Dg.rearrange("p s t -> p (s t)")
            for tb in range(ph.nblk):
                cols = slice(tb * 512, (tb + 1) * 512)
                p_ps = self.ps(2 + tb % 2)
                self.mm(p_ps, wpl[:, g, :], Dflat[:, cols])
                self.act(P3[:, g, cols], p_ps, AF.Identity, scale=self.psT[l][:, g:g + 1])

    def layer(self, ph, l, d, pidx, first=True, last=True, yd=None):
        cfg = self.cfg
        RSO = R3 + 22528 + 4096
        full = all(cfg.get(k, True) for k in ("ffn", "mixer", "attn", "pf", "ffn2"))
        if first or not full:
            self.derive(ph, l)
        dv = self.dvs[l % 2]
        if not full:
            if cfg.get("ffn", True):
                self.rms_mod(ph, dv[:, 0, :], self.shift(l, ph, 0), RSO)
                self.ffn(ph, l, 0, d, dv[:, 3, :])
            if cfg.get("mixer", True):
                self.rms_mod(ph, dv[:, 1, :], self.shift(l, ph, 1), RSO)
                if cfg.get("attn", True):
                    self.attention(ph, l, d, pidx)
                    self.mstage(ph, l, d, 1)
                if cfg.get("pf", True):
                    self.fourier(ph, l, d)
                    self.poolmix(ph, l, d)
                    self.mstage(ph, l, d, 2)
            if cfg.get("ffn2", True):
                self.rms_mod(ph, dv[:, 2, :], self.shift(l, ph, 2), RSO)
                self.ffn(ph, l, 1, d, dv[:, 4, :])
            return
        if not first:
            self.derive(ph, l)
        self.rms_mod(ph, dv[:, 0, :], self.shift(l, ph, 0), RSO)
        self.ffn(ph, l, 0, d, dv[:, 3, :])
        self.rms_mod(ph, dv[:, 1, :], self.shift(l, ph, 1), RSO)
        extra = None
        if self.defer_mod1 and pidx == 0 and l == 0:
            extra = self.stage_mod_gen(d, 1, R1 + 16384, R1 + 24576, 3)
        self.attention(ph, l, d, pidx, extra=extra)
        self.mstage(ph, l, d, 1)
        self.fourier(ph, l, d)
        self.poolmix(ph, l, d)
        self.mstage(ph, l, d, 2)
        self.rms_mod(ph, dv[:, 2, :], self.shift(l, ph, 2), RSO)
        hook = None
        if last and yd is not None and ph.nblk > 1:
            hook = (lambda tb: None,
                    lambda tb: self.store_x(ph, yd, tiles=range(4 * tb, 4 * tb + 4), stoff=R3 + 26624))
            self.tail_stored = True
        self.ffn(ph, l, 1, d, dv[:, 4, :], next_rms=hook)


def build(cfg=None):
    cfg = cfg or {}
    nc = bass.Bass("TRN2", target_bir_lowering=False)
    d = {}

    def inp(name, shape, dt=F32):
        d[name] = nc.dram_tensor(name, list(shape), dt, kind="ExternalInput").ap()

    def outp(name, shape):
        d[name] = nc.dram_tensor(name, list(shape), F32, kind="ExternalOutput").ap()

    inp("xs", [2048, D]); inp("xp", [512, D])
    inp("ck", [2, 512, 8, 128]); inp("cv", [2, 512, 8, 128])
    inp("cond", [16, 128])
    inp("w_ada", [2, D, 9 * D]); inp("b_ada", [2, 72, 128]); inp("norm_g", [2, 24, 128])
    inp("ffn_w13", [2, 2, D, 2 * DFF]); inp("ffn_w2", [2, 2, DFF, D])
    inp("w_in", [2, D, 4096]); inp("q_norm_g", [2, 64]); inp("k_norm_g", [2, 64])
    inp("lam_qk", [2, 256]); inp("subln_g", [2, 128]); inp("w_pool", [2, 4, 128, 128])
    inp("pool_scale", [2, 4, 128]); inp("w_gate", [2, D, 3 * D]); inp("w_pa", [2, D, D])
    inp("w_pp", [2, 512, D]); inp("w_pf", [2, 512, D]); inp("w_out", [2, D, D])
    inp("cb", [128, 640], BF); inp("cf", [128, 288]); inp("rope", [128, 2 * 2048])
    inp("dftS", [8, 128, 16 * 2 * 256], BF); inp("dftP", [128, 2 * 2 * 256], BF)
    outp("ys", [2048, D]); outp("yp", [512, D])
    outp("sk", [2, 2, 256, 8, 128]); outp("sv", [2, 2, 256, 8, 128])

    with ExitStack() as es:
        k = K(nc, es, cfg)
        k.slabctr = 0
        k.setup_consts(d)
        k.epsv = k.av(P_MISC + 256, F32, 1)
        k.memset(k.epsv, EPS)
        k.stage_params(d)
        k.stage_mod(d, 0)
        full = all(cfg.get(kk, True) for kk in ("ffn", "mixer", "attn", "pf", "ffn2"))
        k.defer_mod1 = bool(cfg.get("P", True) and full and cfg.get("layers", 2) > 1)
        if cfg.get("layers", 2) > 1 and not k.defer_mod1:
            k.stage_mod(d, 1)
        phases = []
        if cfg.get("P", True):
            phases.append((Phase("P", 512, [(0, 256), (256, 256)], 0, False), d["xp"], d["yp"]))
        if cfg.get("S", True):
            phases.append((Phase("S", 2048, [(0, 2048)], 1, True), d["xs"], d["ys"]))
        for pidx, (ph, xd, yd) in enumerate(phases):
            k.views(ph)
            k.load_x(ph, xd)
            nl = cfg.get("layers", 2)
            for l in range(nl):
                if pidx == 0 and l == 0 and nl > 1:
                    pass
                k.tail_stored = False
                k.layer(ph, l, d, pidx, first=(l == 0), last=(l == nl - 1), yd=yd)
            if not (nl > 0 and k.tail_stored):
                k.store_x(ph, yd)
        S = k.S
        S.add("sp", lambda e: None, [], [], extra_deps=list(k.out_ops))
        keys = S.finalize()
        sems = {kk: es.enter_context(nc.semaphore(f"s{i}")) for i, kk in enumerate(keys)}
        with nc.Block() as block:
            @block.tensor
            def _(e):
                S.emit("pe", e, sems)

            @block.scalar
            def _(e):
                S.emit("act", e, sems)

            @block.vector
            def _(e):
                S.emit("dve", e, sems)

            @block.gpsimd
            def _(e):
                S.emit("pool", e, sems)

            @block.sync
            def _(e):
                S.emit("sp", e, sems)
        k.nops = len(S.ops)
        k.nsem = len(keys)
    return nc, k


def host_consts():
    bf = ml_dtypes.bfloat16
    cb = np.zeros((128, 640), np.float32)
    cb[:, 0:128] = np.eye(128)
    cb[:, 128:256] = 1.0 / 1024.0
    blk = np.zeros((128, 128))
    blk[:64, :64] = 1.0 / 64
    blk[64:, 64:] = 1.0 / 64
    cb[:, 256:384] = blk
    cc = np.arange(128)[:, None] * np.arange(128)[None, :]
    ang = 2 * np.pi * (cc % 128) / 128.0
    cb[:, 384:512] = np.cos(ang) / np.sqrt(128.0)
    cb[:, 512:640] = np.sin(ang) / np.sqrt(128.0)
    cf = np.zeros((128, 288), np.float32)
    cf[:, 0:128] = np.eye(128)
    R = np.zeros((128, 128), np.float32)
    for m in range(128):
        if (m % 32) < 16:
            R[m + 16, m] = -1.0
        else:
            R[m - 16, m] = 1.0
    cf[:, 128:256] = R
    col = 256
    for w in POOLW:
        hw = w // 2
        for t in range(hw):
            cf[:, col + t] = 1.0 / (t + hw)
        for i in range(hw):
            cf[:, col + hw + i] = 1.0 / (2 * hw - i)
        col += 2 * hw
    p = np.arange(128)
    dd = p % 64
    axis = dd // 32
    freq = dd % 16
    inv = (10000.0 ** (-np.arange(16, dtype=np.float32) / np.float32(16))).astype(np.float32)
    t = np.arange(2048)
    row = (t // GRID_W).astype(np.float32)
    colp = (t % GRID_W).astype(np.float32)
    pos = np.where(axis[:, None] == 0, row[None, :], colp[None, :]).astype(np.float32)
    angr = (pos * inv[freq][:, None]).astype(np.float32)
    rope = np.stack([np.cos(angr), np.sin(angr)], axis=1).astype(np.float32).reshape(128, 4096)

    def dft(L):
        tt = np.arange(L)[:, None].astype(np.int64)
        kk = np.arange(L)[None, :].astype(np.int64)
        a = 2 * np.pi * ((tt * kk) % L) / L
        return np.cos(a) / np.sqrt(L), -np.sin(a) / np.sqrt(L)
    c, s = dft(2048)
    tab = np.stack([c, s], axis=0)
    tab = tab.reshape(2, 16, 128, 8, 256)
    dftS = np.ascontiguousarray(tab.transpose(3, 2, 1, 0, 4)).reshape(8, 128, 16 * 2 * 256)
    c, s = dft(256)
    tab = np.stack([c, s], axis=0).reshape(2, 2, 128, 256)
    dftP = np.ascontiguousarray(tab.transpose(2, 1, 0, 3)).reshape(128, 2 * 2 * 256)
    return dict(cb=cb.astype(bf), cf=cf, rope=rope, dftS=dftS.astype(bf), dftP=dftP.astype(bf))


_CACHE = {}


def make_in_maps(inputs, ncores=8):
    f = lambda a: np.ascontiguousarray(np.asarray(a, dtype=np.float32))
    hc = host_consts()
    shared = dict(
        w_ada=f(inputs["w_ada"]), b_ada=f(inputs["b_ada"]).reshape(2, 72, 128), norm_g=f(inputs["norm_g"]).reshape(2, 24, 128),
        ffn_w13=f(inputs["ffn_w13"]), ffn_w2=f(inputs["ffn_w2"]), w_in=f(inputs["w_in"]),
        q_norm_g=f(inputs["q_norm_g"]), k_norm_g=f(inputs["k_norm_g"]), lam_qk=f(inputs["lam_qk"]).reshape(2, 256),
        subln_g=f(inputs["subln_g"]), w_pool=f(inputs["w_pool"]), pool_scale=f(inputs["pool_scale"]).reshape(2, 4, 128),
        w_gate=f(inputs["w_gate"]), w_pa=f(inputs["w_pa"]), w_pp=f(inputs["w_pp"]), w_pf=f(inputs["w_pf"]),
        w_out=f(inputs["w_out"]), **hc)
    xp = f(inputs["x_prompt"]); xs = f(inputs["x_sample"])
    ck = f(inputs["cache_k"]); cv = f(inputs["cache_v"])
    c = f(inputs["c"]); cc = f(inputs["c_ctx"])
    maps = []
    for i in range(ncores):
        m = dict(shared)
        m["xs"] = xs[i]
        m["xp"] = xp[2 * i:2 * i + 2].reshape(512, D)
        m["ck"] = ck[i]
        m["cv"] = cv[i]
        m["cond"] = np.ascontiguousarray(np.concatenate([cc.reshape(8, 128), c[i].reshape(8, 128)], axis=0))
        maps.append(m)
    return maps


def kernel(**inputs):
    if "nc" not in _CACHE:
        _CACHE["nc"] = build()[0]
    nc = _CACHE["nc"]
    maps = make_in_maps(inputs)
    res = run_bass_kernel_spmd(nc, maps, core_ids=list(range(8)))
    r = res.results
    y_prompt = np.concatenate([r[i]["yp"].reshape(2, 256, D) for i in range(8)], axis=0).astype(np.float32)
    y_sample = np.stack([r[i]["ys"] for i in range(8)], axis=0).astype(np.float32)
    state_k = np.concatenate([r[i]["sk"] for i in range(8)], axis=0).astype(np.float32)
    state_v = np.concatenate([r[i]["sv"] for i in range(8)], axis=0).astype(np.float32)
    return (y_prompt, y_sample, state_k, state_v)
```

```python
import math
from contextlib import ExitStack
import numpy as np
import ml_dtypes
import concourse.bass as bass
import concourse.mybir as mybir
from concourse.bass_utils import run_bass_kernel_spmd

F32 = mybir.dt.float32
BF = mybir.dt.bfloat16
AF = mybir.ActivationFunctionType
ALU = mybir.AluOpType
AX = mybir.AxisListType

D = 1024
NH = 8
DFF = 2816
EPS = 1e-6
GRID_W = 64
POOLW = (2, 4, 8, 16)


def dsz(dt):
    return 4 if dt == F32 else 2


class Op:
    __slots__ = ("eng", "fn", "deps", "dma_sem", "ndma", "count", "sig", "sigsem", "sigcnt", "waits")

    def __init__(self, eng, fn, dma_sem, ndma):
        self.eng = eng
        self.fn = fn
        self.deps = set()
        self.dma_sem = dma_sem
        self.ndma = ndma
        self.count = 0
        self.sig = False
        self.sigsem = None
        self.sigcnt = 0
        self.waits = None


class Sched:
    EPOCH = 8000

    def __init__(self, tracked):
        self.ops = []
        self.cells = {}
        self.tracked = tracked
        self.dma_counts = {}
        self.last_dma_on_sem = {}

    def cells_of(self, ap):
        nm = ap.tensor.name
        info = self.tracked.get(nm)
        if info is None:
            return ()
        gran, rowbytes = info
        d = dsz(ap.dtype)
        pat = ap.ap
        row_elems = rowbytes // d
        off = int(ap.offset) % row_elems
        dims = [(int(s), int(n)) for (s, n) in pat[1:]]
        if not dims:
            dims = [(1, 1)]
        inner_s, inner_n = dims[-1]
        outer = dims[:-1]
        nouter = 1
        for s, n in outer:
            nouter *= n
        if inner_s in (0, 1):
            ilen = inner_n if inner_s == 1 else 1
        else:
            ilen = inner_s * (inner_n - 1) + 1
        runs = []
        if nouter <= 256:
            offs = [off]
            for s, n in outer:
                offs = [o + s * i for o in offs for i in range(n)]
            for o in offs:
                runs.append((o, o + ilen))
        else:
            hi = off + ilen
            for s, n in outer:
                hi += s * (n - 1)
            runs.append((off, hi))
        out = set()
        for lo, hi in runs:
            b0 = (lo * d) // gran
            b1 = (hi * d - 1) // gran
            for b in range(b0, b1 + 1):
                out.add((nm, b))
        return out

    def add(self, eng, fn, reads=(), writes=(), dma_sem=None, ndma=0, extra_deps=()):
        idx = len(self.ops)
        op = Op(eng, fn, dma_sem, ndma)
        deps = set(extra_deps)
        rc = set()
        for a in reads:
            rc |= set(self.cells_of(a))
        wc = set()
        for a in writes:
            wc |= set(self.cells_of(a))
        cells = self.cells
        rkey = eng if dma_sem is None else ("dma", dma_sem)
        for c in rc:
            st = cells.get(c)
            if st is not None:
                if st[0] is not None:
                    deps.add(st[0])
                if c[0].startswith("ps"):
                    for rk, ri in st[1].items():
                        if rk != rkey:
                            deps.add(ri)
        for c in wc:
            st = cells.get(c)
            if st is not None:
                if st[0] is not None:
                    deps.add(st[0])
                deps.update(st[1].values())
        rkey = eng if dma_sem is None else ("dma", dma_sem)
        for c in rc:
            if c in wc:
                continue
            st = cells.get(c)
            if st is None:
                cells[c] = [None, {rkey: idx}]
            else:
                st[1][rkey] = idx
        for c in wc:
            cells[c] = [idx, {}]
        if dma_sem is not None:
            prev = self.last_dma_on_sem.get(dma_sem)
            if prev is not None:
                deps.add(prev)
            self.last_dma_on_sem[dma_sem] = idx
            cnt = self.dma_counts.get(dma_sem, 0) + 16 * ndma
            self.dma_counts[dma_sem] = cnt
            op.count = cnt
        deps.discard(idx)
        op.deps = deps
        self.ops.append(op)
        return idx

    def finalize(self):
        ops = self.ops
        for op in ops:
            nd = set()
            for d in op.deps:
                dop = ops[d]
                if dop.dma_sem is None and dop.eng == "pe" and op.eng == "pe" and op.dma_sem is None:
                    continue
                nd.add(d)
                if dop.dma_sem is None:
                    dop.sig = True
            op.deps = nd
        cnt = {}
        for op in ops:
            if op.dma_sem is None and op.sig:
                c = cnt.get(op.eng, 0)
                ep, within = divmod(c, self.EPOCH)
                op.sigsem = ("eng", op.eng, ep)
                op.sigcnt = within + 1
                cnt[op.eng] = c + 1
        semkeys = set()
        for op in ops:
            w = {}
            for d in op.deps:
                dop = ops[d]
                if dop.dma_sem is not None:
                    k = ("dma", dop.dma_sem)
                    v = dop.count
                else:
                    k = dop.sigsem
                    v = dop.sigcnt
                if w.get(k, 0) < v:
                    w[k] = v
                semkeys.add(k)
            op.waits = w
            if op.dma_sem is not None:
                semkeys.add(("dma", op.dma_sem))
            elif op.sig:
                semkeys.add(op.sigsem)
        return sorted(semkeys, key=str)

    def emit(self, eng, e, sems):
        waited = {}
        for op in self.ops:
            if op.eng != eng:
                continue
            for k, v in op.waits.items():
                if waited.get(k, 0) < v:
                    e.wait_ge(sems[k], v)
                    waited[k] = v
            if op.dma_sem is not None:
                op.fn(e, sems[("dma", op.dma_sem)])
            else:
                ins = op.fn(e)
                if op.sig:
                    ins.then_inc(sems[op.sigsem], 1)


CB_OFF = 0
CF_OFF = 1280
PAR_OFF = 2560
X_OFF = 7680
H_OFF = X_OFF + 65536
R1 = H_OFF + 32768
R2 = R1 + 32768
R3 = R2 + 32768
ARENA = 212736
R3SZ = ARENA - R3
assert R3SZ >= 40960

P_MODT = PAR_OFF
P_BT = P_MODT + 1152
P_GT = P_BT + 576
P_CT = P_GT + 192
P_SCT = P_CT + 64
P_GQK = P_SCT + 64
P_PST = P_GQK + 64
P_NLAM = P_PST + 64
P_GSUB = P_NLAM + 64
P_DV = P_GSUB + 1024
P_MISC = P_DV + 256
assert P_MISC + 512 <= X_OFF


class Phase:
    def __init__(self, name, T, seqs, ci, sample):
        self.name = name
        self.T = T
        self.nblk = T // 512
        self.seqs = seqs
        self.ci = ci
        self.sample = sample
        self.nk = T + 512 if sample else T


class K:
    def __init__(self, nc, es, cfg):
        self.nc = nc
        self.cfg = cfg
        self.arena = es.enter_context(nc.sbuf_tensor("arena", [128, ARENA // 2], BF))
        self.banks = [es.enter_context(nc.psum_tensor(f"ps{i}", [128, 512], F32)) for i in range(8)]
        tracked = {"arena": (256, ARENA)}
        for i in range(8):
            tracked[f"ps{i}"] = (2048, 2048)
        self.S = Sched(tracked)
        self.out_ops = []

    def av(self, off, dt, *dims, p0=0, p1=128):
        n = 1
        for x in dims:
            n *= int(x)
        nb = n * dsz(dt)
        assert off % 4 == 0 and off + nb <= ARENA, (off, nb)
        ap = self.arena[p0:p1, off // 2:(off + nb) // 2]
        if dt != BF:
            ap = ap.bitcast(dt)
        if len(dims) > 1:
            names = " ".join(f"d{i}" for i in range(len(dims)))
            ap = ap.rearrange(f"p ({names}) -> p {names}", **{f"d{i}": int(dims[i]) for i in range(len(dims))})
        return ap

    def ps(self, bank, dt=F32, *dims):
        ap = self.banks[bank][:]
        if dt != F32:
            ap = ap.bitcast(dt)
        tot = 512 if dt == F32 else 1024
        n = 1
        for x in dims:
            n *= int(x)
        if not dims:
            return ap
        ap = ap[:, 0:n]
        if len(dims) > 1:
            names = " ".join(f"d{i}" for i in range(len(dims)))
            ap = ap.rearrange(f"p ({names}) -> p {names}", **{f"d{i}": int(dims[i]) for i in range(len(dims))})
        return ap

    def mm(self, out, lhsT, rhs, start=True, stop=True, skip=False):
        if skip:
            fn = lambda e: e.matmul(out, lhsT, rhs, start=start, stop=stop, skip_group_check=True)
        else:
            fn = lambda e: e.matmul(out, lhsT, rhs, start=start, stop=stop)
        self.S.add("pe", fn, [lhsT, rhs], [out])

    def tr(self, out, in_, ident):
        self.S.add("pe", lambda e: e.transpose(out, in_, ident), [in_, ident], [out])

    def act(self, out, in_, func, bias=None, scale=None, accum=None):
        kw = {}
        reads = [in_]
        if bias is not None:
            kw["bias"] = bias
            if not isinstance(bias, (int, float)):
                reads.append(bias)
        if scale is not None:
            kw["scale"] = scale
            if not isinstance(scale, (int, float)):
                reads.append(scale)
        writes = [out]
        if accum is not None:
            kw["accum_out"] = accum
            writes.append(accum)
        self.S.add("act", lambda e: e.activation(out, in_, func, **kw), reads, writes)

    def tt(self, out, a, b, op, eng="dve"):
        self.S.add(eng, lambda e: e.tensor_tensor(out, a, b, op), [a, b], [out])

    def ts(self, out, a, s1, s2, op0, op1=None, eng="dve"):
        reads = [a]
        if not isinstance(s1, (int, float)):
            reads.append(s1)
        if s2 is not None and not isinstance(s2, (int, float)):
            reads.append(s2)
        if op1 is None:
            fn = lambda e: e.tensor_scalar(out, a, s1, None, op0)
        else:
            fn = lambda e: e.tensor_scalar(out, a, s1, s2, op0, op1)
        self.S.add(eng, fn, reads, [out])

    def stt(self, out, a, s, b, op0, op1, eng="dve"):
        reads = [a, b]
        if not isinstance(s, (int, float)):
            reads.append(s)
        self.S.add(eng, lambda e: e.scalar_tensor_tensor(out, a, s, b, op0, op1), reads, [out])

    def cp(self, out, in_, eng="dve"):
        if eng == "act":
            self.S.add("act", lambda e: e.activation(out, in_, AF.Copy), [in_], [out])
        else:
            self.S.add(eng, lambda e: e.tensor_copy(out, in_), [in_], [out])

    def rcp(self, out, in_):
        self.S.add("dve", lambda e: e.reciprocal(out, in_), [in_], [out])

    def rsum(self, out, in_):
        self.S.add("dve", lambda e: e.reduce_sum(out, in_, AX.X), [in_], [out])

    def memset(self, out, val, eng="pool"):
        self.S.add(eng, lambda e: e.memset(out, val), [], [out])

    def dma(self, eng, sem, pairs, is_out=False):
        pairs = list(pairs)

        def fn(e, s):
            for o, i in pairs:
                e.dma_start(out=o, in_=i).then_inc(s, 16)
        idx = self.S.add(eng, fn, [i for o, i in pairs], [o for o, i in pairs], dma_sem=sem, ndma=len(pairs))
        if is_out:
            self.out_ops.append(idx)
        return idx

    def setup_consts(self, d):
        self.cb = self.av(CB_OFF, BF, 640)
        self.cf = self.av(CF_OFF, F32, 288)
        self.dma("sp", "c0", [(self.cb, d["cb"]), (self.cf, d["cf"])])
        self.identb = self.cb[:, 0:128]
        self.onesD = self.cb[:, 128:256]
        self.bones = self.cb[:, 256:384]
        self.CS = self.cb[:, 384:640]
        self.identf = self.cf[:, 0:128]
        self.Rm = self.cf[:, 128:256]
        self.ptab = self.cf[:, 256:288]

    def load_T(self, dst, src_rows, r, stage_off, bank):
        st = self.av(stage_off, F32, 128, p1=r)
        self.dma("sp", "ldT", [(st, src_rows)])
        pst = self.ps(bank)[:, 0:r]
        self.tr(pst, st, self.identf[0:r, 0:r])
        self.cp(dst, pst)

    def stage_params(self, d):
        S0 = R3 + 36864
        self.modT = [self.av(P_MODT + l * 576, F32, 72, 2) for l in range(2)]
        self.bT = [self.av(P_BT + l * 288, F32, 72) for l in range(2)]
        self.gT = [self.av(P_GT + l * 96, F32, 24) for l in range(2)]
        self.cT = self.av(P_CT, F32, 16)
        self.scT = self.av(P_SCT, BF, 2, 8)
        self.gqk = [self.av(P_GQK + l * 8, F32, 2) for l in range(2)]
        self.psT = [self.av(P_PST + l * 16, F32, 4) for l in range(2)]
        self.nlam = [self.av(P_NLAM + l * 4, F32, 1) for l in range(2)]
        self.gsub = [self.av(P_GSUB + l * 512, F32, 128) for l in range(2)]
        self.dvs = [self.av(P_DV, F32, 5, 8), self.av(P_MISC + 320, F32, 5, 8)]
        self.load_T(self.cT, d["cond"], 16, S0, 0)
        self.act(self.scT.rearrange("p a b -> p (a b)"), self.cT, AF.Silu)
        for l in range(2):
            self.load_T(self.bT[l], d["b_ada"][l], 72, S0, 0)
            self.load_T(self.gT[l], d["norm_g"][l], 24, S0, 0)
            self.load_T(self.psT[l], d["pool_scale"][l], 4, S0, 0)
            qg = d["q_norm_g"][l:l + 1, :].rearrange("o d -> d o")
            kg = d["k_norm_g"][l:l + 1, :].rearrange("o d -> d o")
            self.dma("sp", "gqk", [(self.gqk[l][0:64, 0:1], qg), (self.gqk[l][64:128, 0:1], qg),
                                   (self.gqk[l][0:64, 1:2], kg), (self.gqk[l][64:128, 1:2], kg)])
            lq = self.av(S0 + 1024, F32, 4, 64)
            self.dma("sp", "lq", [(lq.rearrange("p a b -> p (a b)"), d["lam_qk"][l:l + 1, :].partition_broadcast(128))])
            pr = self.av(S0 + 2048, F32, 2, 64)
            self.tt(pr[:, 0, :], lq[:, 0, :], lq[:, 1, :], ALU.mult)
            self.tt(pr[:, 1, :], lq[:, 2, :], lq[:, 3, :], ALU.mult)
            sm = self.av(P_MISC, F32, 2)
            self.rsum(sm, pr)
            ex = self.av(P_MISC + 64, F32, 2)
            self.act(ex, sm, AF.Exp)
            lam_init = 0.8 - 0.6 * math.exp(-0.3 * l)
            self.tt(self.nlam[l], ex[:, 1:2], ex[:, 0:1], ALU.subtract)
            self.ts(self.nlam[l], self.nlam[l], -lam_init, None, ALU.add)
            self.dma("sp", "gsub", [(self.gsub[l], d["subln_g"][l:l + 1, :].partition_broadcast(128))])
            self.ts(self.gsub[l], self.gsub[l], 1.0 - lam_init, None, ALU.mult)
        self.nsl = 0

    def stage_mod(self, d, l):
        if True:
            mps = self.ps(6, F32, 72, 2)
            wv = d["w_ada"][l].rearrange("(kc p) n -> p kc n", p=128)
            for sb in range(18):
                slab = self.av(R3 + (self.nsl % 2) * 8192, BF, 8, 512)
                self.dma("pool", f"wada{self.nsl % 2}", [(slab, wv[:, :, sb * 512:(sb + 1) * 512])])
                self.nsl += 1
                for jj in range(4):
                    j = sb * 4 + jj
                    for kc in range(8):
                        self.mm(mps[:, j, :], slab[:, kc, jj * 128:(jj + 1) * 128], self.scT[:, :, kc],
                                start=(kc == 0), stop=(kc == 7))
            for ci in range(2):
                self.tt(self.modT[l][:, :, ci], mps[:, :, ci], self.bT[l], ALU.add)

    def stage_mod_gen(self, d, l, off0, off1, bank):
        wv = d["w_ada"][l].rearrange("(kc p) n -> p kc n", p=128)
        offs = (off0, off1)
        slabs = {}

        def issue(sb):
            slab = self.av(offs[sb % 2], BF, 8, 512)
            self.dma("pool", f"wadab{sb % 2}", [(slab, wv[:, :, sb * 512:(sb + 1) * 512])])
            slabs[sb] = slab
        issue(0)
        for sb in range(18):
            if sb + 1 < 18:
                issue(sb + 1)
            yield
            slab = slabs.pop(sb)
            mps = self.ps(bank, F32, 4, 2)
            for jj in range(4):
                for kc in range(8):
                    self.mm(mps[:, jj, :], slab[:, kc, jj * 128:(jj + 1) * 128], self.scT[:, :, kc],
                            start=(kc == 0), stop=(kc == 7))
            for ci in range(2):
                self.tt(self.modT[l][:, sb * 4:sb * 4 + 4, ci], mps[:, :, ci], self.bT[l][:, sb * 4:sb * 4 + 4], ALU.add)
            yield

    def derive(self, ph, l):
        m = self.modT[l]
        ci = ph.ci
        dv = self.dvs[l % 2]
        for k in range(3):
            sc = m[:, (3 * k + 1) * 8:(3 * k + 2) * 8, ci]
            self.stt(dv[:, k, :], sc, 1.0, self.gT[l][:, k * 8:(k + 1) * 8], ALU.add, ALU.mult)
        self.ts(dv[:, 3, :], m[:, 16:24, ci], 0.5, None, ALU.mult)
        self.ts(dv[:, 4, :], m[:, 64:72, ci], 0.5, None, ALU.mult)

    def shift(self, l, ph, k):
        return self.modT[l][:, (3 * k) * 8:(3 * k + 1) * 8, ph.ci]

    def views(self, ph):
        T = ph.T
        self.X3 = self.av(X_OFF, F32, 8, T)
        self.H3 = self.av(H_OFF, BF, 8, T)

    def load_x(self, ph, xd):
        for i in range(ph.T // 128):
            st = self.av(R3 + (i % 2) * 4096, F32, 1024)
            self.dma("sp", f"xin{i % 2}", [(st, xd[i * 128:(i + 1) * 128, :])])
            b0 = (i % 2) * 2
            for c in range(8):
                self.tr(self.ps(b0 + c // 4)[:, (c % 4) * 128:(c % 4 + 1) * 128], st[:, c * 128:(c + 1) * 128], self.identf)
            cols = slice(i * 128, (i + 1) * 128)
            self.cp(self.X3[:, 0:4, cols], self.ps(b0, F32, 4, 128), eng="dve")
            self.cp(self.X3[:, 4:8, cols], self.ps(b0 + 1, F32, 4, 128), eng="act")

    def store_x(self, ph, yd):
        for i in range(ph.T // 128):
            st = self.av(R3 + (i % 2) * 4096, F32, 1024)
            b0 = (i % 2) * 2
            cols = slice(i * 128, (i + 1) * 128)
            for c in range(8):
                self.tr(self.ps(b0 + c // 4)[:, (c % 4) * 128:(c % 4 + 1) * 128], self.X3[:, c, cols], self.identf)
            self.cp(st[:, 0:512], self.ps(b0), eng="dve")
            self.cp(st[:, 512:1024], self.ps(b0 + 1), eng="act")
            self.dma("sp", f"xout{i % 2}", [(yd[i * 128:(i + 1) * 128, :], st)], is_out=True)

    def rms_mod(self, ph, A_vec, shift_vec, so):
        for tb in range(ph.nblk):
            self.rms_a(ph, so, tb)
            self.rms_b(ph, A_vec, shift_vec, so, tb)

    def rms_a(self, ph, so, tb):
        cols = slice(tb * 512, (tb + 1) * 512)
        for c in range(8):
            sq = self.av(so + c * 1024, BF, 512)
            x = self.X3[:, c, cols]
            if c % 8 in (1, 4, 6):
                self.tt(sq, x, x, ALU.mult)
            else:
                self.act(sq, x, AF.Square)

    def rms_b(self, ph, A_vec, shift_vec, so, tb):
        cols = slice(tb * 512, (tb + 1) * 512)
        ss = self.ps(7 - tb % 2)
        for c in range(8):
            sq = self.av(so + c * 1024, BF, 512)
            self.mm(ss, self.onesD, sq, start=(c == 0), stop=(c == 7))
        sd = self.av(so + 8192, F32, 512)
        self.act(sd, ss, AF.Sqrt, bias=self.epsv, scale=1.0)
        self.rcp(sd, sd)
        for c in range(8):
            tmp = self.av(so + 10240 + (c % 2) * 2048, F32, 512)
            self.tt(tmp, self.X3[:, c, cols], sd, ALU.mult)
            self.act(self.H3[:, c, cols], tmp, AF.Identity, bias=shift_vec[:, c:c + 1], scale=A_vec[:, c:c + 1])

    def ffn(self, ph, l, f, d, hg_vec, next_rms=None):
        T = ph.T
        w13 = d["ffn_w13"][l, f].rearrange("(kc p) n -> p kc n", p=128)
        w2d = d["ffn_w2"][l, f]
        ACT3 = self.av(R1, BF, 11, T)
        SL = R1 + 45056
        w2 = self.av(R3, BF, 11, 1024)
        SG = R3 + 22528
        for half in range(2):
            ch0 = half * 11
            w2_issued = False
            it = 0
            per = 2 if ph.sample else 4
            SLp = SL if ph.sample else R1 + 16384
            for s0 in range(0, 11, per):
                nj = min(per, 11 - s0)
                n = nj * 128
                sidx = self.slabctr % (2 if ph.sample else 3)
                self.slabctr += 1
                slab = self.av(SLp + sidx * 4096 * per, BF, 8, 2, 128 * per)
                c0 = (ch0 + s0) * 128
                self.dma("pool", f"w13_{sidx}", [(slab[:, :, 0, 0:n], w13[:, :, c0:c0 + n]),
                                                  (slab[:, :, 1, 0:n], w13[:, :, DFF + c0:DFF + c0 + n])])
                if s0 >= per and not w2_issued:
                    self.dma("pool", "w2", [(w2, w2d[ch0 * 128:(ch0 + 11) * 128, :].rearrange("(j p) n -> p j n", p=128))])
                    w2_issued = True
                for jj in range(nj):
                    jl = s0 + jj
                    for tb in range(ph.nblk):
                        cols = slice(tb * 512, (tb + 1) * 512)
                        g_ps = self.ps(it % 2)
                        u_ps = self.ps(2 + it % 2)
                        for kc in range(8):
                            self.mm(g_ps, slab[:, kc, 0, jj * 128:(jj + 1) * 128], self.H3[:, kc, cols], start=(kc == 0), stop=(kc == 7))
                        for kc in range(8):
                            self.mm(u_ps, slab[:, kc, 1, jj * 128:(jj + 1) * 128], self.H3[:, kc, cols], start=(kc == 0), stop=(kc == 7))
                        sg = self.av(SG + (it % 2) * 2048, F32, 512)
                        self.act(sg, g_ps, AF.Silu)
                        self.tt(ACT3[:, jl, cols], sg, u_ps, ALU.mult)
                        it += 1
            it = 0
            if half == 0 or next_rms is None:
                order = [(c, tb) for c in range(8) for tb in range(ph.nblk)]
            else:
                order = [(c, tb) for tb in range(ph.nblk) for c in range(8)]
            for (c, tb) in order:
                cols = slice(tb * 512, (tb + 1) * 512)
                o_ps = self.ps(4 + it % 2)
                for jl in range(11):
                    self.mm(o_ps, w2[:, jl, c * 128:(c + 1) * 128], ACT3[:, jl, cols], start=(jl == 0), stop=(jl == 10))
                self.stt(self.X3[:, c, cols], o_ps, hg_vec[:, c:c + 1], self.X3[:, c, cols], ALU.mult, ALU.add)
                it += 1
                if half == 1 and next_rms is not None and c == 7:
                    if tb >= 1:
                        next_rms[1](tb - 1)
                    next_rms[0](tb)
            if half == 1 and next_rms is not None:
                next_rms[1](ph.nblk - 1)

    def rsqrt_act(self, out, in_, scale):
        self.act(out, in_, AF.Ln, bias=self.epsv, scale=scale)
        self.act(out, out, AF.Exp, scale=-0.5)

    def qk_prep_gen(self, ph, ps_in, gcol, outs, so, rope, kn_keep=None, sq_done=False):
        n = 512
        sq = self.av(so, BF, n)
        if not sq_done:
            self.act(sq, ps_in, AF.Square)
            yield
        ss = self.ps(3)
        self.mm(ss, self.bones, sq)
        sd = self.av(so + 1024, F32, n)
        self.rsqrt_act(sd, ss, 1.0)
        qn = kn_keep if kn_keep is not None else self.av(so + 3072, F32, n)
        self.stt(qn, ps_in, gcol, sd, ALU.mult, ALU.mult)
        if not ph.sample:
            for (p0, p1, dst) in outs:
                self.cp(dst, qn[p0:p1, :], eng="act")
            return
        yield
        rot = self.ps(3)
        self.mm(rot, self.Rm, qn)
        t1 = self.av(so + 5120, F32, n)
        t2 = self.av(so + 7168, F32, n)
        self.tt(t1, qn, rope[:, 0, :], ALU.mult)
        self.tt(t2, rot, rope[:, 1, :], ALU.mult)
        for (p0, p1, dst) in outs:
            self.tt(dst, t1[p0:p1, :], t2[p0:p1, :], ALU.add)

    def attention(self, ph, l, d, pidx, extra=None):
        T = ph.T
        A3 = self.av(R1, BF, 8, T)
        w_in = d["w_in"][l].rearrange("(kc p) n -> p kc n", p=128)
        HBSZ = 18944
        B0 = R2
        Ebase = B0 + 2 * HBSZ
        ROPE = Ebase + 3072
        SLABO = ROPE + 8192
        SO = SLABO + 6144
        CKST = SO + 7168
        SM = SO + 9216
        assert SM + 4096 <= ARENA and SM % 256 == 0
        ropedr = d["rope"].rearrange("p (a t) -> p a t", a=2)
        if not ph.sample:
            sk_st = self.av(ROPE, F32, 4, 128)
            sv_st = self.av(ROPE + 2048, F32, 4, 128)
        hbufs = []
        for hb in range(2):
            base = B0 + hb * HBSZ
            qz = self.av(base, BF, 2, T)
            kT = self.av(base + 8192, BF, ph.nk)
            vx = self.av(base + 13312, BF, 20, 136)
            self.memset(vx[:, :, 128:130], 1.0)
            self.memset(qz[64:128, 0, :], 0.0)
            self.memset(qz[0:64, 1, :], 0.0)
            hbufs.append((qz, kT, vx))
        koff = 512 if ph.sample else 0
        vch0 = 4 if ph.sample else 0
        ropectr = [0]

        def proj_gen(h):
            hb = h % 2
            qz, kT, vx = hbufs[hb]
            slab = self.av(SLABO, BF, 8, 3, 128)
            self.dma("pool", "win", [(slab[:, :, i, :], w_in[:, :, i * 1024 + h * 128:i * 1024 + (h + 1) * 128]) for i in range(3)])
            if ph.sample:
                ckst = self.av(CKST, BF, 4, 128)
                self.dma("pool", "ck", [(ckst, d["ck"][l, :, h, :].rearrange("(i p) e -> p i e", p=128))])
                self.dma("pool", f"cv{hb}", [(vx[:, 0:4, 0:128], d["cv"][l, :, h, :].rearrange("(i p) e -> p i e", p=128))])
                yield
                pb = self.ps(3, BF, 4, 128)
                for i in range(4):
                    self.tr(pb[:, i, :], ckst[:, i, :], self.identb)
                self.cp(kT[:, 0:512], pb.rearrange("p a b -> p (a b)"), eng="dve")
            for tb in range(ph.nblk):
                cols = slice(tb * 512, (tb + 1) * 512)
                rope = None
                if ph.sample:
                    ri = ropectr[0] % 2
                    ropectr[0] += 1
                    rope = self.av(ROPE + ri * 4096, F32, 2, 512)
                    self.dma("sp", f"rope{ri}", [(rope, ropedr[:, :, cols])])
                yield
                q_ps = self.ps(0)
                k_ps = self.ps(1)
                v_ps = self.ps(2, F32, 4, 128)
                for kc in range(8):
                    self.mm(q_ps, slab[:, kc, 0, :], self.H3[:, kc, cols], start=(kc == 0), stop=(kc == 7))
                    yield
                for kc in range(8):
                    self.mm(k_ps, slab[:, kc, 1, :], self.H3[:, kc, cols], start=(kc == 0), stop=(kc == 7))
                    yield
                for i in range(4):
                    tc_ = slice(tb * 512 + i * 128, tb * 512 + (i + 1) * 128)
                    for kc in range(8):
                        self.mm(v_ps[:, i, :], self.H3[:, kc, tc_], slab[:, kc, 2, :], start=(kc == 0), stop=(kc == 7))
                        if kc % 4 == 3:
                            yield
                self.cp(vx[:, vch0 + tb * 4:vch0 + tb * 4 + 4, 0:128], v_ps, eng="dve")
                if not ph.sample:
                    self.cp(sv_st, v_ps)
                    self.dma("sp", "sv", [(d["sv"][b, l, :, h, :].rearrange("(i p) e -> p i e", p=128), sv_st[:, 2 * b:2 * b + 2, :]) for b in range(2)], is_out=True)
                yield from self.qk_prep_gen(ph, q_ps, self.gqk[l][:, 0:1], [(0, 64, qz[0:64, 0, cols]), (64, 128, qz[64:128, 1, cols])], SO, rope)
                kn = None if ph.sample else self.av(SO + 3072, F32, 512)
                kdst = kT[:, koff + tb * 512:koff + (tb + 1) * 512]
                yield from self.qk_prep_gen(ph, k_ps, self.gqk[l][:, 1:2], [(0, 128, kdst)], SO, rope, kn_keep=kn)
                if not ph.sample:
                    yield
                    pk = self.ps(3, F32, 4, 128)
                    for i in range(4):
                        self.tr(pk[:, i, :], kn[:, i * 128:(i + 1) * 128], self.identf)
                    self.cp(sk_st, pk)
                    self.dma("sp", "sk", [(d["sk"][b, l, :, h, :].rearrange("(i p) e -> p i e", p=128), sk_st[:, 2 * b:2 * b + 2, :]) for b in range(2)], is_out=True)

        eit = [0]

        def attn(h, bg):
            qz, kT, vx = hbufs[h % 2]
            its = []
            for (t0, L) in ph.seqs:
                if ph.sample:
                    kch = [(kc, kc * 128) for kc in range(20)]
                else:
                    kch = [(t0 // 128 + j, t0 + j * 128) for j in range(L // 128)]
                for qb in range(L // 256):
                    for ki, (vc, kc0) in enumerate(kch):
                        its.append((t0 + qb * 256, ki, len(kch), vc, kc0))
            Es = {}
            pend = []

            def emitS(n):
                q0, ki, nk, vc, kc0 = its[n]
                e = eit[0]
                eit[0] += 1
                Sp = self.ps(4 + e % 2, F32, 2, 256)
                self.mm(Sp, kT[:, kc0:kc0 + 128], qz[:, :, q0:q0 + 256])
                E = self.av(Ebase + (e % 3) * 1024, BF, 2, 256)
                self.act(E, Sp, AF.Exp, scale=0.125)
                Es[n] = E
            emitS(0)
            if len(its) > 1:
                emitS(1)
            O1 = self.ps(6, F32, 2, 130)
            O2 = self.ps(7, F32, 2, 130)
            for n in range(len(its)):
                q0, ki, nk, vc, kc0 = its[n]
                if n + 2 < len(its):
                    emitS(n + 2)
                E = Es.pop(n)
                for mi, O in enumerate((O1, O2)):
                    for j in range(2):
                        self.mm(O[:, j, :], E[:, mi, j * 128:(j + 1) * 128], vx[:, vc, 0:130],
                                start=(ki == 0 and j == 0), stop=(ki == nk - 1), skip=True)
                if bg is not None:
                    next(bg, None)
                if extra is not None:
                    next(extra, None)
                for g in list(pend):
                    try:
                        next(g)
                    except StopIteration:
                        pend.remove(g)
                if ki == nk - 1:
                    for g in pend:
                        for _ in g:
                            pass
                    pend.clear()
                    g = epi_gen(h, q0, O1, O2)
                    next(g)
                    pend.append(g)
            for g in pend:
                for _ in g:
                    pass
            if bg is not None:
                for _ in bg:
                    pass

        def epi_gen(h, q0, O1, O2):
            Oc1 = self.av(SM + 768, F32, 2, 130)
            Oc2 = self.av(SM + 2048, F32, 2, 130)
            self.cp(Oc1, O1, eng="dve")
            self.cp(Oc2, O2, eng="dve")
            yield
            yield
            rz = self.av(SM, F32, 2, 2)
            self.rcp(rz[:, 0, :], Oc1[:, :, 128])
            self.rcp(rz[:, 1, :], Oc2[:, :, 128])
            yield
            self.ts(rz[:, 1, :], rz[:, 1, :], self.nlam[l][:, 0:1], None, ALU.mult)
            yield
            for j in range(2):
                o = Oc1[:, j, 0:128]
                self.ts(o, o, rz[:, 0, j:j + 1], None, ALU.mult)
            yield
            for j in range(2):
                o = Oc1[:, j, 0:128]
                self.stt(o, Oc2[:, j, 0:128], rz[:, 1, j:j + 1], o, ALU.mult, ALU.add)
            yield
            s2s = []
            for j in range(2):
                o = Oc1[:, j, 0:128]
                junk = self.av(SM + 3328, BF, 128)
                s2 = self.av(SM + 256 + j * 256, F32, 1)
                self.S.add("dve", (lambda junk=junk, o=o, s2=s2: (lambda e: e.scalar_tensor_tensor(junk, o, 1.0, o, ALU.mult, ALU.mult, accum_out=s2)))(), [o], [junk, s2])
                s2s.append(s2)
            yield
            yield
            for j in range(2):
                self.act(s2s[j], s2s[j], AF.Ln, bias=self.epsv, scale=1.0 / 128.0)
            yield
            for j in range(2):
                self.act(s2s[j], s2s[j], AF.Exp, scale=-0.5)
            yield
            yield
            ats = []
            for j in range(2):
                at = self.av(SM + 3584 + j * 256, BF, 128)
                self.stt(at, Oc1[:, j, 0:128], s2s[j], self.gsub[l], ALU.mult, ALU.mult)
                ats.append(at)
            yield
            yield
            pt = self.ps(3, BF, 2, 128)
            for j in range(2):
                self.tr(pt[:, j, :], ats[j], self.identb)
            self.cp(A3[:, h, q0:q0 + 256], pt.rearrange("p a b -> p (a b)"))

        for _ in proj_gen(0):
            pass
        for h in range(NH):
            bg = proj_gen(h + 1) if h + 1 < NH else None
            attn(h, bg)
        if extra is not None:
            for _ in extra:
                pass

    def mstage(self, ph, l, d, which, next_rms=None):
        T = ph.T
        w_gate = d["w_gate"][l].rearrange("(kc p) n -> p kc n", p=128)
        w_out = d["w_out"][l].rearrange("(kc p) n -> p kc n", p=128)
        g2 = self.modT[l][:, 40:48, ph.ci]
        wo = self.av(R3, BF, 8, 1024)
        SLB = R3 + 16384
        SG = R3 + 28672
        if which == 1:
            A3 = self.av(R1, BF, 8, T)
            M3 = self.av(R2, BF, 8, T)
            w_pa = d["w_pa"][l].rearrange("(kc p) n -> p kc n", p=128)
        else:
            P3 = self.av(R2, BF, 4, T)
            F3 = self.av(R2 + 16384, BF, 4, T)
            M3 = self.av(R1, BF, 8, T)
            w_pp = d["w_pp"][l].rearrange("(kc p) n -> p kc n", p=128)
            w_pf = d["w_pf"][l].rearrange("(kc p) n -> p kc n", p=128)
        it = 0
        for c in range(8):
            sidx = c % 2
            cs = slice(c * 128, (c + 1) * 128)
            if which == 1:
                wg = self.av(SLB + sidx * 6144, BF, 8, 128)
                wp = self.av(SLB + sidx * 6144 + 2048, BF, 8, 128)
                self.dma("pool", f"ms{sidx}", [(wg, w_gate[:, :, c * 128:(c + 1) * 128]), (wp, w_pa[:, :, cs])])
            else:
                wgp = self.av(SLB + sidx * 6144, BF, 8, 128)
                wgf = self.av(SLB + sidx * 6144 + 2048, BF, 8, 128)
                wpp = self.av(SLB + sidx * 6144 + 4096, BF, 4, 128)
                wpf = self.av(SLB + sidx * 6144 + 5120, BF, 4, 128)
                self.dma("pool", f"ms{sidx}", [(wgp, w_gate[:, :, 1024 + c * 128:1024 + (c + 1) * 128]),
                                               (wgf, w_gate[:, :, 2048 + c * 128:2048 + (c + 1) * 128]),
                                               (wpp, w_pp[:, :, cs]), (wpf, w_pf[:, :, cs])])
            if c == 1:
                self.dma("pool", "wout", [(wo, w_out)])
            for tb in range(ph.nblk):
                cols = slice(tb * 512, (tb + 1) * 512)
                if which == 1:
                    g_ps = self.ps(it % 2)
                    a_ps = self.ps(2 + it % 2)
                    for kc in range(8):
                        self.mm(g_ps, wg[:, kc, :], self.H3[:, kc, cols], start=(kc == 0), stop=(kc == 7))
                    for kc in range(8):
                        self.mm(a_ps, wp[:, kc, :], A3[:, kc, cols], start=(kc == 0), stop=(kc == 7))
                    sg = self.av(SG + (it % 2) * 2048, F32, 512)
                    self.act(sg, g_ps, AF.Sigmoid)
                    self.tt(M3[:, c, cols], sg, a_ps, ALU.mult)
                else:
                    gp_ps = self.ps(it % 2)
                    gf_ps = self.ps(2 + it % 2)
                    p_ps = self.ps(4)
                    f_ps = self.ps(5)
                    for kc in range(8):
                        self.mm(gp_ps, wgp[:, kc, :], self.H3[:, kc, cols], start=(kc == 0), stop=(kc == 7))
                    for kc in range(8):
                        self.mm(gf_ps, wgf[:, kc, :], self.H3[:, kc, cols], start=(kc == 0), stop=(kc == 7))
                    for kc in range(4):
                        self.mm(p_ps, wpp[:, kc, :], P3[:, kc, cols], start=(kc == 0), stop=(kc == 3))
                    for kc in range(4):
                        self.mm(f_ps, wpf[:, kc, :], F3[:, kc, cols], start=(kc == 0), stop=(kc == 3))
                    sgp = self.av(SG + (it % 2) * 2048, F32, 512)
                    sgf = self.av(SG + 4096 + (it % 2) * 2048, F32, 512)
                    tmp = self.av(SG + 8192, F32, 512)
                    self.act(sgp, gp_ps, AF.Sigmoid)
                    self.act(sgf, gf_ps, AF.Sigmoid)
                    self.tt(tmp, sgp, p_ps, ALU.mult)
                    self.tt(sgf, sgf, f_ps, ALU.mult)
                    self.tt(M3[:, c, cols], tmp, sgf, ALU.add)
                it += 1
        it = 0
        if next_rms is None:
            order = [(c, tb) for c in range(8) for tb in range(ph.nblk)]
        else:
            order = [(c, tb) for tb in range(ph.nblk) for c in range(8)]
        for (c, tb) in order:
            cols = slice(tb * 512, (tb + 1) * 512)
            o_ps = self.ps(6 + it % 2)
            for kc in range(8):
                self.mm(o_ps, wo[:, kc, c * 128:(c + 1) * 128], M3[:, kc, cols], start=(kc == 0), stop=(kc == 7))
            self.stt(self.X3[:, c, cols], o_ps, g2[:, c:c + 1], self.X3[:, c, cols], ALU.mult, ALU.add)
            it += 1
            if next_rms is not None and c == 7:
                if tb >= 1:
                    next_rms[1](tb - 1)
                next_rms[0](tb)
        if next_rms is not None:
            next_rms[1](ph.nblk - 1)

    def fourier(self, ph, l, d):
        T = ph.T
        w_in = d["w_in"][l].rearrange("(kc p) n -> p kc n", p=128)
        U3 = self.av(R2, BF, 4, T)
        F3 = self.av(R2 + 16384, BF, 4, T)
        nch = T // 128
        AT = self.av(R1, BF, nch, 4, 2, 128)
        slab = self.av(R3 + 32768, BF, 8, 512)
        self.dma("pool", "ufs", [(slab, w_in[:, :, 3584:4096])])
        it = 0
        for g in range(4):
            for tb in range(ph.nblk):
                cols = slice(tb * 512, (tb + 1) * 512)
                u_ps = self.ps(it % 2)
                for kc in range(8):
                    self.mm(u_ps, slab[:, kc, g * 128:(g + 1) * 128], self.H3[:, kc, cols], start=(kc == 0), stop=(kc == 7))
                self.cp(U3[:, g, cols], u_ps, eng=("act" if it % 2 else "dve"))
                it += 1
        for i in range(nch):
            b0 = 2 + (i % 2) * 2
            for g in range(4):
                pa = self.ps(b0 + g // 2, F32, 2, 256)
                self.mm(pa[:, g % 2, :], U3[:, g, i * 128:(i + 1) * 128], self.CS)
            self.cp(AT[:, i, 0:2, :, :].rearrange("p a b c -> p (a b c)"), self.ps(b0), eng="dve")
            self.cp(AT[:, i, 2:4, :, :].rearrange("p a b c -> p (a b c)"), self.ps(b0 + 1), eng="act")
        it = 0
        tabn = 0
        for (t0, L) in ph.seqs:
            ni = L // 128
            i0 = t0 // 128
            for tpb in range(L // 256):
                tab = self.av(R3 + (tabn % 2) * 16384, BF, ni, 2, 256)
                if ph.sample:
                    src = d["dftS"][tpb]
                else:
                    src = d["dftP"]
                self.dma("sp", f"tab{tabn % 2}", [(tab.rearrange("p a b c -> p (a b c)"), src)])
                tabn += 1
                for g in range(4):
                    f_ps = self.ps(6 + it % 2)[:, 0:256]
                    n = 0
                    for ii in range(ni):
                        for cs in range(2):
                            self.mm(f_ps, AT[:, i0 + ii, g, cs, :], tab[:, ii, cs, :], start=(n == 0), stop=(n == 2 * ni - 1))
                            n += 1
                    self.cp(F3[:, g, t0 + tpb * 256:t0 + (tpb + 1) * 256], f_ps, eng=("act" if it % 2 else "dve"))
                    it += 1

    def poolmix(self, ph, l, d):
        T = ph.T
        w_in = d["w_in"][l].rearrange("(kc p) n -> p kc n", p=128)
        P3 = self.av(R2, BF, 4, T)
        nseq = len(ph.seqs)
        L = ph.seqs[0][1]
        Lp = L + 16
        slab = self.av(R3, BF, 8, 512)
        self.dma("pool", "ups", [(slab, w_in[:, :, 3072:3584])])
        wpl = self.av(R3 + 8192, BF, 4, 128)
        self.dma("pool", "wpool", [(wpl, d["w_pool"][l].rearrange("g c e -> c g e"))])
        bufsz = ((nseq * Lp * 4 + 255) // 256) * 256
        U = self.av(R1, F32, nseq, Lp)
        Q = [self.av(R1 + bufsz * (1 + i), F32, nseq, Lp) for i in range(2)]
        Dg = self.av(R1 + 3 * bufsz, BF, nseq, L)
        assert 3 * bufsz + T * 2 <= 32768
        it = 0
        for g in range(4):
            w = POOLW[g]
            lv = g + 1
            self.memset(U[:, :, 0:8], 0.0)
            self.memset(U[:, :, L + 8:L + 16], 0.0)
            for tb in range(ph.nblk):
                cols = slice(tb * 512, (tb + 1) * 512)
                u_ps = self.ps(it % 2)
                for kc in range(8):
                    self.mm(u_ps, slab[:, kc, g * 128:(g + 1) * 128], self.H3[:, kc, cols], start=(kc == 0), stop=(kc == 7))
                if nseq == 1:
                    self.cp(U[:, 0, 8 + tb * 512:8 + (tb + 1) * 512], u_ps, eng=("act" if it % 2 else "dve"))
                else:
                    self.cp(U[:, :, 8:8 + L], u_ps.rearrange("p (s t) -> p s t", s=nseq), eng="dve")
                it += 1
            src = U
            for k in range(1, lv + 1):
                sh = 1 << (k - 1)
                dst = Q[(k - 1) % 2]
                self.tt(dst[:, :, 0:Lp - sh], src[:, :, 0:Lp - sh], src[:, :, sh:Lp], ALU.add, eng="dve")
                src = dst
            hw = w // 2
            Ssh = src[:, :, 8 - hw:8 - hw + L]
            Uc = U[:, :, 8:8 + L]
            tmpD = Q[lv % 2][:, :, 0:L]
            self.stt(tmpD, Ssh, 1.0 / w, Uc, ALU.mult, ALU.subtract)
            tb0 = {2: 0, 4: 2, 8: 6, 16: 14}[w]
            fl = self.ptab[:, tb0:tb0 + hw]
            fr = self.ptab[:, tb0 + hw:tb0 + 2 * hw]
            for s in range(nseq):
                bl = self.av(P_MISC + 128, F32, 8)[:, 0:hw]
                self.tt(bl, Ssh[:, s, 0:hw], fl, ALU.mult)
                self.tt(tmpD[:, s, 0:hw], bl, Uc[:, s, 0:hw], ALU.subtract)
                br = self.av(P_MISC + 192, F32, 8)[:, 0:hw]
                self.tt(br, Ssh[:, s, L - hw:L], fr, ALU.mult)
                self.tt(tmpD[:, s, L - hw:L], br, Uc[:, s, L - hw:L], ALU.subtract)
            self.cp(Dg, tmpD, eng="act")
            Dflat = Dg.rearrange("p s t -> p (s t)")
            for tb in range(ph.nblk):
                cols = slice(tb * 512, (tb + 1) * 512)
                p_ps = self.ps(2 + tb % 2)
                self.mm(p_ps, wpl[:, g, :], Dflat[:, cols])
                self.act(P3[:, g, cols], p_ps, AF.Identity, scale=self.psT[l][:, g:g + 1])

    def layer(self, ph, l, d, pidx, first=True, last=True):
        cfg = self.cfg
        RSO = R3 + 22528 + 4096
        full = all(cfg.get(k, True) for k in ("ffn", "mixer", "attn", "pf", "ffn2"))
        if first or not full:
            self.derive(ph, l)
        dv = self.dvs[l % 2]
        if not full:
            if cfg.get("ffn", True):
                self.rms_mod(ph, dv[:, 0, :], self.shift(l, ph, 0), RSO)
                self.ffn(ph, l, 0, d, dv[:, 3, :])
            if cfg.get("mixer", True):
                self.rms_mod(ph, dv[:, 1, :], self.shift(l, ph, 1), RSO)
                if cfg.get("attn", True):
                    self.attention(ph, l, d, pidx)
                    self.mstage(ph, l, d, 1)
                if cfg.get("pf", True):
                    self.fourier(ph, l, d)
                    self.poolmix(ph, l, d)
                    self.mstage(ph, l, d, 2)
            if cfg.get("ffn2", True):
                self.rms_mod(ph, dv[:, 2, :], self.shift(l, ph, 2), RSO)
                self.ffn(ph, l, 1, d, dv[:, 4, :])
            return
        if not first:
            self.derive(ph, l)
        self.rms_mod(ph, dv[:, 0, :], self.shift(l, ph, 0), RSO)
        self.ffn(ph, l, 0, d, dv[:, 3, :])
        self.rms_mod(ph, dv[:, 1, :], self.shift(l, ph, 1), RSO)
        extra = None
        if self.defer_mod1 and pidx == 0 and l == 0:
            extra = self.stage_mod_gen(d, 1, R1 + 16384, R1 + 24576, 3)
        self.attention(ph, l, d, pidx, extra=extra)
        self.mstage(ph, l, d, 1)
        self.fourier(ph, l, d)
        self.poolmix(ph, l, d)
        self.mstage(ph, l, d, 2)
        self.rms_mod(ph, dv[:, 2, :], self.shift(l, ph, 2), RSO)
        self.ffn(ph, l, 1, d, dv[:, 4, :])


def build(cfg=None):
    cfg = cfg or {}
    nc = bass.Bass("TRN2", target_bir_lowering=False)
    d = {}

    def inp(name, shape, dt=F32):
        d[name] = nc.dram_tensor(name, list(shape), dt, kind="ExternalInput").ap()

    def outp(name, shape):
        d[name] = nc.dram_tensor(name, list(shape), F32, kind="ExternalOutput").ap()

    inp("xs", [2048, D]); inp("xp", [512, D])
    inp("ck", [2, 512, 8, 128]); inp("cv", [2, 512, 8, 128])
    inp("cond", [16, 128])
    inp("w_ada", [2, D, 9 * D]); inp("b_ada", [2, 72, 128]); inp("norm_g", [2, 24, 128])
    inp("ffn_w13", [2, 2, D, 2 * DFF]); inp("ffn_w2", [2, 2, DFF, D])
    inp("w_in", [2, D, 4096]); inp("q_norm_g", [2, 64]); inp("k_norm_g", [2, 64])
    inp("lam_qk", [2, 256]); inp("subln_g", [2, 128]); inp("w_pool", [2, 4, 128, 128])
    inp("pool_scale", [2, 4, 128]); inp("w_gate", [2, D, 3 * D]); inp("w_pa", [2, D, D])
    inp("w_pp", [2, 512, D]); inp("w_pf", [2, 512, D]); inp("w_out", [2, D, D])
    inp("cb", [128, 640], BF); inp("cf", [128, 288]); inp("rope", [128, 2 * 2048])
    inp("dftS", [8, 128, 16 * 2 * 256], BF); inp("dftP", [128, 2 * 2 * 256], BF)
    outp("ys", [2048, D]); outp("yp", [512, D])
    outp("sk", [2, 2, 256, 8, 128]); outp("sv", [2, 2, 256, 8, 128])

    with ExitStack() as es:
        k = K(nc, es, cfg)
        k.slabctr = 0
        k.setup_consts(d)
        k.epsv = k.av(P_MISC + 256, F32, 1)
        k.memset(k.epsv, EPS)
        k.stage_params(d)
        k.stage_mod(d, 0)
        full = all(cfg.get(kk, True) for kk in ("ffn", "mixer", "attn", "pf", "ffn2"))
        k.defer_mod1 = bool(cfg.get("P", True) and full and cfg.get("layers", 2) > 1)
        if cfg.get("layers", 2) > 1 and not k.defer_mod1:
            k.stage_mod(d, 1)
        phases = []
        if cfg.get("P", True):
            phases.append((Phase("P", 512, [(0, 256), (256, 256)], 0, False), d["xp"], d["yp"]))
        if cfg.get("S", True):
            phases.append((Phase("S", 2048, [(0, 2048)], 1, True), d["xs"], d["ys"]))
        for pidx, (ph, xd, yd) in enumerate(phases):
            k.views(ph)
            k.load_x(ph, xd)
            nl = cfg.get("layers", 2)
            for l in range(nl):
                if pidx == 0 and l == 0 and nl > 1:
                    pass
                k.layer(ph, l, d, pidx, first=(l == 0), last=(l == nl - 1))
            k.store_x(ph, yd)
        S = k.S
        S.add("sp", lambda e: None, [], [], extra_deps=list(k.out_ops))
        keys = S.finalize()
        sems = {kk: es.enter_context(nc.semaphore(f"s{i}")) for i, kk in enumerate(keys)}
        with nc.Block() as block:
            @block.tensor
            def _(e):
                S.emit("pe", e, sems)

            @block.scalar
            def _(e):
                S.emit("act", e, sems)

            @block.vector
            def _(e):
                S.emit("dve", e, sems)

            @block.gpsimd
            def _(e):
                S.emit("pool", e, sems)

            @block.sync
            def _(e):
                S.emit("sp", e, sems)
        k.nops = len(S.ops)
        k.nsem = len(keys)
    return nc, k


def host_consts():
    bf = ml_dtypes.bfloat16
    cb = np.zeros((128, 640), np.float32)
    cb[:, 0:128] = np.eye(128)
    cb[:, 128:256] = 1.0 / 1024.0
    blk = np.zeros((128, 128))
    blk[:64, :64] = 1.0 / 64
    blk[64:, 64:] = 1.0 / 64
    cb[:, 256:384] = blk
    cc = np.arange(128)[:, None] * np.arange(128)[None, :]
    ang = 2 * np.pi * (cc % 128) / 128.0
    cb[:, 384:512] = np.cos(ang) / np.sqrt(128.0)
    cb[:, 512:640] = np.sin(ang) / np.sqrt(128.0)
    cf = np.zeros((128, 288), np.float32)
    cf[:, 0:128] = np.eye(128)
    R = np.zeros((128, 128), np.float32)
    for m in range(128):
        if (m % 32) < 16:
            R[m + 16, m] = -1.0
        else:
            R[m - 16, m] = 1.0
    cf[:, 128:256] = R
    col = 256
    for w in POOLW:
        hw = w // 2
        for t in range(hw):
            cf[:, col + t] = 1.0 / (t + hw)
        for i in range(hw):
            cf[:, col + hw + i] = 1.0 / (2 * hw - i)
        col += 2 * hw
    p = np.arange(128)
    dd = p % 64
    axis = dd // 32
    freq = dd % 16
    inv = (10000.0 ** (-np.arange(16, dtype=np.float32) / np.float32(16))).astype(np.float32)
    t = np.arange(2048)
    row = (t // GRID_W).astype(np.float32)
    colp = (t % GRID_W).astype(np.float32)
    pos = np.where(axis[:, None] == 0, row[None, :], colp[None, :]).astype(np.float32)
    angr = (pos * inv[freq][:, None]).astype(np.float32)
    rope = np.stack([np.cos(angr), np.sin(angr)], axis=1).astype(np.float32).reshape(128, 4096)

    def dft(L):
        tt = np.arange(L)[:, None].astype(np.int64)
        kk = np.arange(L)[None, :].astype(np.int64)
        a = 2 * np.pi * ((tt * kk) % L) / L
        return np.cos(a) / np.sqrt(L), -np.sin(a) / np.sqrt(L)
    c, s = dft(2048)
    tab = np.stack([c, s], axis=0)
    tab = tab.reshape(2, 16, 128, 8, 256)
    dftS = np.ascontiguousarray(tab.transpose(3, 2, 1, 0, 4)).reshape(8, 128, 16 * 2 * 256)
    c, s = dft(256)
    tab = np.stack([c, s], axis=0).reshape(2, 2, 128, 256)
    dftP = np.ascontiguousarray(tab.transpose(2, 1, 0, 3)).reshape(128, 2 * 2 * 256)
    return dict(cb=cb.astype(bf), cf=cf, rope=rope, dftS=dftS.astype(bf), dftP=dftP.astype(bf))


_CACHE = {}


def make_in_maps(inputs, ncores=8):
    f = lambda a: np.ascontiguousarray(np.asarray(a, dtype=np.float32))
    hc = host_consts()
    shared = dict(
        w_ada=f(inputs["w_ada"]), b_ada=f(inputs["b_ada"]).reshape(2, 72, 128), norm_g=f(inputs["norm_g"]).reshape(2, 24, 128),
        ffn_w13=f(inputs["ffn_w13"]), ffn_w2=f(inputs["ffn_w2"]), w_in=f(inputs["w_in"]),
        q_norm_g=f(inputs["q_norm_g"]), k_norm_g=f(inputs["k_norm_g"]), lam_qk=f(inputs["lam_qk"]).reshape(2, 256),
        subln_g=f(inputs["subln_g"]), w_pool=f(inputs["w_pool"]), pool_scale=f(inputs["pool_scale"]).reshape(2, 4, 128),
        w_gate=f(inputs["w_gate"]), w_pa=f(inputs["w_pa"]), w_pp=f(inputs["w_pp"]), w_pf=f(inputs["w_pf"]),
        w_out=f(inputs["w_out"]), **hc)
    xp = f(inputs["x_prompt"]); xs = f(inputs["x_sample"])
    ck = f(inputs["cache_k"]); cv = f(inputs["cache_v"])
    c = f(inputs["c"]); cc = f(inputs["c_ctx"])
    maps = []
    for i in range(ncores):
        m = dict(shared)
        m["xs"] = xs[i]
        m["xp"] = xp[2 * i:2 * i + 2].reshape(512, D)
        m["ck"] = ck[i]
        m["cv"] = cv[i]
        m["cond"] = np.ascontiguousarray(np.concatenate([cc.reshape(8, 128), c[i].reshape(8, 128)], axis=0))
        maps.append(m)
    return maps


def kernel(**inputs):
    if "nc" not in _CACHE:
        _CACHE["nc"] = build()[0]
    nc = _CACHE["nc"]
    maps = make_in_maps(inputs)
    res = run_bass_kernel_spmd(nc, maps, core_ids=list(range(8)))
    r = res.results
    y_prompt = np.concatenate([r[i]["yp"].reshape(2, 256, D) for i in range(8)], axis=0).astype(np.float32)
    y_sample = np.stack([r[i]["ys"] for i in range(8)], axis=0).astype(np.float32)
    state_k = np.concatenate([r[i]["sk"] for i in range(8)], axis=0).astype(np.float32)
    state_v = np.concatenate([r[i]["sv"] for i in range(8)], axis=0).astype(np.float32)
    return (y_prompt, y_sample, state_k, state_v)
```
